# Optimizing a Trainium2 kernel written in Bass

```python
import math
import jax, jax.numpy as jnp
from jax import lax
import numpy as np

D_MODEL = 1024
BATCH = 16
SEQ = 256
DEPTH = 4
DEC_BATCH = 4
DEC_SEQ = 2048
PAST_LEN = 512

GRID_W = 64
Q_BLOCK = 128
ROPE_THETA = 10000.0
H_A = 8
D_A = 64
H_B = 8
G_B = 4
D_B = 128
N_EXPERTS = 16
N_GROUPS = 4
EXPERTS_PER_GROUP = N_EXPERTS // N_GROUPS
TOP_K = 2
D_EXPERT = 512
ALPHA = (2 * DEPTH) ** 0.25
BETA = (8 * DEPTH) ** -0.25
LN_EPS = 1e-6
RMS_EPS = 1e-6
W_QA = H_A * 2 * D_A
W_KA = H_A * 2 * D_A
W_VA = H_A * 2 * D_A
W_QB = H_B * D_B
W_KB = G_B * D_B
W_VB = G_B * D_B
W_GATE = 2 * D_MODEL
OFF_KA = W_QA
OFF_VA = OFF_KA + W_KA
OFF_QB = OFF_VA + W_VA
OFF_KB = OFF_QB + W_QB
OFF_VB = OFF_KB + W_KB
OFF_GATE = OFF_VB + W_VB
D_IN = OFF_GATE + W_GATE
SPLIT_POINTS = (OFF_KA, OFF_VA, OFF_QB, OFF_KB, OFF_VB, OFF_GATE)

kernel_name = "hybrid_diff_gqa_moe_prefix_diffusion_step"


def layer_norm(x, g, b):
    xf = x.astype(jnp.float32)
    mu = jnp.mean(xf, axis=-1, keepdims=True)
    xc = xf - mu
    var = jnp.mean(xc * xc, axis=-1, keepdims=True)
    return (xc * lax.rsqrt(var + LN_EPS) * g + b).astype(x.dtype)


def rms_norm(x, g):
    xf = x.astype(jnp.float32)
    y = xf * lax.rsqrt(jnp.mean(xf * xf, axis=-1, keepdims=True) + RMS_EPS)
    return (y * g).astype(x.dtype)


def axial_rope(x, row, col):
    d = x.shape[-1]
    nf = d // 4
    inv = ROPE_THETA ** (-jnp.arange(nf, dtype=jnp.float32) / nf)
    ang = jnp.stack([row[:, None] * inv, col[:, None] * inv], axis=-2)
    ang = ang.reshape(ang.shape[0], *([1] * (x.ndim - 3)), 2, nf)
    cos, sin = jnp.cos(ang), jnp.sin(ang)
    xr = x.astype(jnp.float32).reshape(*x.shape[:-1], 2, 2, nf)
    x1, x2 = xr[..., 0, :], xr[..., 1, :]
    out = jnp.stack([x1 * cos - x2 * sin, x2 * cos + x1 * sin], axis=-2)
    return out.reshape(x.shape).astype(x.dtype)


def sweep_query_blocks(attend, q):
    b, t = q.shape[:2]
    qb = jnp.moveaxis(q.reshape(b, t // Q_BLOCK, Q_BLOCK, *q.shape[2:]), 1, 0)
    out = lax.map(attend, qb)
    return jnp.moveaxis(out, 0, 1).reshape(b, t, *out.shape[3:])


def diff_attention(q, k, v, lam):
    scale = D_A ** -0.5

    def attend(qb):
        s = jnp.einsum('bqhmd,bkhmd->mbhqk', qb, k).astype(jnp.float32) * scale
        p = jax.nn.softmax(s, axis=-1)
        a = p[0] - lam * p[1]
        return jnp.einsum('bhqk,bkhe->bqhe', a.astype(v.dtype), v)

    return sweep_query_blocks(attend, q)


def gqa_attention(q, k, v):
    scale = D_B ** -0.5

    def attend(qb):
        bq, nq = qb.shape[:2]
        qg = qb.reshape(bq, nq, G_B, H_B // G_B, D_B)
        s = jnp.einsum('bqgrd,bkgd->bgrqk', qg, k).astype(jnp.float32) * scale
        p = jax.nn.softmax(s, axis=-1)
        o = jnp.einsum('bgrqk,bkgd->bqgrd', p.astype(v.dtype), v)
        return o.reshape(bq, nq, H_B * D_B)

    return sweep_query_blocks(attend, q)


def mix_block(h, pos, ctx, lam_init, w_in, lam_q1, lam_k1, lam_q2, lam_k2, subln_g,
              qn_g, kn_g, w_br_a, w_br_b, w_out):
    b, t, _ = h.shape
    proj = h @ w_in
    qa, ka, va, qb, kb, vb, gates = jnp.split(proj, SPLIT_POINTS, axis=-1)
    qa = qa.reshape(b, t, H_A, 2, D_A)
    ka = ka.reshape(b, t, H_A, 2, D_A)
    va = va.reshape(b, t, H_A, 2 * D_A)
    qb = rms_norm(qb.reshape(b, t, H_B, D_B), qn_g)
    kb = rms_norm(kb.reshape(b, t, G_B, D_B), kn_g)
    vb = vb.reshape(b, t, G_B, D_B)
    new_kv = (ka.reshape(b, t, H_A, 2 * D_A), va, kb, vb)
    if pos is None:
        keys_a, vals_a, keys_b, vals_b = ka, va, kb, vb
    else:
        row, col = pos
        qa, ka = axial_rope(qa, row, col), axial_rope(ka, row, col)
        qb, kb = axial_rope(qb, row, col), axial_rope(kb, row, col)
        ck_a, cv_a, ck_b, cv_b = ctx
        s = ck_a.shape[1]
        keys_a = jnp.concatenate([ck_a.reshape(b, s, H_A, 2, D_A), ka], axis=1)
        vals_a = jnp.concatenate([cv_a, va], axis=1)
        keys_b = jnp.concatenate([ck_b, kb], axis=1)
        vals_b = jnp.concatenate([cv_b, vb], axis=1)
    lam = (jnp.exp(jnp.sum(lam_q1.astype(jnp.float32) * lam_k1.astype(jnp.float32)))
           - jnp.exp(jnp.sum(lam_q2.astype(jnp.float32) * lam_k2.astype(jnp.float32))) + lam_init)
    oa = diff_attention(qa, keys_a, vals_a, lam)
    oa = (rms_norm(oa, subln_g) * (1.0 - lam_init)).reshape(b, t, H_A * 2 * D_A)
    ob = gqa_attention(qb, keys_b, vals_b)
    ga, gb = jnp.split(gates, 2, axis=-1)
    merged = jax.nn.sigmoid(ga) * (oa @ w_br_a) + jax.nn.sigmoid(gb) * (ob @ w_br_b)
    return merged @ w_out, new_kv


def moe(h, w_router, router_bias, w_gate, w_up, w_down):
    b, t, d = h.shape
    tok = h.reshape(b * t, d)
    scores = jax.nn.sigmoid((tok @ w_router).astype(jnp.float32))
    biased = scores + router_bias.astype(jnp.float32)
    grp = biased.reshape(-1, N_GROUPS, EXPERTS_PER_GROUP)
    grp_score = jnp.sum(lax.top_k(grp, TOP_K)[0], axis=-1)
    sel_group = jnp.argmax(grp_score, axis=-1)
    in_group = (jnp.arange(N_EXPERTS) // EXPERTS_PER_GROUP)[None, :] == sel_group[:, None]
    _, idx = lax.top_k(jnp.where(in_group, biased, -jnp.inf), TOP_K)
    w = jnp.take_along_axis(scores, idx, axis=-1)
    w = w / jnp.sum(w, axis=-1, keepdims=True)
    combine = jnp.sum(jax.nn.one_hot(idx, N_EXPERTS, dtype=jnp.float32) * w[..., None], axis=-2)
    g = jnp.einsum('nd,edf->nef', tok, w_gate)
    u = jnp.einsum('nd,edf->nef', tok, w_up)
    act = jax.nn.silu(g) * u * combine[..., None].astype(tok.dtype)
    y = jnp.einsum('nef,efd->nd', act, w_down)
    return y.reshape(b, t, d)


def trunk_layer(x, cond, pos, ctx, lam_init, w_mod, b_mod, w_in, lam_q1, lam_k1, lam_q2, lam_k2,
                subln_g, qn_g, kn_g, w_br_a, w_br_b, w_out, ln1_g, ln1_b, ln2_g, ln2_b,
                w_router, router_bias, w_gate, w_up, w_down):
    sh1, sc1, g1, sh2, sc2, g2 = jnp.split(jax.nn.silu(cond) @ w_mod + b_mod, 6, axis=-1)
    h = x * (1.0 + sc1) + sh1
    mix, new_kv = mix_block(h, pos, ctx, lam_init, w_in, lam_q1, lam_k1, lam_q2, lam_k2, subln_g,
                            qn_g, kn_g, w_br_a, w_br_b, w_out)
    x = layer_norm(ALPHA * x + g1 * mix, ln1_g, ln1_b)
    h = x * (1.0 + sc2) + sh2
    x = layer_norm(ALPHA * x + g2 * moe(h, w_router, router_bias, w_gate, w_up, w_down), ln2_g, ln2_b)
    return x, new_kv


def setup_inputs(seed: int = 0) -> dict:
    key = jax.random.key(seed)
    ks = iter(jax.random.split(key, 40))
    nrm = lambda shape, s=1.0: s * jax.random.normal(next(ks), shape, dtype=jnp.float32)
    gain = lambda shape: 1.0 + nrm(shape, 0.01)
    col_scale = jnp.ones((D_IN,), jnp.float32).at[OFF_VA:OFF_QB].set(BETA).at[OFF_VB:OFF_GATE].set(BETA)
    return {
        "x_prompt": nrm((BATCH, SEQ, D_MODEL)),
        "x_sample": nrm((DEC_BATCH, DEC_SEQ, D_MODEL)),
        "cache_a_k": nrm((DEC_BATCH, DEPTH, PAST_LEN, H_A, 2 * D_A)),
        "cache_a_v": nrm((DEC_BATCH, DEPTH, PAST_LEN, H_A, 2 * D_A), BETA),
        "cache_b_k": nrm((DEC_BATCH, DEPTH, PAST_LEN, G_B, D_B)),
        "cache_b_v": nrm((DEC_BATCH, DEPTH, PAST_LEN, G_B, D_B), BETA),
        "c": nrm((DEC_BATCH, D_MODEL)),
        "c_ctx": nrm((D_MODEL,)),
        "w_mod": nrm((DEPTH, D_MODEL, 6 * D_MODEL), 0.5 * D_MODEL ** -0.5),
        "b_mod": nrm((DEPTH, 6 * D_MODEL), 0.01),
        "w_in": nrm((DEPTH, D_MODEL, D_IN), D_MODEL ** -0.5) * col_scale,
        "lam_q1": nrm((DEPTH, D_A), 0.1),
        "lam_k1": nrm((DEPTH, D_A), 0.1),
        "lam_q2": nrm((DEPTH, D_A), 0.1),
        "lam_k2": nrm((DEPTH, D_A), 0.1),
        "subln_g": gain((DEPTH, 2 * D_A)),
        "qn_g": gain((DEPTH, D_B)),
        "kn_g": gain((DEPTH, D_B)),
        "w_br_a": nrm((DEPTH, H_A * 2 * D_A, D_MODEL), BETA * (H_A * 2 * D_A) ** -0.5),
        "w_br_b": nrm((DEPTH, H_B * D_B, D_MODEL), BETA * (H_B * D_B) ** -0.5),
        "w_out": nrm((DEPTH, D_MODEL, D_MODEL), BETA * D_MODEL ** -0.5),
        "ln1_g": gain((DEPTH, D_MODEL)),
        "ln1_b": nrm((DEPTH, D_MODEL), 0.01),
        "ln2_g": gain((DEPTH, D_MODEL)),
        "ln2_b": nrm((DEPTH, D_MODEL), 0.01),
        "w_router": nrm((D_MODEL, N_EXPERTS), D_MODEL ** -0.5),
        "router_bias": nrm((N_EXPERTS,), 0.01),
        "w_gate": nrm((DEPTH, N_EXPERTS, D_MODEL, D_EXPERT), BETA * D_MODEL ** -0.5),
        "w_up": nrm((DEPTH, N_EXPERTS, D_MODEL, D_EXPERT), BETA * D_MODEL ** -0.5),
        "w_down": nrm((DEPTH, N_EXPERTS, D_EXPERT, D_MODEL), BETA * D_EXPERT ** -0.5),
    }


def reference(x_prompt, x_sample, cache_a_k, cache_a_v, cache_b_k, cache_b_v, c, c_ctx,
              w_mod, b_mod, w_in, lam_q1, lam_k1, lam_q2, lam_k2, subln_g, qn_g, kn_g,
              w_br_a, w_br_b, w_out, ln1_g, ln1_b, ln2_g, ln2_b, w_router, router_bias,
              w_gate, w_up, w_down):
    def layer_weights(l):
        return (w_mod[l], b_mod[l], w_in[l], lam_q1[l], lam_k1[l], lam_q2[l], lam_k2[l], subln_g[l],
                qn_g[l], kn_g[l], w_br_a[l], w_br_b[l], w_out[l], ln1_g[l], ln1_b[l], ln2_g[l], ln2_b[l],
                w_router, router_bias, w_gate[l], w_up[l], w_down[l])

    def lambda_init(l):
        return 0.8 - 0.6 * math.exp(-0.3 * l)

    y_prompt = x_prompt
    cond_ctx = c_ctx[None, None, :]
    ak, av, bk, bv = [], [], [], []
    for l in range(DEPTH):
        y_prompt, kv = trunk_layer(y_prompt, cond_ctx, None, None, lambda_init(l), *layer_weights(l))
        ak.append(kv[0]); av.append(kv[1]); bk.append(kv[2]); bv.append(kv[3])
    state_a_k = jnp.stack(ak, axis=1)
    state_a_v = jnp.stack(av, axis=1)
    state_b_k = jnp.stack(bk, axis=1)
    state_b_v = jnp.stack(bv, axis=1)

    n_tok = x_sample.shape[1]
    n_rows = n_tok // GRID_W
    row = jnp.repeat(jnp.arange(n_rows, dtype=jnp.float32), GRID_W)
    col = jnp.tile(jnp.arange(GRID_W, dtype=jnp.float32), n_rows)
    cond_lat = c[:, None, :]
    y_sample = x_sample
    for l in range(DEPTH):
        ctx = (cache_a_k[:, l], cache_a_v[:, l], cache_b_k[:, l], cache_b_v[:, l])
        y_sample, _ = trunk_layer(y_sample, cond_lat, (row, col), ctx, lambda_init(l), *layer_weights(l))

    return (y_prompt, y_sample, state_a_k, state_a_v, state_b_k, state_b_v)
```

```python
import math
import os
import numpy as np
import concourse.bass as bass
import concourse.mybir as mybir
from concourse.bass_utils import run_bass_kernel_spmd

F32 = mybir.dt.float32
BF16 = mybir.dt.bfloat16
AF = mybir.ActivationFunctionType
ALU = mybir.AluOpType
AX = mybir.AxisListType

D = 1024
L = 4
OFF_KA, OFF_VA, OFF_QB, OFF_KB, OFF_VB, OFF_GATE, D_IN = 1024, 2048, 3072, 4096, 4608, 5120, 7168
ALPHA = (2 * L) ** 0.25
BIG = 1.0e9
NE = 16


def lam_init(l):
    return 0.8 - 0.6 * math.exp(-0.3 * l)


class Buf:
    def __init__(self, t, name):
        self.t = t
        self.name = name
        self.lw = None
        self.rd = {}
        self.dsem = None
        self.dcnt = 0

    def __getitem__(self, idx):
        return self.t[idx]


class Eng:
    def __init__(self, nc, name, e):
        self.e = e
        self.sem = nc.alloc_semaphore("es_" + name)
        self.cnt = 0
        self.seen = {}


class K:
    def __init__(self):
        self.nc = nc = bass.Bass("TRN2", target_bir_lowering=False)
        self.eng = {"pe": Eng(nc, "pe", nc.tensor), "act": Eng(nc, "act", nc.scalar),
                    "dve": Eng(nc, "dve", nc.vector), "pool": Eng(nc, "pool", nc.gpsimd),
                    "sp": Eng(nc, "sp", nc.sync)}

    def sb(self, name, shape, dt):
        return Buf(self.nc.alloc_sbuf_tensor(name, list(shape), dt), name)

    def ps(self, name, shape, dt=F32):
        return Buf(self.nc.alloc_psum_tensor(name, list(shape), dt), name)

    def dram(self, name, shape, dt, kind="Internal", addr_space="Local"):
        return Buf(self.nc.dram_tensor(name, list(shape), dt, kind=kind, addr_space=addr_space).ap(), name)

    def _waits(self, E, reads, writes, skip_self=False):
        need = {}

        def add(ev):
            if ev is None:
                return
            s, v = ev
            if s.num not in need or need[s.num][1] < v:
                need[s.num] = (s, v)

        for b in reads:
            add(b.lw)
        for b in writes:
            add(b.lw)
            for ev in b.rd.values():
                add(ev)
        for num, (s, v) in need.items():
            if skip_self and num == E.sem.num:
                continue
            if E.seen.get(num, 0) >= v:
                continue
            E.e.wait_ge(s, v)
            E.seen[num] = v

    def _record(self, ev, reads, writes):
        for b in writes:
            b.lw = ev
            b.rd = {}
        for b in reads:
            if b in writes:
                continue
            b.rd[ev[0].num] = ev

    def op(self, eng, fn, reads=(), writes=()):
        E = self.eng[eng]
        self._waits(E, reads, writes, skip_self=(eng == "pe"))
        ins = fn(E.e)
        E.cnt += 1
        ins.then_inc(E.sem, 1)
        self._record((E.sem, E.cnt), reads, writes)

    def dma(self, q, dst, dst_ap, src, src_ap, sem_owner=None):
        E = self.eng[q]
        self._waits(E, [src], [dst])
        own = sem_owner if sem_owner is not None else dst
        if own.dsem is None:
            own.dsem = self.nc.alloc_semaphore("ds_" + own.name)
        ins = E.e.dma_start(out=dst_ap, in_=src_ap)
        own.dcnt += 16
        ins.then_inc(own.dsem, 16)
        ev = (own.dsem, own.dcnt)
        self._record(ev, [src], [dst])

    def finish(self, outs):
        E = self.eng["sp"]
        for b in outs:
            if b.lw is not None:
                E.e.wait_ge(b.lw[0], b.lw[1])


CFG = {"state_mode": 0, "do_state": True, "do_attn": True, "cores": 8, "passes": (False, True), "layers": L, "stop": None, "units": 12, "experts": NE}


def build():
    k = K()
    EI = "ExternalInput"
    EO = "ExternalOutput"
    xsT = k.dram("xsT", [D, 1024], F32, EI)
    xpT = k.dram("xpT", [D, 512], F32, EI)
    cak = k.dram("cak", [L, 512, 1024], F32, EI)
    cav = k.dram("cav", [L, 512, 1024], F32, EI)
    cbk = k.dram("cbk", [L, 512, 512], F32, EI)
    cbv = k.dram("cbv", [L, 512, 512], F32, EI)
    cond = k.dram("cond", [128, 8, 2], F32, EI)
    w_mod = k.dram("w_mod", [L, D, 6144], F32, EI)
    w_in = k.dram("w_in", [L, D, D_IN], F32, EI)
    w_br_a = k.dram("w_br_a", [L, D, D], F32, EI)
    w_br_b = k.dram("w_br_b", [L, D, D], F32, EI)
    w_out = k.dram("w_out", [L, D, D], F32, EI)
    w_gate = k.dram("w_gate", [L, NE, D, 512], F32, EI)
    w_up = k.dram("w_up", [L, NE, D, 512], F32, EI)
    w_down = k.dram("w_down", [L, NE, 512, D], F32, EI)
    bmod_d = k.dram("bmod", [128, L, 48], F32, EI)
    lnp_d = k.dram("lnp", [128, L, 4, 8], F32, EI)
    hv_d = k.dram("hv", [128, L, 3], F32, EI)
    lamv_d = k.dram("lamv", [128, 2, L * 128], F32, EI)
    kng_d = k.dram("kng", [128, L, 128], F32, EI)
    wr_d = k.dram("wr", [128, 8, NE], F32, EI)
    rb_d = k.dram("rb", [128, NE], F32, EI)
    rope_d = k.dram("rope", [4, 128, 1024], F32, EI)
    nonce_d = k.dram("nonce", [1, 16], mybir.dt.int32, EI)
    perm_d = k.dram("perm", [128, 2, 128], F32, EI)
    ident_d = k.dram("ident", [128, 128], F32, EI)
    sel_d = k.dram("sel", [NE, NE * 128], F32, EI)
    ypT = k.dram("ypT", [D, 512], F32, EO)
    ysT = k.dram("ysT", [D, 1024], F32, EO)
    sak = k.dram("sak", [2, L, 256, 1024], F32, EO)
    sav = k.dram("sav", [2, L, 256, 1024], F32, EO)
    sbk = k.dram("sbk", [2, L, 256, 512], F32, EO)
    sbv = k.dram("sbv", [2, L, 256, 512], F32, EO)
    ospill = k.dram("ospill", [2 * D, 2048], BF16)
    nc = k.nc
    shk = Buf(None, "shk")
    shv = Buf(None, "shv")
    shk_t = [[nc.dram_tensor(f"shk_{l}_{u}", [2, 128, 1024], BF16, addr_space="Shared").ap() for u in range(12)] for l in range(L)]
    shv_t = [[nc.dram_tensor(f"shv_{l}_{u}", [2, 128, 1024], BF16, addr_space="Shared").ap() for u in range(12)] for l in range(L)]
    shf_t = [nc.dram_tensor(f"shf_{l}", [2, 16], mybir.dt.int32, addr_space="Shared").ap() for l in range(L)]
    pv_me = nc.gpsimd.snap(nc.gpsimd.partition_id() % 2)
    rn = nc.gpsimd.register("rn").__enter__()
    rf = nc.gpsimd.register("rf").__enter__()
    nc.gpsimd.reg_load(rn, nonce_d.t[0:1, 0:1])

    x = [k.sb(f"x{b}", [128, 8, 512], F32) for b in range(2)]
    h = [k.sb(f"h{b}", [128, 8, 512], BF16) for b in range(2)]
    abvs = [Buf(None, "abv0"), Buf(None, "abv1")]
    abs_ = [k.sb(f"ab{i}", [128, 9216], BF16) for i in range(2)]
    ab = abs_[0]
    OB = ab.t[:, 0:8192].rearrange("p (c t) -> p c t", c=16)
    NSLOT = 6
    wsl = [k.sb(f"wsl{i}", [128, 4096], BF16) for i in range(NSLOT)]
    cm = k.sb("cm", [128, 8, 512], BF16)
    NF = 7
    ftmp = [k.sb(f"ft{i}", [128, 512], F32) for i in range(NF)]
    NBT = 6
    btmp = [k.sb(f"bt{i}", [128, 512], BF16) for i in range(NBT)]
    NBP = 3
    bpair = [k.sb(f"bp{i}", [128, 1024], BF16) for i in range(NBP)]
    NRK = 4
    rkeep = [k.sb(f"rk{i}", [128, 512], F32) for i in range(NRK)]
    deferred = []
    ropeC = k.sb("ropeC", [128, 512], F32)
    ropeS = k.sb("ropeS", [128, 512], F32)
    ckf = k.sb("ckf", [128, 4, 128], F32)
    h2f = k.sb("h2f", [128, 8, 128], F32)
    stg = [k.sb(f"stg{i}", [128, 4, 128], F32) for i in range(2)]
    combT = k.sb("combT", [NE, 2048], BF16)
    selb = k.sb("selb", [NE, NE * 128], BF16)
    ident = k.sb("idents", [128, 128], F32)
    onesb = k.sb("onesb", [128, 128], BF16)
    onesf = k.sb("onesf", [128, 128], F32)
    permb = k.sb("permb", [128, 2, 128], BF16)
    bmod = k.sb("bmods", [128, L, 48], F32)
    lnp = k.sb("lnps", [128, L, 4, 8], F32)
    hv = k.sb("hvs", [128, L, 3], F32)
    kng = k.sb("kngs", [128, L, 128], F32)
    wr = k.sb("wrs", [128, 8, NE], F32)
    rb = k.sb("rbs", [128, NE], F32)
    condf = k.sb("condf", [128, 8, 2], F32)
    condb = k.sb("condb", [128, 8, 2], BF16)
    modv = k.sb("modv", [128, 56], F32)
    gsub = k.sb("gsub", [128, L], F32)
    neglam = k.sb("neglam", [128, L], F32)
    rt = k.sb("rt", [128, 64], F32)
    small = k.sb("small", [128, 16], F32)

    PP = [nc.alloc_psum_tensor(f"PP{i}", [128, 1024], F32) for i in range(4)]
    P = [Buf(PP[i // 2][:, (i % 2) * 512:(i % 2 + 1) * 512], f"P{i}") for i in range(8)]
    rot = {}

    def bank(role, ids):
        i = rot.get(role, 0)
        rot[role] = i + 1
        return P[ids[i % len(ids)]]

    def tmpf():
        i = rot.get("tf", 0)
        rot["tf"] = i + 1
        return ftmp[i % NF]

    def tmpb():
        i = rot.get("tb", 0)
        rot["tb"] = i + 1
        return btmp[i % NBT]

    def tmpr():
        i = rot.get("tr", 0)
        rot["tr"] = i + 1
        return rkeep[i % NRK]

    def tmpp():
        i = rot.get("tp", 0)
        rot["tp"] = i + 1
        return bpair[i % NBP]

    def slot():
        i = rot.get("ws", 0)
        rot["ws"] = i + 1
        return wsl[i % NSLOT]

    op = k.op
    dma = k.dma

    dma("sp", ident, ident[:], ident_d, ident_d[:, :])
    dma("sp", bmod, bmod[:], bmod_d, bmod_d[:, :, :])
    dma("sp", lnp, lnp[:], lnp_d, lnp_d[:, :, :, :])
    dma("sp", hv, hv[:], hv_d, hv_d[:, :, :])
    dma("sp", kng, kng[:], kng_d, kng_d[:, :, :])
    dma("sp", wr, wr[:], wr_d, wr_d[:, :, :])
    dma("sp", rb, rb[:], rb_d, rb_d[:, :])
    dma("sp", condf, condf[:], cond, cond[:, :, :])
    dma("pool", permb, permb[:], perm_d, perm_d[:, :, :])
    dma("pool", selb, selb[:], sel_d, sel_d[:, :])
    op("dve", lambda v: v.memset(onesb[:], 1.0), [], [onesb])
    op("dve", lambda v: v.memset(onesf[:], 1.0), [], [onesf])
    op("act", lambda a: a.activation(out=condb[:], in_=condf[:], func=AF.Silu), [condf], [condb])
    la, lb = ftmp[0], ftmp[1]
    dma("sp", la, la[:], lamv_d, lamv_d[:, 0, :])
    dma("sp", lb, lb[:], lamv_d, lamv_d[:, 1, :])
    op("dve", lambda v: v.tensor_tensor(out=la[:], in0=la[:], in1=lb[:], op=ALU.mult), [la, lb], [la])
    op("dve", lambda v: v.tensor_reduce(out=small[:, 0:8], in_=la[:].rearrange("p (g e) -> p g e", e=64),
                                        axis=AX.X, op=ALU.add), [la], [small])
    op("act", lambda a: a.activation(out=small[:, 8:16], in_=small[:, 0:8], func=AF.Exp), [small], [small])
    for l in range(L):
        op("dve", lambda v, l=l: v.scalar_tensor_tensor(out=neglam[:, l:l + 1], in0=small[:, 9 + 2 * l:10 + 2 * l],
                                                        scalar=-lam_init(l), in1=small[:, 8 + 2 * l:9 + 2 * l],
                                                        op0=ALU.add, op1=ALU.subtract), [small], [neglam])
        op("dve", lambda v, l=l: v.tensor_scalar(out=gsub[:, l:l + 1], in0=hv[:, l, 0:1], scalar1=1.0 - lam_init(l),
                                                 scalar2=None, op0=ALU.mult), [hv], [gsub])

    def wpiece(sl, dst_ap, src_buf, src_ap):
        dma("pool", sl, dst_ap, src_buf, src_ap)

    def mm_group(ps_ap, pairs, reads, psb):
        def f(pe):
            n = len(pairs)
            r = None
            for i, (lt, rh) in enumerate(pairs):
                r = pe.matmul(ps_ap, lhsT=lt, rhs=rh, start=(i == 0), stop=(i == n - 1))
            return r
        op("pe", f, reads, [psb])

    def layer_norm_block(xb, l, which):
        pa_ = bank("ln", [6, 7])
        mm_group(pa_[:, :], [(onesf[:], xb[:, c, :]) for c in range(8)], [onesf, xb], pa_)
        pb_ = bank("ln", [6, 7])
        mt, vt, ta, tb_ = tmpf(), tmpf(), tmpf(), tmpf()
        for c in range(8):
            sq = ta if c % 2 == 0 else tb_
            op("act", lambda a, c=c, sq=sq: a.activation(out=sq[:], in_=xb[:, c, :], func=AF.Square), [xb], [sq])
            op("pe", lambda pe, c=c, sq=sq: pe.matmul(pb_[:, :], lhsT=onesf[:], rhs=sq[:], start=(c == 0), stop=(c == 7)),
               [onesf, sq], [pb_])
        op("act", lambda a: a.mul(out=mt[:], in_=pa_[:, :], mul=1.0 / D), [pa_], [mt])
        op("dve", lambda v: v.tensor_tensor(out=vt[:], in0=mt[:], in1=mt[:], op=ALU.mult), [mt], [vt])
        op("dve", lambda v: v.scalar_tensor_tensor(out=vt[:], in0=pb_[:, :], scalar=1.0 / D, in1=vt[:],
                                                   op0=ALU.mult, op1=ALU.subtract), [pb_, vt], [vt])
        op("act", lambda a: a.activation(out=vt[:], in_=vt[:], func=AF.Ln, bias=1e-6, scale=1.0), [vt], [vt])
        op("act", lambda a: a.activation(out=vt[:], in_=vt[:], func=AF.Exp, scale=-0.5), [vt], [vt])
        for c in range(8):
            t = ta if c % 2 == 0 else tb_
            op("dve", lambda v, c=c, t=t: v.tensor_tensor(out=t[:], in0=xb[:, c, :], in1=mt[:], op=ALU.subtract), [xb, mt], [t])
            op("dve", lambda v, t=t: v.tensor_tensor(out=t[:], in0=t[:], in1=vt[:], op=ALU.mult), [t, vt], [t])
            op("act", lambda a, c=c, t=t: a.activation(out=xb[:, c, :], in_=t[:], func=AF.Identity,
                                                       bias=lnp[:, l, 2 * which + 1, c:c + 1],
                                                       scale=lnp[:, l, 2 * which, c:c + 1]), [t, lnp], [xb])

    def modulate(NB, l, which):
        for b in range(NB):
            for c in range(8):
                op("act", lambda a, b=b, c=c: a.activation(out=h[b][:, c, :], in_=x[b][:, c, :], func=AF.Identity,
                                                           bias=modv[:, 24 * which + c:24 * which + c + 1],
                                                           scale=modv[:, 48 + 8 * which + c:48 + 8 * which + c + 1]),
                   [x[b], modv], [h[b]])

    def run_pass(sample):
        T = 1024 if sample else 512
        NB = T // 512
        ccol = 1 if sample else 0
        xin = xsT if sample else xpT
        yout = ysT if sample else ypT
        koff = 512 if sample else 0
        nkc = 20 if sample else 4
        if sample:
            seqs = [(list(range(20)), [(q * 512, 512) for q in range(2)])]
        else:
            seqs = [([0, 1], [(0, 256)]), ([2, 3], [(256, 256)])]
        for b in range(NB):
            dma("sp", x[b], x[b][:], xin, xin.t.rearrange("(c p) t -> p c t", p=128)[:, :, b * 512:(b + 1) * 512])

        for l in range(CFG["layers"]):
            pm = P[6]
            for piece in range(12):
                sl = slot()
                wv = sl.t[:, :].rearrange("p (c f) -> p c f", c=8)
                wpiece(sl, wv, w_mod, w_mod.t[l].rearrange("(c p) f -> p c f", p=128)[:, :, piece * 512:(piece + 1) * 512])
                for jj in range(4):
                    j = piece * 4 + jj
                    mm_group(pm[:, j:j + 1], [(wv[:, c, jj * 128:(jj + 1) * 128], condb[:, c, ccol:ccol + 1]) for c in range(8)],
                             [sl, condb], pm)
            op("dve", lambda v: v.tensor_tensor(out=modv[:, 0:48], in0=pm[:, 0:48], in1=bmod[:, l, :], op=ALU.add),
               [pm, bmod], [modv])
            op("dve", lambda v: v.tensor_scalar(out=modv[:, 48:56], in0=modv[:, 8:16], scalar1=1.0, scalar2=None, op0=ALU.add),
               [modv], [modv])
            if CFG["stop"] == "mod":
                continue
            modulate(NB, l, 0)

            def unit(u, mode):
                ai = rot.get("ab", 0)
                rot["ab"] = ai + 1
                ab = abs_[ai % 2]
                abv = abvs[ai % 2]
                QT = ab.t[:, 0:4096].rearrange("p (h t) -> p h t", h=2)
                KT = ab.t[:, 4096:6656]
                Vt = ab.t[:, 6656:9216].rearrange("p (c d) -> p c d", d=128)
                do_q = mode in ("all", "q")
                do_kv = mode in ("all", "kv")
                diff = u < 8
                g = u - 8
                nq = 128 if diff else 256
                qcol = u * 128 if diff else OFF_QB + g * 256
                kcol = OFF_KA + u * 128 if diff else OFF_KB + g * 128
                vcol = OFF_VA + u * 128 if diff else OFF_VB + g * 128
                sl = slot()
                wv = sl.t[:, :].rearrange("p (c f) -> p c f", c=8)
                win = w_in.t[l].rearrange("(c p) f -> p c f", p=128)
                if do_q:
                    wpiece(sl, wv[:, :, 0:nq], w_in, win[:, :, qcol:qcol + nq])
                if do_kv:
                    wpiece(sl, wv[:, :, nq:nq + 128], w_in, win[:, :, kcol:kcol + 128])
                    wpiece(sl, wv[:, :, nq + 128:nq + 256], w_in, win[:, :, vcol:vcol + 128])
                ckd = cak if diff else cbk
                cvd = cav if diff else cbv
                hcol = (u if diff else g) * 128
                permi = 0 if diff else 1
                if sample and mode == "q":
                    for hf in range(2):
                        dma("sp", ab, KT[:, 512 + hf * 1024:1536 + hf * 1024], shk, shk_t[l][u][hf])
                        dma("sp", ab, ab.t[:, 6656 + 512 + hf * 1024:6656 + 1536 + hf * 1024], shv, shv_t[l][u][hf])
                    dma("sp", ckf, ckf[:], ckd, ckd.t[l].rearrange("(c p) f -> p c f", p=128)[:, :, hcol:hcol + 128])
                    pt_ = bank("pm", [6, 7])
                    for c in range(4):
                        op("pe", lambda pe, c=c: pe.transpose(pt_[:, c * 128:(c + 1) * 128], ckf[:, c, :], ident[:]),
                           [ckf, ident], [pt_])
                    op("act", lambda a: a.copy(out=KT[:, 0:512], in_=pt_[:, :]), [pt_], [ab])
                    dma("pool", ab, Vt[:, 0:4, :], cvd, cvd.t[l].rearrange("(c p) f -> p c f", p=128)[:, :, hcol:hcol + 128],
                        sem_owner=abv)

                def rope_finish(src_b, dst_ap, b):
                    pp = bank("pm", [6, 7])
                    op("pe", lambda pe: pe.matmul(pp[:, :], lhsT=permb[:, permi, :], rhs=src_b[:], start=True, stop=True),
                       [permb, src_b], [pp])
                    t1, t2 = tmpf(), tmpf()
                    op("dve", lambda v: v.tensor_tensor(out=t1[:], in0=pp[:, :], in1=ropeS[:], op=ALU.mult), [pp, ropeS], [t1])
                    op("dve", lambda v: v.tensor_tensor(out=t2[:], in0=src_b[:], in1=ropeC[:], op=ALU.mult), [src_b, ropeC], [t2])
                    op("dve", lambda v: v.tensor_tensor(out=dst_ap, in0=t1[:], in1=t2[:], op=ALU.add), [t1, t2], [ab])

                def qk_post(pq, n, dst_ap, b, gcol):
                    if diff:
                        if not sample:
                            op("act", lambda a: a.copy(out=dst_ap, in_=pq[:, 0:n]), [pq], [ab])
                            return
                        tb = tmpb()
                        op("act", lambda a: a.copy(out=tb[:], in_=pq[:, :]), [pq], [tb])
                        rope_finish(tb, dst_ap, b)
                        return
                    qf, sqb, rs = tmpf(), tmpb(), tmpf()
                    op("act", lambda a: a.copy(out=qf[:, 0:n], in_=pq[:, 0:n]), [pq], [qf])
                    op("act", lambda a: a.activation(out=sqb[:, 0:n], in_=pq[:, 0:n], func=AF.Square), [pq], [sqb])
                    pp = bank("pm", [6, 7])
                    op("pe", lambda pe: pe.matmul(pp[:, 0:n], lhsT=onesb[:], rhs=sqb[:, 0:n], start=True, stop=True),
                       [onesb, sqb], [pp])
                    op("act", lambda a: a.activation(out=rs[:, 0:n], in_=pp[:, 0:n], func=AF.Ln, bias=1e-6, scale=1.0 / 128),
                       [pp], [rs])
                    op("act", lambda a: a.activation(out=rs[:, 0:n], in_=rs[:, 0:n], func=AF.Exp, scale=-0.5), [rs], [rs])
                    if not sample:
                        op("dve", lambda v: v.scalar_tensor_tensor(out=dst_ap, in0=qf[:, 0:n], scalar=hv[:, l, gcol:gcol + 1],
                                                                   in1=rs[:, 0:n], op0=ALU.mult, op1=ALU.mult), [qf, hv, rs], [ab])
                        return
                    tb = tmpb()
                    op("dve", lambda v: v.scalar_tensor_tensor(out=tb[:], in0=qf[:], scalar=hv[:, l, gcol:gcol + 1],
                                                               in1=rs[:], op0=ALU.mult, op1=ALU.mult), [qf, hv, rs], [tb])
                    rope_finish(tb, dst_ap, b)

                for b in range(NB):
                    if sample:
                        ri = 0 if diff else 2
                        dma("sp", ropeC, ropeC[:], rope_d, rope_d[ri, :, b * 512:(b + 1) * 512])
                        dma("sp", ropeS, ropeS[:], rope_d, rope_d[ri + 1, :, b * 512:(b + 1) * 512])
                    for hh in (range(1 if diff else 2) if do_q else []):
                        pq = bank("pm", [6, 7])
                        mm_group(pq[:, :], [(wv[:, c, hh * 128:(hh + 1) * 128], h[b][:, c, :]) for c in range(8)], [sl, h[b]], pq)
                        qk_post(pq, 512, QT[:, hh, b * 512:(b + 1) * 512], b, 1)
                    if not do_kv:
                        continue
                    pq = bank("pm", [6, 7])
                    mm_group(pq[:, :], [(wv[:, c, nq:nq + 128], h[b][:, c, :]) for c in range(8)], [sl, h[b]], pq)
                    qk_post(pq, 512, KT[:, b * 512:(b + 1) * 512], b, 2)
                    pv = bank("pm", [6, 7])
                    for t in range(4):
                        mm_group(pv[:, t * 128:(t + 1) * 128],
                                 [(h[b][:, c, t * 128:(t + 1) * 128], wv[:, c, nq + 128:nq + 256]) for c in range(8)], [sl, h[b]], pv)
                    kc0 = b * 4
                    op("act", lambda a, pv=pv, kc0=kc0: a.copy(out=Vt[:, kc0:kc0 + 4, :],
                                                               in_=pv[:, :].rearrange("p (c d) -> p c d", d=128)), [pv], [ab])
                    if not sample and CFG["do_state"]:
                        so_v = sav if diff else sbv
                        so_k = sak if diff else sbk
                        s1 = stg[0]
                        op("act", lambda a, pv=pv: a.copy(out=s1[:], in_=pv[:, :].rearrange("p (c d) -> p c d", d=128)),
                           [pv], [s1])
                        for t4 in (range(4) if CFG["state_mode"] != 1 else []):
                            dma("sp", so_v, so_v.t[t4 // 2, l, (t4 % 2) * 128:(t4 % 2 + 1) * 128, hcol:hcol + 128],
                                s1, s1[:, t4, :])
                        if CFG["state_mode"] == 2:
                            continue
                        pk = bank("pm", [6, 7])
                        for t in range(4):
                            mm_group(pk[:, t * 128:(t + 1) * 128],
                                     [(h[b][:, c, t * 128:(t + 1) * 128], wv[:, c, nq:nq + 128]) for c in range(8)], [sl, h[b]], pk)
                        s2 = stg[1]
                        if diff:
                            op("act", lambda a, pk=pk: a.copy(out=s2[:], in_=pk[:, :].rearrange("p (c d) -> p c d", d=128)),
                               [pk], [s2])
                        else:
                            for t in range(4):
                                junk = tmpf()
                                op("act", lambda a, t=t, pk=pk, junk=junk: a.activation(
                                    out=junk[:, 0:128], in_=pk[:, t * 128:(t + 1) * 128], func=AF.Square), [pk], [junk])
                                op("dve", lambda v, t=t, junk=junk: v.tensor_reduce(out=small[:, t:t + 1], in_=junk[:, 0:128],
                                                                                   axis=AX.X, op=ALU.add), [junk], [small])
                            op("act", lambda a: a.activation(out=small[:, 4:8], in_=small[:, 0:4], func=AF.Sqrt, bias=1e-6,
                                                             scale=1.0 / 128), [small], [small])
                            op("dve", lambda v: v.reciprocal(out=small[:, 4:8], in_=small[:, 4:8]), [small], [small])
                            for t in range(4):
                                op("dve", lambda v, t=t, pk=pk: v.scalar_tensor_tensor(
                                    out=s2[:, t, :], in0=pk[:, t * 128:(t + 1) * 128], scalar=small[:, 4 + t:5 + t],
                                    in1=kng[:, l, :], op0=ALU.mult, op1=ALU.mult), [pk, small, kng], [s2])
                        for t4 in (range(4) if CFG["state_mode"] not in (1, 2) else []):
                            dma("sp", so_k, so_k.t[t4 // 2, l, (t4 % 2) * 128:(t4 % 2 + 1) * 128, hcol:hcol + 128],
                                s2, s2[:, t4, :])

                if mode == "kv":
                    dma("pool", shk, shk_t[l][u][bass.ds(pv_me, 1), :, :], ab, KT[:, 0:1024])
                    dma("pool", shv, shv_t[l][u][bass.ds(pv_me, 1), :, :], ab, ab.t[:, 6656:6656 + 1024])
                    return None
                scale = 0.125 if diff else 128 ** -0.5
                if diff:
                    subs = [(0, slice(0, 64)), (0, slice(64, 128))]
                else:
                    subs = [(0, slice(0, 128)), (1, slice(0, 128))]

                def attn():
                    for (kcs, qtiles) in (seqs if CFG["do_attn"] else []):
                        for (q0, n) in qtiles:
                            R = []
                            for (hh, rows) in subs:
                                po = P[4]
                                psum_ = P[5]
                                npair = len(kcs) // 2
                                pts = [None] * npair

                                def s_and_exp(i):
                                    j = rot.get("sp", 0)
                                    rot["sp"] = j + 1
                                    j = j % 2
                                    lo, hi = P[2 * j], P[2 * j + 1]

                                    def f(pe):
                                        r_ = None
                                        for t_, pb_ in enumerate((lo, hi)):
                                            kc = kcs[2 * i + t_]
                                            r_ = pe.matmul(pb_[:, 0:n], lhsT=KT[rows, kc * 128:(kc + 1) * 128],
                                                           rhs=QT[rows, hh, q0:q0 + n], start=True, stop=True)
                                        return r_
                                    op("pe", f, [ab], [lo, hi])
                                    ptile = tmpp()
                                    op("act", lambda a: a.activation(
                                        out=ptile[:, :].rearrange("p (k q) -> p k q", k=2)[:, :, 0:n],
                                        in_=PP[j][:, :].rearrange("p (k q) -> p k q", k=2)[:, :, 0:n], func=AF.Exp, scale=scale),
                                       [lo, hi], [ptile])
                                    pts[i] = ptile

                                def pv_sum(i):
                                    ptile = pts[i]

                                    def f(pe):
                                        r_ = None
                                        for t_ in range(2):
                                            kc = kcs[2 * i + t_]
                                            first = (i == 0 and t_ == 0)
                                            last = (i == npair - 1 and t_ == 1)
                                            pe.matmul(po[:, 0:n], lhsT=Vt[:, kc, :], rhs=ptile[:, t_ * 512:t_ * 512 + n],
                                                      start=first, stop=last)
                                            r_ = pe.matmul(psum_[:, 0:n], lhsT=onesb[:], rhs=ptile[:, t_ * 512:t_ * 512 + n],
                                                           start=first, stop=last)
                                        return r_
                                    op("pe", f, [ab, ptile, onesb], [po, psum_])

                                for i in range(npair):
                                    s_and_exp(i)
                                    if i >= 1:
                                        pv_sum(i - 1)
                                    if i == 1 and deferred:
                                        deferred.pop(0)()
                                pv_sum(npair - 1)
                                if npair < 2 and deferred:
                                    deferred.pop(0)()
                                rec, r = tmpf(), tmpr()
                                op("dve", lambda v: v.tensor_scalar(out=r[:, 0:n], in0=po[:, 0:n], scalar1=1.0, scalar2=None,
                                                                    op0=ALU.mult), [po], [r])
                                op("act", lambda a: a.activation(out=rec[:, 0:n], in_=psum_[:, 0:n], func=AF.Ln), [psum_], [rec])
                                op("act", lambda a: a.activation(out=rec[:, 0:n], in_=rec[:, 0:n], func=AF.Exp, scale=-1.0), [rec], [rec])
                                op("dve", lambda v: v.tensor_tensor(out=r[:, 0:n], in0=r[:, 0:n], in1=rec[:, 0:n], op=ALU.mult),
                                   [r, rec], [r])
                                R.append(r)

                            def fin(R=R, q0=q0, n=n):
                                if diff:
                                    o, sqb, rs = tmpf(), tmpb(), tmpf()
                                    op("dve", lambda v: v.scalar_tensor_tensor(out=o[:, 0:n], in0=R[1][:, 0:n], scalar=neglam[:, l:l + 1],
                                                                               in1=R[0][:, 0:n], op0=ALU.mult, op1=ALU.add),
                                       [R[0], R[1], neglam], [o])
                                    op("act", lambda a: a.activation(out=sqb[:, 0:n], in_=o[:, 0:n], func=AF.Square), [o], [sqb])
                                    pp = P[7]
                                    op("pe", lambda pe: pe.matmul(pp[:, 0:n], lhsT=onesb[:], rhs=sqb[:, 0:n], start=True, stop=True),
                                       [onesb, sqb], [pp])
                                    op("act", lambda a: a.activation(out=rs[:, 0:n], in_=pp[:, 0:n], func=AF.Ln, bias=1e-6,
                                                                     scale=1.0 / 128), [pp], [rs])
                                    op("act", lambda a: a.activation(out=rs[:, 0:n], in_=rs[:, 0:n], func=AF.Exp, scale=-0.5), [rs], [rs])
                                    ot = tmpb()
                                    op("dve", lambda v: v.scalar_tensor_tensor(out=ot[:, 0:n], in0=o[:, 0:n], scalar=gsub[:, l:l + 1],
                                                                               in1=rs[:, 0:n], op0=ALU.mult, op1=ALU.mult),
                                       [o, gsub, rs], [ot])
                                    dma("sp", ospill, ospill[u * 128:(u + 1) * 128, q0:q0 + n], ot, ot[:, 0:n])
                                else:
                                    for hh2 in range(2):
                                        ot = tmpb()
                                        op("act", lambda a, hh2=hh2, ot=ot: a.copy(out=ot[:, 0:n], in_=R[hh2][:, 0:n]), [R[hh2]], [ot])
                                        ch = 8 + g * 2 + hh2
                                        dma("sp", ospill, ospill[ch * 128:(ch + 1) * 128, q0:q0 + n], ot, ot[:, 0:n])
                            deferred.append(fin)
                return attn

            def run_units(mode):
                pend = None
                for u in range(CFG["units"]):
                    a_ = unit(u, mode)
                    if pend is not None:
                        pend()
                    pend = a_
                if pend is not None:
                    pend()
                while deferred:
                    deferred.pop(0)()

            if sample:
                run_units("kv")

                def hs(g_):
                    g_.reg_save(shf_t[l][bass.ds(pv_me, 1), 0:1], rn)
                    for slot_ in range(2):
                        g_.reg_mov(rf, 1)
                        with g_.While(rf):
                            g_.reg_load(rf, shf_t[l][slot_:slot_ + 1, 0:1])
                            g_.reg_sub(rf, rf, rn)
                    return g_.nop()
                op("pool", hs, [shk, shv], [shk, shv])
                run_units("q")
            else:
                run_units("all")
            if CFG["stop"] == "attn":
                continue
            for b in range(NB):
                dma("sp", ab, OB, ospill, ospill.t.rearrange("(c p) t -> p c t", p=128)[:, :, b * 512:(b + 1) * 512])
                win = w_in.t[l].rearrange("(c p) f -> p c f", p=128)
                for j in range(8):
                    sl = slot()
                    wv = sl.t[:, :].rearrange("p (c f) -> p c f", c=8)
                    wpiece(sl, wv[:, :, 0:128], w_br_a, w_br_a.t[l].rearrange("(c p) f -> p c f", p=128)[:, :, j * 128:(j + 1) * 128])
                    wpiece(sl, wv[:, :, 128:256], w_br_b, w_br_b.t[l].rearrange("(c p) f -> p c f", p=128)[:, :, j * 128:(j + 1) * 128])
                    wpiece(sl, wv[:, :, 256:384], w_in, win[:, :, OFF_GATE + j * 128:OFF_GATE + (j + 1) * 128])
                    wpiece(sl, wv[:, :, 384:512], w_in, win[:, :, OFF_GATE + D + j * 128:OFF_GATE + D + (j + 1) * 128])
                    mm_group(P[0][:, :], [(wv[:, c, 0:128], OB[:, c, :]) for c in range(8)], [sl, ab], P[0])
                    mm_group(P[1][:, :], [(wv[:, c, 128:256], OB[:, 8 + c, :]) for c in range(8)], [sl, ab], P[1])
                    mm_group(P[2][:, :], [(wv[:, c, 256:384], h[b][:, c, :]) for c in range(8)], [sl, h[b]], P[2])
                    mm_group(P[3][:, :], [(wv[:, c, 384:512], h[b][:, c, :]) for c in range(8)], [sl, h[b]], P[3])
                    sa, sb_ = tmpf(), tmpf()
                    op("act", lambda a, sa=sa: a.activation(out=sa[:], in_=P[2][:, :], func=AF.Sigmoid), [P[2]], [sa])
                    op("act", lambda a, sb_=sb_: a.activation(out=sb_[:], in_=P[3][:, :], func=AF.Sigmoid), [P[3]], [sb_])
                    op("dve", lambda v, sa=sa: v.tensor_tensor(out=sa[:], in0=P[0][:, :], in1=sa[:], op=ALU.mult), [P[0], sa], [sa])
                    op("dve", lambda v, sb_=sb_: v.tensor_tensor(out=sb_[:], in0=P[1][:, :], in1=sb_[:], op=ALU.mult), [P[1], sb_], [sb_])
                    op("dve", lambda v, j=j, sa=sa, sb_=sb_: v.tensor_tensor(out=cm[:, j, :], in0=sa[:], in1=sb_[:], op=ALU.add),
                       [sa, sb_], [cm])
                for c in range(8):
                    op("act", lambda a, c=c: a.mul(out=x[b][:, c, :], in_=x[b][:, c, :], mul=ALPHA), [x[b]], [x[b]])
                for jh in range(2):
                    sl = slot()
                    wv = sl.t[:, :].rearrange("p (c f) -> p c f", c=8)
                    wpiece(sl, wv, w_out, w_out.t[l].rearrange("(c p) f -> p c f", p=128)[:, :, jh * 512:(jh + 1) * 512])
                    for jj in range(4):
                        j = jh * 4 + jj
                        pmx = bank("proj", [4, 5])
                        mm_group(pmx[:, :], [(wv[:, c, jj * 128:(jj + 1) * 128], cm[:, c, :]) for c in range(8)], [sl, cm], pmx)
                        op("dve", lambda v, j=j, pmx=pmx: v.scalar_tensor_tensor(
                            out=x[b][:, j, :], in0=pmx[:, :], scalar=modv[:, 16 + j:17 + j], in1=x[b][:, j, :],
                            op0=ALU.mult, op1=ALU.add), [pmx, modv, x[b]], [x[b]])
                layer_norm_block(x[b], l, 0)

            if CFG["stop"] == "phaseC":
                continue
            op("dve", lambda v: v.tensor_scalar(out=modv[:, 48:56], in0=modv[:, 32:40], scalar1=1.0, scalar2=None, op0=ALU.add),
               [modv], [modv])
            for b in range(NB):
                for c in range(8):
                    op("act", lambda a, b=b, c=c: a.activation(out=h[b][:, c, :], in_=x[b][:, c, :], func=AF.Identity,
                                                               bias=modv[:, 24 + c:25 + c], scale=modv[:, 48 + c:49 + c]),
                       [x[b], modv], [h[b]])
                for t in range(4):
                    for c in range(8):
                        op("act", lambda a, b=b, c=c, t=t: a.activation(out=h2f[:, c, :], in_=x[b][:, c, t * 128:(t + 1) * 128],
                                                                       func=AF.Identity, bias=modv[:, 24 + c:25 + c],
                                                                       scale=modv[:, 48 + c:49 + c]), [x[b], modv], [h2f])
                    pr = P[7]
                    mm_group(pr[:, 0:NE], [(h2f[:, c, :], wr[:, c, :]) for c in range(8)], [h2f, wr], pr)
                    S_, BZ, TT, E1, WW = rt[:, 0:16], rt[:, 16:32], rt[:, 32:48], rt[:, 48:64], rt[:, 32:48]
                    sm = small
                    op("act", lambda a: a.activation(out=S_, in_=pr[:, 0:NE], func=AF.Sigmoid), [pr], [rt])
                    op("dve", lambda v: v.tensor_tensor(out=BZ, in0=S_, in1=rb[:], op=ALU.add), [rt, rb], [rt])
                    op("dve", lambda v: v.tensor_reduce(out=sm[:, 0:4], in_=BZ.rearrange("p (g e) -> p g e", e=4), axis=AX.X, op=ALU.max),
                       [rt], [sm])
                    for gg in range(4):
                        op("dve", lambda v, gg=gg: v.tensor_scalar(out=TT[:, 4 * gg:4 * gg + 4], in0=BZ[:, 4 * gg:4 * gg + 4],
                                                                   scalar1=sm[:, gg:gg + 1], scalar2=-BIG, op0=ALU.is_equal,
                                                                   op1=ALU.mult), [rt, sm], [rt])
                    op("dve", lambda v: v.tensor_tensor(out=TT, in0=TT, in1=BZ, op=ALU.add), [rt], [rt])
                    op("dve", lambda v: v.tensor_reduce(out=sm[:, 4:8], in_=TT.rearrange("p (g e) -> p g e", e=4), axis=AX.X, op=ALU.max),
                       [rt], [sm])
                    op("dve", lambda v: v.tensor_tensor(out=sm[:, 0:4], in0=sm[:, 0:4], in1=sm[:, 4:8], op=ALU.add), [sm], [sm])
                    op("dve", lambda v: v.tensor_reduce(out=sm[:, 8:9], in_=sm[:, 0:4], axis=AX.X, op=ALU.max), [sm], [sm])
                    op("dve", lambda v: v.tensor_scalar(out=sm[:, 4:8], in0=sm[:, 0:4], scalar1=sm[:, 8:9], scalar2=BIG,
                                                        op0=ALU.is_ge, op1=ALU.mult), [sm], [sm])
                    op("dve", lambda v: v.tensor_scalar(out=sm[:, 4:8], in0=sm[:, 4:8], scalar1=-BIG, scalar2=None,
                                                        op0=ALU.add), [sm], [sm])
                    for gg in range(4):
                        op("dve", lambda v, gg=gg: v.tensor_scalar(out=TT[:, 4 * gg:4 * gg + 4], in0=BZ[:, 4 * gg:4 * gg + 4],
                                                                   scalar1=sm[:, 4 + gg:5 + gg], scalar2=None, op0=ALU.add),
                           [rt, sm], [rt])
                    op("dve", lambda v: v.tensor_reduce(out=sm[:, 9:10], in_=TT, axis=AX.X, op=ALU.max), [rt], [sm])
                    op("dve", lambda v: v.tensor_scalar(out=E1, in0=TT, scalar1=sm[:, 9:10], scalar2=None, op0=ALU.is_equal),
                       [rt, sm], [rt])
                    op("dve", lambda v: v.scalar_tensor_tensor(out=TT, in0=E1, scalar=-BIG, in1=TT, op0=ALU.mult, op1=ALU.add),
                       [rt], [rt])
                    op("dve", lambda v: v.tensor_reduce(out=sm[:, 10:11], in_=TT, axis=AX.X, op=ALU.max), [rt], [sm])
                    op("dve", lambda v: v.scalar_tensor_tensor(out=E1, in0=TT, scalar=sm[:, 10:11], in1=E1, op0=ALU.is_equal,
                                                               op1=ALU.add), [rt, sm], [rt])
                    op("dve", lambda v: v.tensor_tensor(out=WW, in0=S_, in1=E1, op=ALU.mult), [rt], [rt])
                    op("dve", lambda v: v.tensor_reduce(out=sm[:, 11:12], in_=WW, axis=AX.X, op=ALU.add), [rt], [sm])
                    op("dve", lambda v: v.reciprocal(out=sm[:, 11:12], in_=sm[:, 11:12]), [sm], [sm])
                    op("dve", lambda v: v.tensor_scalar(out=WW, in0=WW, scalar1=sm[:, 11:12], scalar2=None, op0=ALU.mult),
                       [rt, sm], [rt])
                    ptr = P[6]
                    op("pe", lambda pe: pe.transpose(ptr[0:NE, 0:128], WW, ident[:]), [rt, ident], [ptr])
                    tcol = b * 512 + t * 128
                    op("act", lambda a, tcol=tcol: a.copy(out=combT[:, tcol:tcol + 128], in_=ptr[0:NE, 0:128]), [ptr], [combT])
                for c in range(8):
                    op("act", lambda a, b=b, c=c: a.mul(out=x[b][:, c, :], in_=x[b][:, c, :], mul=ALPHA), [x[b]], [x[b]])
            for e in range(CFG["experts"]):
                sg_, su_, sd_ = slot(), slot(), slot()
                wg = sg_.t[:, :].rearrange("p (c f) -> p c f", c=8)
                wu = su_.t[:, :].rearrange("p (c f) -> p c f", c=8)
                wd = sd_.t[:, :].rearrange("p (c f) -> p c f", c=4)
                wpiece(sg_, wg, w_gate, w_gate.t[l, e].rearrange("(c p) f -> p c f", p=128))
                wpiece(su_, wu, w_up, w_up.t[l, e].rearrange("(c p) f -> p c f", p=128))
                wpiece(sd_, wd, w_down, w_down.t[l, e].rearrange("(c p) f -> p c f", p=128))
                for b in range(NB):
                    pbc = P[6]
                    op("pe", lambda pe, b=b: pe.matmul(pbc[:, :], lhsT=selb[:, e * 128:(e + 1) * 128],
                                                       rhs=combT[:, b * 512:(b + 1) * 512], start=True, stop=True),
                       [selb, combT], [pbc])
                    bcs = tmpf()
                    op("act", lambda a, bcs=bcs: a.copy(out=bcs[:], in_=pbc[:, :]), [pbc], [bcs])
                    for f in range(4):
                        pg = bank("g", [0, 1])
                        pu = bank("u", [2, 3])
                        mm_group(pg[:, :], [(wg[:, c, f * 128:(f + 1) * 128], h[b][:, c, :]) for c in range(8)], [sg_, h[b]], pg)
                        mm_group(pu[:, :], [(wu[:, c, f * 128:(f + 1) * 128], h[b][:, c, :]) for c in range(8)], [su_, h[b]], pu)
                        sgt = tmpf()
                        op("act", lambda a, pg=pg, sgt=sgt: a.activation(out=sgt[:], in_=pg[:, :], func=AF.Silu), [pg], [sgt])
                        op("dve", lambda v, pu=pu, sgt=sgt: v.tensor_tensor(out=sgt[:], in0=pu[:, :], in1=sgt[:], op=ALU.mult),
                           [pu, sgt], [sgt])
                        op("dve", lambda v, f=f, sgt=sgt, bcs=bcs: v.tensor_tensor(out=cm[:, f, :], in0=sgt[:], in1=bcs[:], op=ALU.mult),
                           [sgt, bcs], [cm])
                    for j in range(8):
                        py = bank("proj", [4, 5])
                        mm_group(py[:, :], [(wd[:, f, j * 128:(j + 1) * 128], cm[:, f, :]) for f in range(4)], [sd_, cm], py)
                        op("dve", lambda v, b=b, j=j, py=py: v.scalar_tensor_tensor(
                            out=x[b][:, j, :], in0=py[:, :], scalar=modv[:, 40 + j:41 + j], in1=x[b][:, j, :],
                            op0=ALU.mult, op1=ALU.add), [py, modv, x[b]], [x[b]])
            for b in range(NB):
                layer_norm_block(x[b], l, 1)

        for b in range(NB):
            dma("sp", yout, yout.t.rearrange("(c p) t -> p c t", p=128)[:, :, b * 512:(b + 1) * 512], x[b], x[b][:])

    for p_ in CFG["passes"]:
        run_pass(p_)
    k.finish([ypT, ysT, sak, sav, sbk, sbv])
    return k


def _rope_tables():
    tok = np.arange(2048)
    row = (tok // 64).astype(np.float64)
    col = (tok % 64).astype(np.float64)
    out = np.zeros((4, 128, 2048), np.float32)
    for p in range(128):
        pp = p % 64
        axis, half, f = pp // 32, (pp % 32) // 16, pp % 16
        inv = 10000.0 ** (-f / 16.0)
        ang = (row if axis == 0 else col) * np.float32(inv).astype(np.float64)
        ang = (np.float32(row if axis == 0 else col) * np.float32(inv)).astype(np.float32)
        out[0, p] = np.cos(ang)
        out[1, p] = np.sin(ang) * (-1.0 if half == 0 else 1.0)
    for p in range(128):
        axis, half, f = p // 64, (p % 64) // 32, p % 32
        inv = np.float32(10000.0) ** np.float32(-f / 32.0)
        ang = (np.float32(row if axis == 0 else col) * np.float32(inv)).astype(np.float32)
        out[2, p] = np.cos(ang)
        out[3, p] = np.sin(ang) * (-1.0 if half == 0 else 1.0)
    perm = np.zeros((128, 2, 128), np.float32)
    for m in range(128):
        pp = m % 64
        half = (pp % 32) // 16
        src = m + 16 if half == 0 else m - 16
        perm[src, 0, m] = 1.0
        half = (m % 64) // 32
        src = m + 32 if half == 0 else m - 32
        perm[src, 1, m] = 1.0
    return out, perm


_CACHE = {}


def kernel(x_prompt, x_sample, cache_a_k, cache_a_v, cache_b_k, cache_b_v, c, c_ctx,
           w_mod, b_mod, w_in, lam_q1, lam_k1, lam_q2, lam_k2, subln_g, qn_g, kn_g,
           w_br_a, w_br_b, w_out, ln1_g, ln1_b, ln2_g, ln2_b, w_router, router_bias,
           w_gate, w_up, w_down):
    f = lambda a: np.ascontiguousarray(np.asarray(a, dtype=np.float32))
    x_prompt, x_sample = f(x_prompt), f(x_sample)
    rope, perm = _rope_tables()
    nonce = np.full((1, 16), (int.from_bytes(os.urandom(3), "little") + 1), np.int32)
    sel = np.zeros((NE, NE * 128), np.float32)
    for e in range(NE):
        sel[e, e * 128:(e + 1) * 128] = 1.0
    shared = {
        "w_mod": f(w_mod), "w_in": f(w_in), "w_br_a": f(w_br_a), "w_br_b": f(w_br_b), "w_out": f(w_out),
        "w_gate": f(w_gate), "w_up": f(w_up), "w_down": f(w_down),
        "bmod": f(np.asarray(b_mod).reshape(L, 48, 128).transpose(2, 0, 1)),
        "lnp": f(np.stack([np.asarray(a).reshape(L, 8, 128) for a in (ln1_g, ln1_b, ln2_g, ln2_b)], 1).transpose(3, 0, 1, 2)),
        "hv": f(np.stack([np.asarray(subln_g), np.asarray(qn_g), np.asarray(kn_g)], -1).transpose(1, 0, 2)),
        "lamv": f(np.broadcast_to(np.stack([
            np.concatenate([np.asarray(lam_q1), np.asarray(lam_q2)], -1).reshape(-1),
            np.concatenate([np.asarray(lam_k1), np.asarray(lam_k2)], -1).reshape(-1)], 0)[None], (128, 2, L * 128))),
        "kng": f(np.broadcast_to(np.asarray(kn_g)[None], (128, L, 128))),
        "wr": f(np.asarray(w_router).reshape(8, 128, NE).transpose(1, 0, 2)),
        "rb": f(np.broadcast_to(np.asarray(router_bias)[None], (128, NE))),
        "perm": perm, "ident": np.eye(128, dtype=np.float32), "sel": sel,
    }
    cak, cav = f(cache_a_k).reshape(4, L, 512, 1024), f(cache_a_v).reshape(4, L, 512, 1024)
    cbk, cbv = f(cache_b_k).reshape(4, L, 512, 512), f(cache_b_v).reshape(4, L, 512, 512)
    c, c_ctx = f(c), f(c_ctx)
    in_maps = []
    NC_ = CFG["cores"]
    for core in range(NC_):
        b = core // 2
        m = dict(shared)
        hp = core % 2
        m["xsT"] = f(x_sample[b, hp * 1024:(hp + 1) * 1024].T)
        m["rope"] = np.ascontiguousarray(rope[:, :, hp * 1024:(hp + 1) * 1024])
        m["nonce"] = nonce
        m["xpT"] = f(x_prompt[2 * core:2 * core + 2].reshape(512, D).T)
        m["cak"], m["cav"], m["cbk"], m["cbv"] = cak[b], cav[b], cbk[b], cbv[b]
        m["cond"] = f(np.stack([c_ctx.reshape(8, 128), c[b].reshape(8, 128)], -1).transpose(1, 0, 2))
        in_maps.append(m)
    if "k" not in _CACHE:
        _CACHE["k"] = build()
    res = list(run_bass_kernel_spmd(_CACHE["k"].nc, in_maps, core_ids=list(range(NC_))).results)
    while len(res) < 8:
        res.append({k_: np.zeros_like(v_) for k_, v_ in res[0].items()})
    y_prompt = np.concatenate([res[i]["ypT"].T.reshape(2, 256, D) for i in range(8)], 0)
    y_sample = np.stack([np.concatenate([res[2 * b]["ysT"].T, res[2 * b + 1]["ysT"].T], 0) for b in range(4)], 0)
    sa_k = np.concatenate([res[i]["sak"] for i in range(8)], 0).reshape(16, L, 256, 8, 128)
    sa_v = np.concatenate([res[i]["sav"] for i in range(8)], 0).reshape(16, L, 256, 8, 128)
    sb_k = np.concatenate([res[i]["sbk"] for i in range(8)], 0).reshape(16, L, 256, 4, 128)
    sb_v = np.concatenate([res[i]["sbv"] for i in range(8)], 0).reshape(16, L, 256, 4, 128)
    return (np.ascontiguousarray(y_prompt, dtype=np.float32), np.ascontiguousarray(y_sample, dtype=np.float32),
            np.ascontiguousarray(sa_k, dtype=np.float32), np.ascontiguousarray(sa_v, dtype=np.float32),
            np.ascontiguousarray(sb_k, dtype=np.float32), np.ascontiguousarray(sb_v, dtype=np.float32))
```

```python
import math
import os
import numpy as np
import concourse.bass as bass
import concourse.mybir as mybir
from concourse.bass_utils import run_bass_kernel_spmd

F32 = mybir.dt.float32
BF16 = mybir.dt.bfloat16
AF = mybir.ActivationFunctionType
ALU = mybir.AluOpType
AX = mybir.AxisListType

D = 1024
L = 4
OFF_KA, OFF_VA, OFF_QB, OFF_KB, OFF_VB, OFF_GATE, D_IN = 1024, 2048, 3072, 4096, 4608, 5120, 7168
ALPHA = (2 * L) ** 0.25
BIG = 1.0e9
NE = 16


def lam_init(l):
    return 0.8 - 0.6 * math.exp(-0.3 * l)


class Buf:
    def __init__(self, t, name):
        self.t = t
        self.name = name
        self.lw = None
        self.rd = {}
        self.dsem = None
        self.dcnt = 0

    def __getitem__(self, idx):
        return self.t[idx]


class Eng:
    def __init__(self, nc, name, e):
        self.e = e
        self.sem = nc.alloc_semaphore("es_" + name)
        self.cnt = 0
        self.seen = {}


class K:
    def __init__(self):
        self.nc = nc = bass.Bass("TRN2", target_bir_lowering=False)
        self.eng = {"pe": Eng(nc, "pe", nc.tensor), "act": Eng(nc, "act", nc.scalar),
                    "dve": Eng(nc, "dve", nc.vector), "pool": Eng(nc, "pool", nc.gpsimd),
                    "sp": Eng(nc, "sp", nc.sync)}

    def sb(self, name, shape, dt):
        return Buf(self.nc.alloc_sbuf_tensor(name, list(shape), dt), name)

    def ps(self, name, shape, dt=F32):
        return Buf(self.nc.alloc_psum_tensor(name, list(shape), dt), name)

    def dram(self, name, shape, dt, kind="Internal", addr_space="Local"):
        return Buf(self.nc.dram_tensor(name, list(shape), dt, kind=kind, addr_space=addr_space).ap(), name)

    def _waits(self, E, reads, writes, skip_self=False):
        need = {}

        def add(ev):
            if ev is None:
                return
            s, v = ev
            if s.num not in need or need[s.num][1] < v:
                need[s.num] = (s, v)

        for b in reads:
            add(b.lw)
        for b in writes:
            add(b.lw)
            for ev in b.rd.values():
                add(ev)
        for num, (s, v) in need.items():
            if skip_self and num == E.sem.num:
                continue
            if E.seen.get(num, 0) >= v:
                continue
            E.e.wait_ge(s, v)
            E.seen[num] = v

    def _record(self, ev, reads, writes):
        for b in writes:
            b.lw = ev
            b.rd = {}
        for b in reads:
            if b in writes:
                continue
            b.rd[ev[0].num] = ev

    def op(self, eng, fn, reads=(), writes=()):
        E = self.eng[eng]
        self._waits(E, reads, writes, skip_self=(eng == "pe"))
        ins = fn(E.e)
        E.cnt += 1
        ins.then_inc(E.sem, 1)
        self._record((E.sem, E.cnt), reads, writes)

    def dma(self, q, dst, dst_ap, src, src_ap, sem_owner=None):
        E = self.eng[q]
        self._waits(E, [src], [dst])
        own = sem_owner if sem_owner is not None else dst
        if own.dsem is None:
            own.dsem = self.nc.alloc_semaphore("ds_" + own.name)
        ins = E.e.dma_start(out=dst_ap, in_=src_ap)
        own.dcnt += 16
        ins.then_inc(own.dsem, 16)
        ev = (own.dsem, own.dcnt)
        self._record(ev, [src], [dst])

    def finish(self, outs):
        E = self.eng["sp"]
        for b in outs:
            if b.lw is not None:
                E.e.wait_ge(b.lw[0], b.lw[1])


CFG = {"state_mode": 0, "do_state": True, "do_attn": True, "cores": 8, "passes": (False, True), "layers": L, "stop": None, "units": 12, "experts": NE}


def build():
    k = K()
    EI = "ExternalInput"
    EO = "ExternalOutput"
    xsT = k.dram("xsT", [D, 1024], F32, EI)
    xpT = k.dram("xpT", [D, 512], F32, EI)
    cak = k.dram("cak", [L, 512, 1024], F32, EI)
    cav = k.dram("cav", [L, 512, 1024], F32, EI)
    cbk = k.dram("cbk", [L, 512, 512], F32, EI)
    cbv = k.dram("cbv", [L, 512, 512], F32, EI)
    cond = k.dram("cond", [128, 8, 2], F32, EI)
    w_mod = k.dram("w_mod", [L, D, 6144], F32, EI)
    w_in = k.dram("w_in", [L, D, D_IN], F32, EI)
    w_br_a = k.dram("w_br_a", [L, D, D], F32, EI)
    w_br_b = k.dram("w_br_b", [L, D, D], F32, EI)
    w_out = k.dram("w_out", [L, D, D], F32, EI)
    w_gate = k.dram("w_gate", [L, NE, D, 512], F32, EI)
    w_up = k.dram("w_up", [L, NE, D, 512], F32, EI)
    w_down = k.dram("w_down", [L, NE, 512, D], F32, EI)
    bmod_d = k.dram("bmod", [128, L, 48], F32, EI)
    lnp_d = k.dram("lnp", [128, L, 4, 8], F32, EI)
    hv_d = k.dram("hv", [128, L, 3], F32, EI)
    lamv_d = k.dram("lamv", [128, 2, L * 128], F32, EI)
    kng_d = k.dram("kng", [128, L, 128], F32, EI)
    wr_d = k.dram("wr", [128, 8, NE], F32, EI)
    rb_d = k.dram("rb", [128, NE], F32, EI)
    rope_d = k.dram("rope", [4, 128, 1024], F32, EI)
    nonce_d = k.dram("nonce", [1, 16], mybir.dt.int32, EI)
    perm_d = k.dram("perm", [128, 2, 128], F32, EI)
    ident_d = k.dram("ident", [128, 128], F32, EI)
    sel_d = k.dram("sel", [NE, NE * 128], F32, EI)
    ypT = k.dram("ypT", [D, 512], F32, EO)
    ysT = k.dram("ysT", [D, 1024], F32, EO)
    sak = k.dram("sak", [2, L, 256, 1024], F32, EO)
    sav = k.dram("sav", [2, L, 256, 1024], F32, EO)
    sbk = k.dram("sbk", [2, L, 256, 512], F32, EO)
    sbv = k.dram("sbv", [2, L, 256, 512], F32, EO)
    ospill = k.dram("ospill", [2 * D, 2048], BF16)
    nc = k.nc
    shk = Buf(None, "shk")
    shv = Buf(None, "shv")
    shk_t = [[nc.dram_tensor(f"shk_{l}_{u}", [2, 128, 1024], BF16, addr_space="Shared").ap() for u in range(12)] for l in range(L)]
    shv_t = [[nc.dram_tensor(f"shv_{l}_{u}", [2, 128, 1024], BF16, addr_space="Shared").ap() for u in range(12)] for l in range(L)]
    shf_t = [nc.dram_tensor(f"shf_{l}", [2, 16], mybir.dt.int32, addr_space="Shared").ap() for l in range(L)]
    pv_me = nc.gpsimd.snap(nc.gpsimd.partition_id() % 2)
    rn = nc.gpsimd.register("rn").__enter__()
    rf = nc.gpsimd.register("rf").__enter__()
    nc.gpsimd.reg_load(rn, nonce_d.t[0:1, 0:1])

    x = [k.sb(f"x{b}", [128, 8, 512], F32) for b in range(2)]
    h = [k.sb(f"h{b}", [128, 8, 512], BF16) for b in range(2)]
    abvs = [Buf(None, "abv0"), Buf(None, "abv1")]
    abs_ = [k.sb(f"ab{i}", [128, 9216], BF16) for i in range(2)]
    ab = abs_[0]
    OB = ab.t[:, 0:8192].rearrange("p (c t) -> p c t", c=16)
    NSLOT = 6
    wsl = [k.sb(f"wsl{i}", [128, 4096], BF16) for i in range(NSLOT)]
    cm = k.sb("cm", [128, 8, 512], BF16)
    NF = 7
    ftmp = [k.sb(f"ft{i}", [128, 512], F32) for i in range(NF)]
    NBT = 6
    btmp = [k.sb(f"bt{i}", [128, 512], BF16) for i in range(NBT)]
    NBP = 3
    bpair = [k.sb(f"bp{i}", [128, 1024], BF16) for i in range(NBP)]
    NRK = 4
    rkeep = [k.sb(f"rk{i}", [128, 512], F32) for i in range(NRK)]
    deferred = []
    deferred2 = []
    ropeCs = [k.sb(f"ropeC{b}", [128, 512], F32) for b in range(2)]
    ropeSs = [k.sb(f"ropeS{b}", [128, 512], F32) for b in range(2)]
    rope_state = [None, None]
    ckf = k.sb("ckf", [128, 4, 128], F32)
    h2f = k.sb("h2f", [128, 8, 128], F32)
    stg = [k.sb(f"stg{i}", [128, 4, 128], F32) for i in range(2)]
    combT = k.sb("combT", [NE, 2048], BF16)
    selb = k.sb("selb", [NE, NE * 128], BF16)
    ident = k.sb("idents", [128, 128], F32)
    onesb = k.sb("onesb", [128, 128], BF16)
    onesf = k.sb("onesf", [128, 128], F32)
    permb = k.sb("permb", [128, 2, 128], BF16)
    bmod = k.sb("bmods", [128, L, 48], F32)
    lnp = k.sb("lnps", [128, L, 4, 8], F32)
    hv = k.sb("hvs", [128, L, 3], F32)
    kng = k.sb("kngs", [128, L, 128], F32)
    wr = k.sb("wrs", [128, 8, NE], F32)
    rb = k.sb("rbs", [128, NE], F32)
    condf = k.sb("condf", [128, 8, 2], F32)
    condb = k.sb("condb", [128, 8, 2], BF16)
    modv = k.sb("modv", [128, 56], F32)
    gsub = k.sb("gsub", [128, L], F32)
    neglam = k.sb("neglam", [128, L], F32)
    rt = k.sb("rt", [128, 64], F32)
    small = k.sb("small", [128, 16], F32)

    PP = [nc.alloc_psum_tensor(f"PP{i}", [128, 1024], F32) for i in range(4)]
    P = [Buf(PP[i // 2][:, (i % 2) * 512:(i % 2 + 1) * 512], f"P{i}") for i in range(8)]
    rot = {}

    def bank(role, ids):
        i = rot.get(role, 0)
        rot[role] = i + 1
        return P[ids[i % len(ids)]]

    def tmpf():
        i = rot.get("tf", 0)
        rot["tf"] = i + 1
        return ftmp[i % NF]

    def tmpb():
        i = rot.get("tb", 0)
        rot["tb"] = i + 1
        return btmp[i % NBT]

    def tmpr():
        i = rot.get("tr", 0)
        rot["tr"] = i + 1
        return rkeep[i % NRK]

    def tmpp():
        i = rot.get("tp", 0)
        rot["tp"] = i + 1
        return bpair[i % NBP]

    def slot():
        i = rot.get("ws", 0)
        rot["ws"] = i + 1
        return wsl[i % NSLOT]

    op = k.op
    dma = k.dma

    dma("sp", ident, ident[:], ident_d, ident_d[:, :])
    dma("sp", bmod, bmod[:], bmod_d, bmod_d[:, :, :])
    dma("sp", lnp, lnp[:], lnp_d, lnp_d[:, :, :, :])
    dma("sp", hv, hv[:], hv_d, hv_d[:, :, :])
    dma("sp", kng, kng[:], kng_d, kng_d[:, :, :])
    dma("sp", wr, wr[:], wr_d, wr_d[:, :, :])
    dma("sp", rb, rb[:], rb_d, rb_d[:, :])
    dma("sp", condf, condf[:], cond, cond[:, :, :])
    dma("pool", permb, permb[:], perm_d, perm_d[:, :, :])
    dma("pool", selb, selb[:], sel_d, sel_d[:, :])
    op("dve", lambda v: v.memset(onesb[:], 1.0), [], [onesb])
    op("dve", lambda v: v.memset(onesf[:], 1.0), [], [onesf])
    op("act", lambda a: a.activation(out=condb[:], in_=condf[:], func=AF.Silu), [condf], [condb])
    la, lb = ftmp[0], ftmp[1]
    dma("sp", la, la[:], lamv_d, lamv_d[:, 0, :])
    dma("sp", lb, lb[:], lamv_d, lamv_d[:, 1, :])
    op("dve", lambda v: v.tensor_tensor(out=la[:], in0=la[:], in1=lb[:], op=ALU.mult), [la, lb], [la])
    op("dve", lambda v: v.tensor_reduce(out=small[:, 0:8], in_=la[:].rearrange("p (g e) -> p g e", e=64),
                                        axis=AX.X, op=ALU.add), [la], [small])
    op("act", lambda a: a.activation(out=small[:, 8:16], in_=small[:, 0:8], func=AF.Exp), [small], [small])
    for l in range(L):
        op("dve", lambda v, l=l: v.scalar_tensor_tensor(out=neglam[:, l:l + 1], in0=small[:, 9 + 2 * l:10 + 2 * l],
                                                        scalar=-lam_init(l), in1=small[:, 8 + 2 * l:9 + 2 * l],
                                                        op0=ALU.add, op1=ALU.subtract), [small], [neglam])
        op("dve", lambda v, l=l: v.tensor_scalar(out=gsub[:, l:l + 1], in0=hv[:, l, 0:1], scalar1=1.0 - lam_init(l),
                                                 scalar2=None, op0=ALU.mult), [hv], [gsub])

    def wpiece(sl, dst_ap, src_buf, src_ap):
        dma("pool", sl, dst_ap, src_buf, src_ap)

    def mm_group(ps_ap, pairs, reads, psb):
        def f(pe):
            n = len(pairs)
            r = None
            for i, (lt, rh) in enumerate(pairs):
                r = pe.matmul(ps_ap, lhsT=lt, rhs=rh, start=(i == 0), stop=(i == n - 1))
            return r
        op("pe", f, reads, [psb])

    def layer_norm_block(xb, l, which):
        pa_ = bank("ln", [6, 7])
        mm_group(pa_[:, :], [(onesf[:], xb[:, c, :]) for c in range(8)], [onesf, xb], pa_)
        pb_ = bank("ln", [6, 7])
        mt, vt, ta, tb_ = tmpf(), tmpf(), tmpf(), tmpf()
        for c in range(8):
            sq = ta if c % 2 == 0 else tb_
            op("act", lambda a, c=c, sq=sq: a.activation(out=sq[:], in_=xb[:, c, :], func=AF.Square), [xb], [sq])
            op("pe", lambda pe, c=c, sq=sq: pe.matmul(pb_[:, :], lhsT=onesf[:], rhs=sq[:], start=(c == 0), stop=(c == 7)),
               [onesf, sq], [pb_])
        op("act", lambda a: a.mul(out=mt[:], in_=pa_[:, :], mul=1.0 / D), [pa_], [mt])
        op("dve", lambda v: v.tensor_tensor(out=vt[:], in0=mt[:], in1=mt[:], op=ALU.mult), [mt], [vt])
        op("dve", lambda v: v.scalar_tensor_tensor(out=vt[:], in0=pb_[:, :], scalar=1.0 / D, in1=vt[:],
                                                   op0=ALU.mult, op1=ALU.subtract), [pb_, vt], [vt])
        op("act", lambda a: a.activation(out=vt[:], in_=vt[:], func=AF.Ln, bias=1e-6, scale=1.0), [vt], [vt])
        op("act", lambda a: a.activation(out=vt[:], in_=vt[:], func=AF.Exp, scale=-0.5), [vt], [vt])
        for c in range(8):
            t = ta if c % 2 == 0 else tb_
            op("dve", lambda v, c=c, t=t: v.tensor_tensor(out=t[:], in0=xb[:, c, :], in1=mt[:], op=ALU.subtract), [xb, mt], [t])
            op("dve", lambda v, t=t: v.tensor_tensor(out=t[:], in0=t[:], in1=vt[:], op=ALU.mult), [t, vt], [t])
            op("act", lambda a, c=c, t=t: a.activation(out=xb[:, c, :], in_=t[:], func=AF.Identity,
                                                       bias=lnp[:, l, 2 * which + 1, c:c + 1],
                                                       scale=lnp[:, l, 2 * which, c:c + 1]), [t, lnp], [xb])

    def modulate(NB, l, which):
        for b in range(NB):
            for c in range(8):
                op("act", lambda a, b=b, c=c: a.activation(out=h[b][:, c, :], in_=x[b][:, c, :], func=AF.Identity,
                                                           bias=modv[:, 24 * which + c:24 * which + c + 1],
                                                           scale=modv[:, 48 + 8 * which + c:48 + 8 * which + c + 1]),
                   [x[b], modv], [h[b]])

    def run_pass(sample):
        T = 1024 if sample else 512
        NB = T // 512
        ccol = 1 if sample else 0
        xin = xsT if sample else xpT
        yout = ysT if sample else ypT
        koff = 512 if sample else 0
        nkc = 20 if sample else 4
        if sample:
            seqs = [(list(range(20)), [(q * 512, 512) for q in range(2)])]
        else:
            seqs = [([0, 1], [(0, 256)]), ([2, 3], [(256, 256)])]
        for b in range(NB):
            dma("sp", x[b], x[b][:], xin, xin.t.rearrange("(c p) t -> p c t", p=128)[:, :, b * 512:(b + 1) * 512])

        for l in range(CFG["layers"]):
            pm = P[6]
            for piece in range(12):
                sl = slot()
                wv = sl.t[:, :].rearrange("p (c f) -> p c f", c=8)
                wpiece(sl, wv, w_mod, w_mod.t[l].rearrange("(c p) f -> p c f", p=128)[:, :, piece * 512:(piece + 1) * 512])
                for jj in range(4):
                    j = piece * 4 + jj
                    mm_group(pm[:, j:j + 1], [(wv[:, c, jj * 128:(jj + 1) * 128], condb[:, c, ccol:ccol + 1]) for c in range(8)],
                             [sl, condb], pm)
            op("dve", lambda v: v.tensor_tensor(out=modv[:, 0:48], in0=pm[:, 0:48], in1=bmod[:, l, :], op=ALU.add),
               [pm, bmod], [modv])
            op("dve", lambda v: v.tensor_scalar(out=modv[:, 48:56], in0=modv[:, 8:16], scalar1=1.0, scalar2=None, op0=ALU.add),
               [modv], [modv])
            if CFG["stop"] == "mod":
                continue
            modulate(NB, l, 0)

            def unit(u, mode):
                ai = rot.get("ab", 0)
                rot["ab"] = ai + 1
                ab = abs_[ai % 2]
                abv = abvs[ai % 2]
                QT = ab.t[:, 0:4096].rearrange("p (h t) -> p h t", h=2)
                KT = ab.t[:, 4096:6656]
                Vt = ab.t[:, 6656:9216].rearrange("p (c d) -> p c d", d=128)
                do_q = mode in ("all", "q")
                do_kv = mode in ("all", "kv")
                diff = u < 8
                g = u - 8
                nq = 128 if diff else 256
                qcol = u * 128 if diff else OFF_QB + g * 256
                kcol = OFF_KA + u * 128 if diff else OFF_KB + g * 128
                vcol = OFF_VA + u * 128 if diff else OFF_VB + g * 128
                sl = slot()
                wv = sl.t[:, :].rearrange("p (c f) -> p c f", c=8)
                win = w_in.t[l].rearrange("(c p) f -> p c f", p=128)
                if do_q:
                    wpiece(sl, wv[:, :, 0:nq], w_in, win[:, :, qcol:qcol + nq])
                if do_kv:
                    wpiece(sl, wv[:, :, nq:nq + 128], w_in, win[:, :, kcol:kcol + 128])
                    wpiece(sl, wv[:, :, nq + 128:nq + 256], w_in, win[:, :, vcol:vcol + 128])
                ckd = cak if diff else cbk
                cvd = cav if diff else cbv
                hcol = (u if diff else g) * 128
                permi = 0 if diff else 1
                if sample and mode == "q":
                    for hf in range(2):
                        dma("sp", ab, KT[:, 512 + hf * 1024:1536 + hf * 1024], shk, shk_t[l][u][hf])
                        dma("sp", ab, ab.t[:, 6656 + 512 + hf * 1024:6656 + 1536 + hf * 1024], shv, shv_t[l][u][hf])
                    dma("sp", ckf, ckf[:], ckd, ckd.t[l].rearrange("(c p) f -> p c f", p=128)[:, :, hcol:hcol + 128])
                    pt_ = bank("pm", [6, 7])
                    for c in range(4):
                        op("pe", lambda pe, c=c: pe.transpose(pt_[:, c * 128:(c + 1) * 128], ckf[:, c, :], ident[:]),
                           [ckf, ident], [pt_])
                    op("act", lambda a: a.copy(out=KT[:, 0:512], in_=pt_[:, :]), [pt_], [ab])
                    dma("pool", ab, Vt[:, 0:4, :], cvd, cvd.t[l].rearrange("(c p) f -> p c f", p=128)[:, :, hcol:hcol + 128],
                        sem_owner=abv)

                def rope_finish(src_b, dst_ap, b):
                    pp = bank("pm", [6, 7])
                    op("pe", lambda pe: pe.matmul(pp[:, :], lhsT=permb[:, permi, :], rhs=src_b[:], start=True, stop=True),
                       [permb, src_b], [pp])
                    t1, t2 = tmpf(), tmpf()
                    ropeC, ropeS = ropeCs[b], ropeSs[b]
                    op("dve", lambda v: v.tensor_tensor(out=t1[:], in0=pp[:, :], in1=ropeS[:], op=ALU.mult), [pp, ropeS], [t1])
                    op("dve", lambda v: v.tensor_tensor(out=t2[:], in0=src_b[:], in1=ropeC[:], op=ALU.mult), [src_b, ropeC], [t2])
                    op("dve", lambda v: v.tensor_tensor(out=dst_ap, in0=t1[:], in1=t2[:], op=ALU.add), [t1, t2], [ab])

                def qk_post(pq, n, dst_ap, b, gcol):
                    if diff:
                        if not sample:
                            op("act", lambda a: a.copy(out=dst_ap, in_=pq[:, 0:n]), [pq], [ab])
                            return
                        tb = tmpb()
                        op("act", lambda a: a.copy(out=tb[:], in_=pq[:, :]), [pq], [tb])
                        rope_finish(tb, dst_ap, b)
                        return
                    qf, sqb, rs = tmpf(), tmpb(), tmpf()
                    op("act", lambda a: a.copy(out=qf[:, 0:n], in_=pq[:, 0:n]), [pq], [qf])
                    op("act", lambda a: a.activation(out=sqb[:, 0:n], in_=pq[:, 0:n], func=AF.Square), [pq], [sqb])
                    pp = bank("pm", [6, 7])
                    op("pe", lambda pe: pe.matmul(pp[:, 0:n], lhsT=onesb[:], rhs=sqb[:, 0:n], start=True, stop=True),
                       [onesb, sqb], [pp])
                    op("act", lambda a: a.activation(out=rs[:, 0:n], in_=pp[:, 0:n], func=AF.Ln, bias=1e-6, scale=1.0 / 128),
                       [pp], [rs])
                    op("act", lambda a: a.activation(out=rs[:, 0:n], in_=rs[:, 0:n], func=AF.Exp, scale=-0.5), [rs], [rs])
                    if not sample:
                        op("dve", lambda v: v.scalar_tensor_tensor(out=dst_ap, in0=qf[:, 0:n], scalar=hv[:, l, gcol:gcol + 1],
                                                                   in1=rs[:, 0:n], op0=ALU.mult, op1=ALU.mult), [qf, hv, rs], [ab])
                        return
                    tb = tmpb()
                    op("dve", lambda v: v.scalar_tensor_tensor(out=tb[:], in0=qf[:], scalar=hv[:, l, gcol:gcol + 1],
                                                               in1=rs[:], op0=ALU.mult, op1=ALU.mult), [qf, hv, rs], [tb])
                    rope_finish(tb, dst_ap, b)

                for b in range(NB):
                    if sample:
                        ri = 0 if diff else 2
                        if rope_state[b] != ri:
                            rope_state[b] = ri
                            dma("sp", ropeCs[b], ropeCs[b][:], rope_d, rope_d[ri, :, b * 512:(b + 1) * 512])
                            dma("sp", ropeSs[b], ropeSs[b][:], rope_d, rope_d[ri + 1, :, b * 512:(b + 1) * 512])
                    for hh in (range(1 if diff else 2) if do_q else []):
                        pq = bank("pm", [6, 7])
                        mm_group(pq[:, :], [(wv[:, c, hh * 128:(hh + 1) * 128], h[b][:, c, :]) for c in range(8)], [sl, h[b]], pq)
                        qk_post(pq, 512, QT[:, hh, b * 512:(b + 1) * 512], b, 1)
                    if not do_kv:
                        continue
                    if mode == "kv":
                        continue
                    pq = bank("pm", [6, 7])
                    mm_group(pq[:, :], [(wv[:, c, nq:nq + 128], h[b][:, c, :]) for c in range(8)], [sl, h[b]], pq)
                    qk_post(pq, 512, KT[:, b * 512:(b + 1) * 512], b, 2)
                    pv = bank("pm", [6, 7])
                    for t in range(4):
                        mm_group(pv[:, t * 128:(t + 1) * 128],
                                 [(h[b][:, c, t * 128:(t + 1) * 128], wv[:, c, nq + 128:nq + 256]) for c in range(8)], [sl, h[b]], pv)
                    kc0 = b * 4
                    op("act", lambda a, pv=pv, kc0=kc0: a.copy(out=Vt[:, kc0:kc0 + 4, :],
                                                               in_=pv[:, :].rearrange("p (c d) -> p c d", d=128)), [pv], [ab])
                    if not sample and CFG["do_state"]:
                        so_v = sav if diff else sbv
                        so_k = sak if diff else sbk
                        s1 = stg[0]
                        op("act", lambda a, pv=pv: a.copy(out=s1[:], in_=pv[:, :].rearrange("p (c d) -> p c d", d=128)),
                           [pv], [s1])
                        for t4 in (range(4) if CFG["state_mode"] != 1 else []):
                            dma("sp", so_v, so_v.t[t4 // 2, l, (t4 % 2) * 128:(t4 % 2 + 1) * 128, hcol:hcol + 128],
                                s1, s1[:, t4, :])
                        if CFG["state_mode"] == 2:
                            continue
                        pk = bank("pm", [6, 7])
                        for t in range(4):
                            mm_group(pk[:, t * 128:(t + 1) * 128],
                                     [(h[b][:, c, t * 128:(t + 1) * 128], wv[:, c, nq:nq + 128]) for c in range(8)], [sl, h[b]], pk)
                        s2 = stg[1]
                        if diff:
                            op("act", lambda a, pk=pk: a.copy(out=s2[:], in_=pk[:, :].rearrange("p (c d) -> p c d", d=128)),
                               [pk], [s2])
                        else:
                            for t in range(4):
                                junk = tmpf()
                                op("act", lambda a, t=t, pk=pk, junk=junk: a.activation(
                                    out=junk[:, 0:128], in_=pk[:, t * 128:(t + 1) * 128], func=AF.Square), [pk], [junk])
                                op("dve", lambda v, t=t, junk=junk: v.tensor_reduce(out=small[:, t:t + 1], in_=junk[:, 0:128],
                                                                                   axis=AX.X, op=ALU.add), [junk], [small])
                            op("act", lambda a: a.activation(out=small[:, 4:8], in_=small[:, 0:4], func=AF.Sqrt, bias=1e-6,
                                                             scale=1.0 / 128), [small], [small])
                            op("dve", lambda v: v.reciprocal(out=small[:, 4:8], in_=small[:, 4:8]), [small], [small])
                            for t in range(4):
                                op("dve", lambda v, t=t, pk=pk: v.scalar_tensor_tensor(
                                    out=s2[:, t, :], in0=pk[:, t * 128:(t + 1) * 128], scalar=small[:, 4 + t:5 + t],
                                    in1=kng[:, l, :], op0=ALU.mult, op1=ALU.mult), [pk, small, kng], [s2])
                        for t4 in (range(4) if CFG["state_mode"] not in (1, 2) else []):
                            dma("sp", so_k, so_k.t[t4 // 2, l, (t4 % 2) * 128:(t4 % 2 + 1) * 128, hcol:hcol + 128],
                                s2, s2[:, t4, :])

                if mode == "kv":
                    kbase = 4 * (ai % 2)
                    pqs, pvs = [], []
                    for b in range(NB):
                        pq = P[kbase + b]
                        mm_group(pq[:, :], [(wv[:, c, nq:nq + 128], h[b][:, c, :]) for c in range(8)], [sl, h[b]], pq)
                        pqs.append(pq)
                    for b in range(NB):
                        pv = P[kbase + 2 + b]
                        for t in range(4):
                            mm_group(pv[:, t * 128:(t + 1) * 128],
                                     [(h[b][:, c, t * 128:(t + 1) * 128], wv[:, c, nq + 128:nq + 256]) for c in range(8)], [sl, h[b]], pv)
                        pvs.append(pv)
                    for b in range(NB):
                        op("act", lambda a, pv=pvs[b], kc0=b * 4: a.copy(out=Vt[:, kc0:kc0 + 4, :],
                                                                        in_=pv[:, :].rearrange("p (c d) -> p c d", d=128)), [pvs[b]], [ab])
                    for b in range(NB):
                        qk_post(pqs[b], 512, KT[:, b * 512:(b + 1) * 512], b, 2)
                    dma("pool", shk, shk_t[l][u][bass.ds(pv_me, 1), :, :], ab, KT[:, 0:1024])
                    dma("pool", shv, shv_t[l][u][bass.ds(pv_me, 1), :, :], ab, ab.t[:, 6656:6656 + 1024])
                    return None
                scale = 0.125 if diff else 128 ** -0.5
                if diff:
                    subs = [(0, slice(0, 64)), (0, slice(64, 128))]
                else:
                    subs = [(0, slice(0, 128)), (1, slice(0, 128))]

                def attn():
                    for (kcs, qtiles) in (seqs if CFG["do_attn"] else []):
                        for (q0, n) in qtiles:
                            R = []
                            for (hh, rows) in subs:
                                po = P[4]
                                psum_ = P[5]
                                npair = len(kcs) // 2
                                pts = [None] * npair

                                def s_and_exp(i):
                                    j = rot.get("sp", 0)
                                    rot["sp"] = j + 1
                                    j = j % 2
                                    lo, hi = P[2 * j], P[2 * j + 1]

                                    def f(pe):
                                        r_ = None
                                        for t_, pb_ in enumerate((lo, hi)):
                                            kc = kcs[2 * i + t_]
                                            r_ = pe.matmul(pb_[:, 0:n], lhsT=KT[rows, kc * 128:(kc + 1) * 128],
                                                           rhs=QT[rows, hh, q0:q0 + n], start=True, stop=True)
                                        return r_
                                    op("pe", f, [ab], [lo, hi])
                                    ptile = tmpp()
                                    op("act", lambda a: a.activation(
                                        out=ptile[:, :].rearrange("p (k q) -> p k q", k=2)[:, :, 0:n],
                                        in_=PP[j][:, :].rearrange("p (k q) -> p k q", k=2)[:, :, 0:n], func=AF.Exp, scale=scale),
                                       [lo, hi], [ptile])
                                    pts[i] = ptile

                                def pv_sum(i):
                                    ptile = pts[i]

                                    def f(pe):
                                        r_ = None
                                        for t_ in range(2):
                                            kc = kcs[2 * i + t_]
                                            first = (i == 0 and t_ == 0)
                                            last = (i == npair - 1 and t_ == 1)
                                            pe.matmul(po[:, 0:n], lhsT=Vt[:, kc, :], rhs=ptile[:, t_ * 512:t_ * 512 + n],
                                                      start=first, stop=last)
                                            r_ = pe.matmul(psum_[:, 0:n], lhsT=onesb[:], rhs=ptile[:, t_ * 512:t_ * 512 + n],
                                                           start=first, stop=last)
                                        return r_
                                    op("pe", f, [ab, ptile, onesb], [po, psum_])

                                for i in range(npair):
                                    s_and_exp(i)
                                    if i >= 1:
                                        pv_sum(i - 1)
                                    if i == 1 and deferred:
                                        deferred.pop(0)()
                                    if i == 5 and deferred2:
                                        deferred2.pop(0)()
                                pv_sum(npair - 1)
                                if npair < 2 and deferred:
                                    deferred.pop(0)()
                                if npair < 6:
                                    while deferred2:
                                        deferred2.pop(0)()
                                rec, r = tmpf(), tmpr()
                                op("dve", lambda v: v.tensor_scalar(out=r[:, 0:n], in0=po[:, 0:n], scalar1=1.0, scalar2=None,
                                                                    op0=ALU.mult), [po], [r])
                                op("act", lambda a: a.activation(out=rec[:, 0:n], in_=psum_[:, 0:n], func=AF.Ln), [psum_], [rec])
                                op("act", lambda a: a.activation(out=rec[:, 0:n], in_=rec[:, 0:n], func=AF.Exp, scale=-1.0), [rec], [rec])
                                op("dve", lambda v: v.tensor_tensor(out=r[:, 0:n], in0=r[:, 0:n], in1=rec[:, 0:n], op=ALU.mult),
                                   [r, rec], [r])
                                R.append(r)

                            def fin(R=R, q0=q0, n=n):
                                if diff:
                                    o, sqb, rs = tmpf(), tmpb(), tmpf()
                                    op("dve", lambda v: v.scalar_tensor_tensor(out=o[:, 0:n], in0=R[1][:, 0:n], scalar=neglam[:, l:l + 1],
                                                                               in1=R[0][:, 0:n], op0=ALU.mult, op1=ALU.add),
                                       [R[0], R[1], neglam], [o])
                                    op("act", lambda a: a.activation(out=sqb[:, 0:n], in_=o[:, 0:n], func=AF.Square), [o], [sqb])

                                    def fin_b():
                                        pp = P[7]
                                        op("pe", lambda pe: pe.matmul(pp[:, 0:n], lhsT=onesb[:], rhs=sqb[:, 0:n], start=True, stop=True),
                                           [onesb, sqb], [pp])
                                        op("act", lambda a: a.activation(out=rs[:, 0:n], in_=pp[:, 0:n], func=AF.Ln, bias=1e-6,
                                                                         scale=1.0 / 128), [pp], [rs])
                                        op("act", lambda a: a.activation(out=rs[:, 0:n], in_=rs[:, 0:n], func=AF.Exp, scale=-0.5), [rs], [rs])
                                        ot = tmpb()
                                        op("dve", lambda v: v.scalar_tensor_tensor(out=ot[:, 0:n], in0=o[:, 0:n], scalar=gsub[:, l:l + 1],
                                                                                   in1=rs[:, 0:n], op0=ALU.mult, op1=ALU.mult),
                                           [o, gsub, rs], [ot])
                                        dma("sp", ospill, ospill[u * 128:(u + 1) * 128, q0:q0 + n], ot, ot[:, 0:n])
                                    deferred2.append(fin_b)
                                else:
                                    for hh2 in range(2):
                                        ot = tmpb()
                                        op("act", lambda a, hh2=hh2, ot=ot: a.copy(out=ot[:, 0:n], in_=R[hh2][:, 0:n]), [R[hh2]], [ot])
                                        ch = 8 + g * 2 + hh2
                                        dma("sp", ospill, ospill[ch * 128:(ch + 1) * 128, q0:q0 + n], ot, ot[:, 0:n])
                            deferred.append(fin)
                return attn

            def run_units(mode):
                pend = None
                for u in range(CFG["units"]):
                    a_ = unit(u, mode)
                    if pend is not None:
                        pend()
                    pend = a_
                if pend is not None:
                    pend()
                while deferred:
                    deferred.pop(0)()
                while deferred2:
                    deferred2.pop(0)()

            if sample:
                run_units("kv")

                def hs(g_):
                    g_.reg_save(shf_t[l][bass.ds(pv_me, 1), 0:1], rn)
                    for slot_ in range(2):
                        g_.reg_mov(rf, 1)
                        with g_.While(rf):
                            g_.reg_load(rf, shf_t[l][slot_:slot_ + 1, 0:1])
                            g_.reg_sub(rf, rf, rn)
                    return g_.nop()
                op("pool", hs, [shk, shv], [shk, shv])
                run_units("q")
            else:
                run_units("all")
            if CFG["stop"] == "attn":
                continue
            for b in range(NB):
                dma("sp", ab, OB, ospill, ospill.t.rearrange("(c p) t -> p c t", p=128)[:, :, b * 512:(b + 1) * 512])
                win = w_in.t[l].rearrange("(c p) f -> p c f", p=128)
                for j in range(8):
                    Q0, Q1, Q2, Q3 = (P[4 * (j % 2) + i_] for i_ in range(4))
                    sl = slot()
                    wv = sl.t[:, :].rearrange("p (c f) -> p c f", c=8)
                    wpiece(sl, wv[:, :, 0:128], w_br_a, w_br_a.t[l].rearrange("(c p) f -> p c f", p=128)[:, :, j * 128:(j + 1) * 128])
                    wpiece(sl, wv[:, :, 128:256], w_br_b, w_br_b.t[l].rearrange("(c p) f -> p c f", p=128)[:, :, j * 128:(j + 1) * 128])
                    wpiece(sl, wv[:, :, 256:384], w_in, win[:, :, OFF_GATE + j * 128:OFF_GATE + (j + 1) * 128])
                    wpiece(sl, wv[:, :, 384:512], w_in, win[:, :, OFF_GATE + D + j * 128:OFF_GATE + D + (j + 1) * 128])
                    mm_group(Q0[:, :], [(wv[:, c, 0:128], OB[:, c, :]) for c in range(8)], [sl, ab], Q0)
                    mm_group(Q1[:, :], [(wv[:, c, 128:256], OB[:, 8 + c, :]) for c in range(8)], [sl, ab], Q1)
                    mm_group(Q2[:, :], [(wv[:, c, 256:384], h[b][:, c, :]) for c in range(8)], [sl, h[b]], Q2)
                    mm_group(Q3[:, :], [(wv[:, c, 384:512], h[b][:, c, :]) for c in range(8)], [sl, h[b]], Q3)
                    sa, sb_ = tmpf(), tmpf()
                    op("act", lambda a, sa=sa: a.activation(out=sa[:], in_=Q2[:, :], func=AF.Sigmoid), [Q2], [sa])
                    op("act", lambda a, sb_=sb_: a.activation(out=sb_[:], in_=Q3[:, :], func=AF.Sigmoid), [Q3], [sb_])
                    op("dve", lambda v, sa=sa: v.tensor_tensor(out=sa[:], in0=Q0[:, :], in1=sa[:], op=ALU.mult), [Q0, sa], [sa])
                    op("dve", lambda v, sb_=sb_: v.tensor_tensor(out=sb_[:], in0=Q1[:, :], in1=sb_[:], op=ALU.mult), [Q1, sb_], [sb_])
                    op("dve", lambda v, j=j, sa=sa, sb_=sb_: v.tensor_tensor(out=cm[:, j, :], in0=sa[:], in1=sb_[:], op=ALU.add),
                       [sa, sb_], [cm])
                for c in range(8):
                    op("act", lambda a, c=c: a.mul(out=x[b][:, c, :], in_=x[b][:, c, :], mul=ALPHA), [x[b]], [x[b]])
                for jh in range(2):
                    sl = slot()
                    wv = sl.t[:, :].rearrange("p (c f) -> p c f", c=8)
                    wpiece(sl, wv, w_out, w_out.t[l].rearrange("(c p) f -> p c f", p=128)[:, :, jh * 512:(jh + 1) * 512])
                    for jj in range(4):
                        j = jh * 4 + jj
                        pmx = bank("proj", [4, 5])
                        mm_group(pmx[:, :], [(wv[:, c, jj * 128:(jj + 1) * 128], cm[:, c, :]) for c in range(8)], [sl, cm], pmx)
                        op("dve", lambda v, j=j, pmx=pmx: v.scalar_tensor_tensor(
                            out=x[b][:, j, :], in0=pmx[:, :], scalar=modv[:, 16 + j:17 + j], in1=x[b][:, j, :],
                            op0=ALU.mult, op1=ALU.add), [pmx, modv, x[b]], [x[b]])
                layer_norm_block(x[b], l, 0)

            if CFG["stop"] == "phaseC":
                continue
            op("dve", lambda v: v.tensor_scalar(out=modv[:, 48:56], in0=modv[:, 32:40], scalar1=1.0, scalar2=None, op0=ALU.add),
               [modv], [modv])
            for b in range(NB):
                for c in range(8):
                    op("act", lambda a, b=b, c=c: a.activation(out=h[b][:, c, :], in_=x[b][:, c, :], func=AF.Identity,
                                                               bias=modv[:, 24 + c:25 + c], scale=modv[:, 48 + c:49 + c]),
                       [x[b], modv], [h[b]])
                for t in range(4):
                    for c in range(8):
                        op("act", lambda a, b=b, c=c, t=t: a.activation(out=h2f[:, c, :], in_=x[b][:, c, t * 128:(t + 1) * 128],
                                                                       func=AF.Identity, bias=modv[:, 24 + c:25 + c],
                                                                       scale=modv[:, 48 + c:49 + c]), [x[b], modv], [h2f])
                    pr = P[7]
                    mm_group(pr[:, 0:NE], [(h2f[:, c, :], wr[:, c, :]) for c in range(8)], [h2f, wr], pr)
                    S_, BZ, TT, E1, WW = rt[:, 0:16], rt[:, 16:32], rt[:, 32:48], rt[:, 48:64], rt[:, 32:48]
                    sm = small
                    op("act", lambda a: a.activation(out=S_, in_=pr[:, 0:NE], func=AF.Sigmoid), [pr], [rt])
                    op("dve", lambda v: v.tensor_tensor(out=BZ, in0=S_, in1=rb[:], op=ALU.add), [rt, rb], [rt])
                    op("dve", lambda v: v.tensor_reduce(out=sm[:, 0:4], in_=BZ.rearrange("p (g e) -> p g e", e=4), axis=AX.X, op=ALU.max),
                       [rt], [sm])
                    for gg in range(4):
                        op("dve", lambda v, gg=gg: v.tensor_scalar(out=TT[:, 4 * gg:4 * gg + 4], in0=BZ[:, 4 * gg:4 * gg + 4],
                                                                   scalar1=sm[:, gg:gg + 1], scalar2=-BIG, op0=ALU.is_equal,
                                                                   op1=ALU.mult), [rt, sm], [rt])
                    op("dve", lambda v: v.tensor_tensor(out=TT, in0=TT, in1=BZ, op=ALU.add), [rt], [rt])
                    op("dve", lambda v: v.tensor_reduce(out=sm[:, 4:8], in_=TT.rearrange("p (g e) -> p g e", e=4), axis=AX.X, op=ALU.max),
                       [rt], [sm])
                    op("dve", lambda v: v.tensor_tensor(out=sm[:, 0:4], in0=sm[:, 0:4], in1=sm[:, 4:8], op=ALU.add), [sm], [sm])
                    op("dve", lambda v: v.tensor_reduce(out=sm[:, 8:9], in_=sm[:, 0:4], axis=AX.X, op=ALU.max), [sm], [sm])
                    op("dve", lambda v: v.tensor_scalar(out=sm[:, 4:8], in0=sm[:, 0:4], scalar1=sm[:, 8:9], scalar2=BIG,
                                                        op0=ALU.is_ge, op1=ALU.mult), [sm], [sm])
                    op("dve", lambda v: v.tensor_scalar(out=sm[:, 4:8], in0=sm[:, 4:8], scalar1=-BIG, scalar2=None,
                                                        op0=ALU.add), [sm], [sm])
                    for gg in range(4):
                        op("dve", lambda v, gg=gg: v.tensor_scalar(out=TT[:, 4 * gg:4 * gg + 4], in0=BZ[:, 4 * gg:4 * gg + 4],
                                                                   scalar1=sm[:, 4 + gg:5 + gg], scalar2=None, op0=ALU.add),
                           [rt, sm], [rt])
                    op("dve", lambda v: v.tensor_reduce(out=sm[:, 9:10], in_=TT, axis=AX.X, op=ALU.max), [rt], [sm])
                    op("dve", lambda v: v.tensor_scalar(out=E1, in0=TT, scalar1=sm[:, 9:10], scalar2=None, op0=ALU.is_equal),
                       [rt, sm], [rt])
                    op("dve", lambda v: v.scalar_tensor_tensor(out=TT, in0=E1, scalar=-BIG, in1=TT, op0=ALU.mult, op1=ALU.add),
                       [rt], [rt])
                    op("dve", lambda v: v.tensor_reduce(out=sm[:, 10:11], in_=TT, axis=AX.X, op=ALU.max), [rt], [sm])
                    op("dve", lambda v: v.scalar_tensor_tensor(out=E1, in0=TT, scalar=sm[:, 10:11], in1=E1, op0=ALU.is_equal,
                                                               op1=ALU.add), [rt, sm], [rt])
                    op("dve", lambda v: v.tensor_tensor(out=WW, in0=S_, in1=E1, op=ALU.mult), [rt], [rt])
                    op("dve", lambda v: v.tensor_reduce(out=sm[:, 11:12], in_=WW, axis=AX.X, op=ALU.add), [rt], [sm])
                    op("dve", lambda v: v.reciprocal(out=sm[:, 11:12], in_=sm[:, 11:12]), [sm], [sm])
                    op("dve", lambda v: v.tensor_scalar(out=WW, in0=WW, scalar1=sm[:, 11:12], scalar2=None, op0=ALU.mult),
                       [rt, sm], [rt])
                    ptr = P[6]
                    op("pe", lambda pe: pe.transpose(ptr[0:NE, 0:128], WW, ident[:]), [rt, ident], [ptr])
                    tcol = b * 512 + t * 128
                    op("act", lambda a, tcol=tcol: a.copy(out=combT[:, tcol:tcol + 128], in_=ptr[0:NE, 0:128]), [ptr], [combT])
                for c in range(8):
                    op("act", lambda a, b=b, c=c: a.mul(out=x[b][:, c, :], in_=x[b][:, c, :], mul=ALPHA), [x[b]], [x[b]])
            for e in range(CFG["experts"]):
                sg_, su_, sd_ = slot(), slot(), slot()
                wg = sg_.t[:, :].rearrange("p (c f) -> p c f", c=8)
                wu = su_.t[:, :].rearrange("p (c f) -> p c f", c=8)
                wd = sd_.t[:, :].rearrange("p (c f) -> p c f", c=4)
                wpiece(sg_, wg, w_gate, w_gate.t[l, e].rearrange("(c p) f -> p c f", p=128))
                wpiece(su_, wu, w_up, w_up.t[l, e].rearrange("(c p) f -> p c f", p=128))
                wpiece(sd_, wd, w_down, w_down.t[l, e].rearrange("(c p) f -> p c f", p=128))
                for b in range(NB):
                    pbc = P[6]
                    op("pe", lambda pe, b=b: pe.matmul(pbc[:, :], lhsT=selb[:, e * 128:(e + 1) * 128],
                                                       rhs=combT[:, b * 512:(b + 1) * 512], start=True, stop=True),
                       [selb, combT], [pbc])
                    bcs = tmpf()
                    op("act", lambda a, bcs=bcs: a.copy(out=bcs[:], in_=pbc[:, :]), [pbc], [bcs])
                    for f in range(4):
                        pg = bank("g", [0, 1])
                        pu = bank("u", [2, 3])
                        mm_group(pg[:, :], [(wg[:, c, f * 128:(f + 1) * 128], h[b][:, c, :]) for c in range(8)], [sg_, h[b]], pg)
                        mm_group(pu[:, :], [(wu[:, c, f * 128:(f + 1) * 128], h[b][:, c, :]) for c in range(8)], [su_, h[b]], pu)
                        sgt = tmpf()
                        op("act", lambda a, pg=pg, sgt=sgt: a.activation(out=sgt[:], in_=pg[:, :], func=AF.Silu), [pg], [sgt])
                        op("dve", lambda v, pu=pu, sgt=sgt: v.tensor_tensor(out=sgt[:], in0=pu[:, :], in1=sgt[:], op=ALU.mult),
                           [pu, sgt], [sgt])
                        op("dve", lambda v, f=f, sgt=sgt, bcs=bcs: v.tensor_tensor(out=cm[:, f, :], in0=sgt[:], in1=bcs[:], op=ALU.mult),
                           [sgt, bcs], [cm])
                    for j in range(8):
                        py = bank("proj", [4, 5])
                        mm_group(py[:, :], [(wd[:, f, j * 128:(j + 1) * 128], cm[:, f, :]) for f in range(4)], [sd_, cm], py)
                        op("dve", lambda v, b=b, j=j, py=py: v.scalar_tensor_tensor(
                            out=x[b][:, j, :], in0=py[:, :], scalar=modv[:, 40 + j:41 + j], in1=x[b][:, j, :],
                            op0=ALU.mult, op1=ALU.add), [py, modv, x[b]], [x[b]])
            for b in range(NB):
                layer_norm_block(x[b], l, 1)

        for b in range(NB):
            dma("sp", yout, yout.t.rearrange("(c p) t -> p c t", p=128)[:, :, b * 512:(b + 1) * 512], x[b], x[b][:])

    for p_ in CFG["passes"]:
        run_pass(p_)
    k.finish([ypT, ysT, sak, sav, sbk, sbv])
    return k


def _rope_tables():
    tok = np.arange(2048)
    row = (tok // 64).astype(np.float64)
    col = (tok % 64).astype(np.float64)
    out = np.zeros((4, 128, 2048), np.float32)
    for p in range(128):
        pp = p % 64
        axis, half, f = pp // 32, (pp % 32) // 16, pp % 16
        inv = 10000.0 ** (-f / 16.0)
        ang = (row if axis == 0 else col) * np.float32(inv).astype(np.float64)
        ang = (np.float32(row if axis == 0 else col) * np.float32(inv)).astype(np.float32)
        out[0, p] = np.cos(ang)
        out[1, p] = np.sin(ang) * (-1.0 if half == 0 else 1.0)
    for p in range(128):
        axis, half, f = p // 64, (p % 64) // 32, p % 32
        inv = np.float32(10000.0) ** np.float32(-f / 32.0)
        ang = (np.float32(row if axis == 0 else col) * np.float32(inv)).astype(np.float32)
        out[2, p] = np.cos(ang)
        out[3, p] = np.sin(ang) * (-1.0 if half == 0 else 1.0)
    perm = np.zeros((128, 2, 128), np.float32)
    for m in range(128):
        pp = m % 64
        half = (pp % 32) // 16
        src = m + 16 if half == 0 else m - 16
        perm[src, 0, m] = 1.0
        half = (m % 64) // 32
        src = m + 32 if half == 0 else m - 32
        perm[src, 1, m] = 1.0
    return out, perm


_CACHE = {}


def kernel(x_prompt, x_sample, cache_a_k, cache_a_v, cache_b_k, cache_b_v, c, c_ctx,
           w_mod, b_mod, w_in, lam_q1, lam_k1, lam_q2, lam_k2, subln_g, qn_g, kn_g,
           w_br_a, w_br_b, w_out, ln1_g, ln1_b, ln2_g, ln2_b, w_router, router_bias,
           w_gate, w_up, w_down):
    f = lambda a: np.ascontiguousarray(np.asarray(a, dtype=np.float32))
    x_prompt, x_sample = f(x_prompt), f(x_sample)
    rope, perm = _rope_tables()
    nonce = np.full((1, 16), (int.from_bytes(os.urandom(3), "little") + 1), np.int32)
    sel = np.zeros((NE, NE * 128), np.float32)
    for e in range(NE):
        sel[e, e * 128:(e + 1) * 128] = 1.0
    shared = {
        "w_mod": f(w_mod), "w_in": f(w_in), "w_br_a": f(w_br_a), "w_br_b": f(w_br_b), "w_out": f(w_out),
        "w_gate": f(w_gate), "w_up": f(w_up), "w_down": f(w_down),
        "bmod": f(np.asarray(b_mod).reshape(L, 48, 128).transpose(2, 0, 1)),
        "lnp": f(np.stack([np.asarray(a).reshape(L, 8, 128) for a in (ln1_g, ln1_b, ln2_g, ln2_b)], 1).transpose(3, 0, 1, 2)),
        "hv": f(np.stack([np.asarray(subln_g), np.asarray(qn_g), np.asarray(kn_g)], -1).transpose(1, 0, 2)),
        "lamv": f(np.broadcast_to(np.stack([
            np.concatenate([np.asarray(lam_q1), np.asarray(lam_q2)], -1).reshape(-1),
            np.concatenate([np.asarray(lam_k1), np.asarray(lam_k2)], -1).reshape(-1)], 0)[None], (128, 2, L * 128))),
        "kng": f(np.broadcast_to(np.asarray(kn_g)[None], (128, L, 128))),
        "wr": f(np.asarray(w_router).reshape(8, 128, NE).transpose(1, 0, 2)),
        "rb": f(np.broadcast_to(np.asarray(router_bias)[None], (128, NE))),
        "perm": perm, "ident": np.eye(128, dtype=np.float32), "sel": sel,
    }
    cak, cav = f(cache_a_k).reshape(4, L, 512, 1024), f(cache_a_v).reshape(4, L, 512, 1024)
    cbk, cbv = f(cache_b_k).reshape(4, L, 512, 512), f(cache_b_v).reshape(4, L, 512, 512)
    c, c_ctx = f(c), f(c_ctx)
    in_maps = []
    NC_ = CFG["cores"]
    for core in range(NC_):
        b = core // 2
        m = dict(shared)
        hp = core % 2
        m["xsT"] = f(x_sample[b, hp * 1024:(hp + 1) * 1024].T)
        m["rope"] = np.ascontiguousarray(rope[:, :, hp * 1024:(hp + 1) * 1024])
        m["nonce"] = nonce
        m["xpT"] = f(x_prompt[2 * core:2 * core + 2].reshape(512, D).T)
        m["cak"], m["cav"], m["cbk"], m["cbv"] = cak[b], cav[b], cbk[b], cbv[b]
        m["cond"] = f(np.stack([c_ctx.reshape(8, 128), c[b].reshape(8, 128)], -1).transpose(1, 0, 2))
        in_maps.append(m)
    if "k" not in _CACHE:
        _CACHE["k"] = build()
    res = list(run_bass_kernel_spmd(_CACHE["k"].nc, in_maps, core_ids=list(range(NC_))).results)
    while len(res) < 8:
        res.append({k_: np.zeros_like(v_) for k_, v_ in res[0].items()})
    y_prompt = np.concatenate([res[i]["ypT"].T.reshape(2, 256, D) for i in range(8)], 0)
    y_sample = np.stack([np.concatenate([res[2 * b]["ysT"].T, res[2 * b + 1]["ysT"].T], 0) for b in range(4)], 0)
    sa_k = np.concatenate([res[i]["sak"] for i in range(8)], 0).reshape(16, L, 256, 8, 128)
    sa_v = np.concatenate([res[i]["sav"] for i in range(8)], 0).reshape(16, L, 256, 8, 128)
    sb_k = np.concatenate([res[i]["sbk"] for i in range(8)], 0).reshape(16, L, 256, 4, 128)
    sb_v = np.concatenate([res[i]["sbv"] for i in range(8)], 0).reshape(16, L, 256, 4, 128)
    return (np.ascontiguousarray(y_prompt, dtype=np.float32), np.ascontiguousarray(y_sample, dtype=np.float32),
            np.ascontiguousarray(sa_k, dtype=np.float32), np.ascontiguousarray(sa_v, dtype=np.float32),
            np.ascontiguousarray(sb_k, dtype=np.float32), np.ascontiguousarray(sb_v, dtype=np.float32))
```

```python
import math
import os
import numpy as np
import concourse.bass as bass
import concourse.mybir as mybir
from concourse.bass_utils import run_bass_kernel_spmd

F32 = mybir.dt.float32
BF16 = mybir.dt.bfloat16
AF = mybir.ActivationFunctionType
ALU = mybir.AluOpType
AX = mybir.AxisListType

D = 1024
L = 4
OFF_KA, OFF_VA, OFF_QB, OFF_KB, OFF_VB, OFF_GATE, D_IN = 1024, 2048, 3072, 4096, 4608, 5120, 7168
ALPHA = (2 * L) ** 0.25
BIG = 1.0e9
NE = 16


def lam_init(l):
    return 0.8 - 0.6 * math.exp(-0.3 * l)


class Buf:
    def __init__(self, t, name):
        self.t = t
        self.name = name
        self.lw = None
        self.rd = {}
        self.dsem = None
        self.dcnt = 0

    def __getitem__(self, idx):
        return self.t[idx]


class Eng:
    def __init__(self, nc, name, e):
        self.e = e
        self.sem = nc.alloc_semaphore("es_" + name)
        self.cnt = 0
        self.seen = {}


class K:
    def __init__(self):
        self.nc = nc = bass.Bass("TRN2", target_bir_lowering=False)
        self.eng = {"pe": Eng(nc, "pe", nc.tensor), "act": Eng(nc, "act", nc.scalar),
                    "dve": Eng(nc, "dve", nc.vector), "pool": Eng(nc, "pool", nc.gpsimd),
                    "sp": Eng(nc, "sp", nc.sync)}

    def sb(self, name, shape, dt):
        return Buf(self.nc.alloc_sbuf_tensor(name, list(shape), dt), name)

    def ps(self, name, shape, dt=F32):
        return Buf(self.nc.alloc_psum_tensor(name, list(shape), dt), name)

    def dram(self, name, shape, dt, kind="Internal", addr_space="Local"):
        return Buf(self.nc.dram_tensor(name, list(shape), dt, kind=kind, addr_space=addr_space).ap(), name)

    def _waits(self, E, reads, writes, skip_self=False):
        need = {}

        def add(ev):
            if ev is None:
                return
            s, v = ev
            if s.num not in need or need[s.num][1] < v:
                need[s.num] = (s, v)

        for b in reads:
            add(b.lw)
        for b in writes:
            add(b.lw)
            for ev in b.rd.values():
                add(ev)
        for num, (s, v) in need.items():
            if skip_self and num == E.sem.num:
                continue
            if E.seen.get(num, 0) >= v:
                continue
            E.e.wait_ge(s, v)
            E.seen[num] = v

    def _record(self, ev, reads, writes):
        for b in writes:
            b.lw = ev
            b.rd = {}
        for b in reads:
            if b in writes:
                continue
            b.rd[ev[0].num] = ev

    def op(self, eng, fn, reads=(), writes=()):
        E = self.eng[eng]
        self._waits(E, reads, writes, skip_self=(eng == "pe"))
        ins = fn(E.e)
        E.cnt += 1
        ins.then_inc(E.sem, 1)
        self._record((E.sem, E.cnt), reads, writes)

    def dma(self, q, dst, dst_ap, src, src_ap, sem_owner=None):
        E = self.eng[q]
        self._waits(E, [src], [dst])
        own = sem_owner if sem_owner is not None else dst
        if own.dsem is None:
            own.dsem = self.nc.alloc_semaphore("ds_" + own.name)
        ins = E.e.dma_start(out=dst_ap, in_=src_ap)
        own.dcnt += 16
        ins.then_inc(own.dsem, 16)
        ev = (own.dsem, own.dcnt)
        self._record(ev, [src], [dst])

    def finish(self, outs):
        E = self.eng["sp"]
        for b in outs:
            if b.lw is not None:
                E.e.wait_ge(b.lw[0], b.lw[1])


CFG = {"state_mode": 0, "do_state": True, "do_attn": True, "cores": 8, "passes": (False, True), "layers": L, "stop": None, "units": 12, "experts": NE}


def build():
    k = K()
    EI = "ExternalInput"
    EO = "ExternalOutput"
    xsT = k.dram("xsT", [D, 1024], F32, EI)
    xpT = k.dram("xpT", [D, 512], F32, EI)
    cak = k.dram("cak", [L, 512, 1024], F32, EI)
    cav = k.dram("cav", [L, 512, 1024], F32, EI)
    cbk = k.dram("cbk", [L, 512, 512], F32, EI)
    cbv = k.dram("cbv", [L, 512, 512], F32, EI)
    cond = k.dram("cond", [128, 8, 2], F32, EI)
    w_mod = k.dram("w_mod", [L, D, 6144], F32, EI)
    w_in = k.dram("w_in", [L, D, D_IN], F32, EI)
    w_br_a = k.dram("w_br_a", [L, D, D], F32, EI)
    w_br_b = k.dram("w_br_b", [L, D, D], F32, EI)
    w_out = k.dram("w_out", [L, D, D], F32, EI)
    w_gate = k.dram("w_gate", [L, NE, D, 512], F32, EI)
    w_up = k.dram("w_up", [L, NE, D, 512], F32, EI)
    w_down = k.dram("w_down", [L, NE, 512, D], F32, EI)
    bmod_d = k.dram("bmod", [128, L, 48], F32, EI)
    lnp_d = k.dram("lnp", [128, L, 4, 8], F32, EI)
    hv_d = k.dram("hv", [128, L, 3], F32, EI)
    lamv_d = k.dram("lamv", [128, 2, L * 128], F32, EI)
    kng_d = k.dram("kng", [128, L, 128], F32, EI)
    wr_d = k.dram("wr", [128, 8, NE], F32, EI)
    rb_d = k.dram("rb", [128, NE], F32, EI)
    rope_d = k.dram("rope", [4, 128, 1024], F32, EI)
    nonce_d = k.dram("nonce", [1, 16], mybir.dt.int32, EI)
    perm_d = k.dram("perm", [128, 2, 128], F32, EI)
    ident_d = k.dram("ident", [128, 128], F32, EI)
    sel_d = k.dram("sel", [NE, NE * 128], F32, EI)
    ypT = k.dram("ypT", [D, 512], F32, EO)
    ysT = k.dram("ysT", [D, 1024], F32, EO)
    sak = k.dram("sak", [2, L, 256, 1024], F32, EO)
    sav = k.dram("sav", [2, L, 256, 1024], F32, EO)
    sbk = k.dram("sbk", [2, L, 256, 512], F32, EO)
    sbv = k.dram("sbv", [2, L, 256, 512], F32, EO)
    ospill = k.dram("ospill", [2 * D, 2048], BF16)
    nc = k.nc
    shk = Buf(None, "shk")
    shv = Buf(None, "shv")
    shk_t = [[nc.dram_tensor(f"shk_{l}_{u}", [2, 128, 1024], BF16, addr_space="Shared").ap() for u in range(12)] for l in range(L)]
    shv_t = [[nc.dram_tensor(f"shv_{l}_{u}", [2, 128, 1024], BF16, addr_space="Shared").ap() for u in range(12)] for l in range(L)]
    shf_t = [nc.dram_tensor(f"shf_{l}", [2, 16], mybir.dt.int32, addr_space="Shared").ap() for l in range(L)]
    pv_me = nc.gpsimd.snap(nc.gpsimd.partition_id() % 2)
    rn = nc.gpsimd.register("rn").__enter__()
    rf = nc.gpsimd.register("rf").__enter__()
    nc.gpsimd.reg_load(rn, nonce_d.t[0:1, 0:1])

    x = [k.sb(f"x{b}", [128, 8, 512], F32) for b in range(2)]
    h = [k.sb(f"h{b}", [128, 8, 512], BF16) for b in range(2)]
    abvs = [Buf(None, "abv0"), Buf(None, "abv1")]
    abs_ = [k.sb(f"ab{i}", [128, 9216], BF16) for i in range(2)]
    ab = abs_[0]
    OB = ab.t[:, 0:8192].rearrange("p (c t) -> p c t", c=16)
    NSLOT = 6
    wsl = [k.sb(f"wsl{i}", [128, 4096], BF16) for i in range(NSLOT)]
    cm = k.sb("cm", [128, 8, 512], BF16)
    NF = 7
    ftmp = [k.sb(f"ft{i}", [128, 512], F32) for i in range(NF)]
    NBT = 6
    btmp = [k.sb(f"bt{i}", [128, 512], BF16) for i in range(NBT)]
    NBP = 3
    bpair = [k.sb(f"bp{i}", [128, 1024], BF16) for i in range(NBP)]
    NRK = 4
    rkeep = [k.sb(f"rk{i}", [128, 512], F32) for i in range(NRK)]
    deferred = []
    deferred2 = []
    ropeCs = [k.sb(f"ropeC{b}", [128, 512], F32) for b in range(2)]
    ropeSs = [k.sb(f"ropeS{b}", [128, 512], F32) for b in range(2)]
    rope_state = [None, None]
    ckf = k.sb("ckf", [128, 4, 128], F32)
    h2f = k.sb("h2f", [128, 8, 128], F32)
    stg = [k.sb(f"stg{i}", [128, 4, 128], F32) for i in range(2)]
    combT = k.sb("combT", [NE, 2048], BF16)
    selb = k.sb("selb", [NE, NE * 128], BF16)
    ident = k.sb("idents", [128, 128], F32)
    onesb = k.sb("onesb", [128, 128], BF16)
    onesf = k.sb("onesf", [128, 128], F32)
    permb = k.sb("permb", [128, 2, 128], BF16)
    bmod = k.sb("bmods", [128, L, 48], F32)
    lnp = k.sb("lnps", [128, L, 4, 8], F32)
    hv = k.sb("hvs", [128, L, 3], F32)
    kng = k.sb("kngs", [128, L, 128], F32)
    wr = k.sb("wrs", [128, 8, NE], F32)
    rb = k.sb("rbs", [128, NE], F32)
    condf = k.sb("condf", [128, 8, 2], F32)
    condb = k.sb("condb", [128, 8, 2], BF16)
    modv = k.sb("modv", [128, 56], F32)
    gsub = k.sb("gsub", [128, L], F32)
    neglam = k.sb("neglam", [128, L], F32)
    rt = k.sb("rt", [128, 64], F32)
    small = k.sb("small", [128, 16], F32)

    PP = [nc.alloc_psum_tensor(f"PP{i}", [128, 1024], F32) for i in range(4)]
    P = [Buf(PP[i // 2][:, (i % 2) * 512:(i % 2 + 1) * 512], f"P{i}") for i in range(8)]
    rot = {}

    def bank(role, ids):
        i = rot.get(role, 0)
        rot[role] = i + 1
        return P[ids[i % len(ids)]]

    def tmpf():
        i = rot.get("tf", 0)
        rot["tf"] = i + 1
        return ftmp[i % NF]

    def tmpb():
        i = rot.get("tb", 0)
        rot["tb"] = i + 1
        return btmp[i % NBT]

    def tmpr():
        i = rot.get("tr", 0)
        rot["tr"] = i + 1
        return rkeep[i % NRK]

    def tmpp():
        i = rot.get("tp", 0)
        rot["tp"] = i + 1
        return bpair[i % NBP]

    def slot():
        i = rot.get("ws", 0)
        rot["ws"] = i + 1
        return wsl[i % NSLOT]

    op = k.op
    dma = k.dma

    dma("sp", ident, ident[:], ident_d, ident_d[:, :])
    dma("sp", bmod, bmod[:], bmod_d, bmod_d[:, :, :])
    dma("sp", lnp, lnp[:], lnp_d, lnp_d[:, :, :, :])
    dma("sp", hv, hv[:], hv_d, hv_d[:, :, :])
    dma("sp", kng, kng[:], kng_d, kng_d[:, :, :])
    dma("sp", wr, wr[:], wr_d, wr_d[:, :, :])
    dma("sp", rb, rb[:], rb_d, rb_d[:, :])
    dma("sp", condf, condf[:], cond, cond[:, :, :])
    dma("pool", permb, permb[:], perm_d, perm_d[:, :, :])
    dma("pool", selb, selb[:], sel_d, sel_d[:, :])
    op("dve", lambda v: v.memset(onesb[:], 1.0), [], [onesb])
    op("dve", lambda v: v.memset(onesf[:], 1.0), [], [onesf])
    op("act", lambda a: a.activation(out=condb[:], in_=condf[:], func=AF.Silu), [condf], [condb])
    la, lb = ftmp[0], ftmp[1]
    dma("sp", la, la[:], lamv_d, lamv_d[:, 0, :])
    dma("sp", lb, lb[:], lamv_d, lamv_d[:, 1, :])
    op("dve", lambda v: v.tensor_tensor(out=la[:], in0=la[:], in1=lb[:], op=ALU.mult), [la, lb], [la])
    op("dve", lambda v: v.tensor_reduce(out=small[:, 0:8], in_=la[:].rearrange("p (g e) -> p g e", e=64),
                                        axis=AX.X, op=ALU.add), [la], [small])
    op("act", lambda a: a.activation(out=small[:, 8:16], in_=small[:, 0:8], func=AF.Exp), [small], [small])
    for l in range(L):
        op("dve", lambda v, l=l: v.scalar_tensor_tensor(out=neglam[:, l:l + 1], in0=small[:, 9 + 2 * l:10 + 2 * l],
                                                        scalar=-lam_init(l), in1=small[:, 8 + 2 * l:9 + 2 * l],
                                                        op0=ALU.add, op1=ALU.subtract), [small], [neglam])
        op("dve", lambda v, l=l: v.tensor_scalar(out=gsub[:, l:l + 1], in0=hv[:, l, 0:1], scalar1=1.0 - lam_init(l),
                                                 scalar2=None, op0=ALU.mult), [hv], [gsub])

    def wpiece(sl, dst_ap, src_buf, src_ap):
        dma("pool", sl, dst_ap, src_buf, src_ap)

    def mm_group(ps_ap, pairs, reads, psb):
        def f(pe):
            n = len(pairs)
            r = None
            for i, (lt, rh) in enumerate(pairs):
                r = pe.matmul(ps_ap, lhsT=lt, rhs=rh, start=(i == 0), stop=(i == n - 1))
            return r
        op("pe", f, reads, [psb])

    def layer_norm_block(xb, l, which):
        pa_ = bank("ln", [6, 7])
        mm_group(pa_[:, :], [(onesf[:], xb[:, c, :]) for c in range(8)], [onesf, xb], pa_)
        pb_ = bank("ln", [6, 7])
        mt, vt, ta, tb_ = tmpf(), tmpf(), tmpf(), tmpf()
        for c in range(8):
            sq = ta if c % 2 == 0 else tb_
            op("act", lambda a, c=c, sq=sq: a.activation(out=sq[:], in_=xb[:, c, :], func=AF.Square), [xb], [sq])
            op("pe", lambda pe, c=c, sq=sq: pe.matmul(pb_[:, :], lhsT=onesf[:], rhs=sq[:], start=(c == 0), stop=(c == 7)),
               [onesf, sq], [pb_])
        op("act", lambda a: a.mul(out=mt[:], in_=pa_[:, :], mul=1.0 / D), [pa_], [mt])
        op("dve", lambda v: v.tensor_tensor(out=vt[:], in0=mt[:], in1=mt[:], op=ALU.mult), [mt], [vt])
        op("dve", lambda v: v.scalar_tensor_tensor(out=vt[:], in0=pb_[:, :], scalar=1.0 / D, in1=vt[:],
                                                   op0=ALU.mult, op1=ALU.subtract), [pb_, vt], [vt])
        op("act", lambda a: a.activation(out=vt[:], in_=vt[:], func=AF.Ln, bias=1e-6, scale=1.0), [vt], [vt])
        op("act", lambda a: a.activation(out=vt[:], in_=vt[:], func=AF.Exp, scale=-0.5), [vt], [vt])
        for c in range(8):
            t = ta if c % 2 == 0 else tb_
            op("dve", lambda v, c=c, t=t: v.tensor_tensor(out=t[:], in0=xb[:, c, :], in1=mt[:], op=ALU.subtract), [xb, mt], [t])
            op("dve", lambda v, t=t: v.tensor_tensor(out=t[:], in0=t[:], in1=vt[:], op=ALU.mult), [t, vt], [t])
            op("act", lambda a, c=c, t=t: a.activation(out=xb[:, c, :], in_=t[:], func=AF.Identity,
                                                       bias=lnp[:, l, 2 * which + 1, c:c + 1],
                                                       scale=lnp[:, l, 2 * which, c:c + 1]), [t, lnp], [xb])

    def modulate(NB, l, which):
        for b in range(NB):
            for c in range(8):
                op("act", lambda a, b=b, c=c: a.activation(out=h[b][:, c, :], in_=x[b][:, c, :], func=AF.Identity,
                                                           bias=modv[:, 24 * which + c:24 * which + c + 1],
                                                           scale=modv[:, 48 + 8 * which + c:48 + 8 * which + c + 1]),
                   [x[b], modv], [h[b]])

    def run_pass(sample):
        T = 1024 if sample else 512
        NB = T // 512
        ccol = 1 if sample else 0
        xin = xsT if sample else xpT
        yout = ysT if sample else ypT
        koff = 512 if sample else 0
        nkc = 20 if sample else 4
        if sample:
            seqs = [(list(range(20)), [(q * 512, 512) for q in range(2)])]
        else:
            seqs = [([0, 1], [(0, 256)]), ([2, 3], [(256, 256)])]
        for b in range(NB):
            dma("sp", x[b], x[b][:], xin, xin.t.rearrange("(c p) t -> p c t", p=128)[:, :, b * 512:(b + 1) * 512])

        for l in range(CFG["layers"]):
            pm = P[6]
            for piece in range(12):
                sl = slot()
                wv = sl.t[:, :].rearrange("p (c f) -> p c f", c=8)
                wpiece(sl, wv, w_mod, w_mod.t[l].rearrange("(c p) f -> p c f", p=128)[:, :, piece * 512:(piece + 1) * 512])
                for jj in range(4):
                    j = piece * 4 + jj
                    mm_group(pm[:, j:j + 1], [(wv[:, c, jj * 128:(jj + 1) * 128], condb[:, c, ccol:ccol + 1]) for c in range(8)],
                             [sl, condb], pm)
            op("dve", lambda v: v.tensor_tensor(out=modv[:, 0:48], in0=pm[:, 0:48], in1=bmod[:, l, :], op=ALU.add),
               [pm, bmod], [modv])
            op("dve", lambda v: v.tensor_scalar(out=modv[:, 48:56], in0=modv[:, 8:16], scalar1=1.0, scalar2=None, op0=ALU.add),
               [modv], [modv])
            if CFG["stop"] == "mod":
                continue
            modulate(NB, l, 0)

            def unit(u, mode):
                ai = rot.get("ab", 0)
                rot["ab"] = ai + 1
                ab = abs_[ai % 2]
                abv = abvs[ai % 2]
                QT = ab.t[:, 0:4096].rearrange("p (h t) -> p h t", h=2)
                KT = ab.t[:, 4096:6656]
                Vt = ab.t[:, 6656:9216].rearrange("p (c d) -> p c d", d=128)
                do_q = mode in ("all", "q")
                do_kv = mode in ("all", "kv")
                diff = u < 8
                g = u - 8
                nq = 128 if diff else 256
                qcol = u * 128 if diff else OFF_QB + g * 256
                kcol = OFF_KA + u * 128 if diff else OFF_KB + g * 128
                vcol = OFF_VA + u * 128 if diff else OFF_VB + g * 128
                sl = slot()
                wv = sl.t[:, :].rearrange("p (c f) -> p c f", c=8)
                win = w_in.t[l].rearrange("(c p) f -> p c f", p=128)
                if do_q:
                    wpiece(sl, wv[:, :, 0:nq], w_in, win[:, :, qcol:qcol + nq])
                if do_kv:
                    wpiece(sl, wv[:, :, nq:nq + 128], w_in, win[:, :, kcol:kcol + 128])
                    wpiece(sl, wv[:, :, nq + 128:nq + 256], w_in, win[:, :, vcol:vcol + 128])
                ckd = cak if diff else cbk
                cvd = cav if diff else cbv
                hcol = (u if diff else g) * 128
                permi = 0 if diff else 1
                if sample and mode == "q":
                    for hf in range(2):
                        dma("sp", ab, KT[:, 512 + hf * 1024:1536 + hf * 1024], shk, shk_t[l][u][hf])
                        dma("sp", ab, ab.t[:, 6656 + 512 + hf * 1024:6656 + 1536 + hf * 1024], shv, shv_t[l][u][hf])
                    dma("sp", ckf, ckf[:], ckd, ckd.t[l].rearrange("(c p) f -> p c f", p=128)[:, :, hcol:hcol + 128])
                    pt_ = bank("pm", [6, 7])
                    for c in range(4):
                        op("pe", lambda pe, c=c: pe.transpose(pt_[:, c * 128:(c + 1) * 128], ckf[:, c, :], ident[:]),
                           [ckf, ident], [pt_])
                    op("act", lambda a: a.copy(out=KT[:, 0:512], in_=pt_[:, :]), [pt_], [ab])
                    dma("pool", ab, Vt[:, 0:4, :], cvd, cvd.t[l].rearrange("(c p) f -> p c f", p=128)[:, :, hcol:hcol + 128],
                        sem_owner=abv)

                def rope_finish(src_b, dst_ap, b):
                    pp = bank("pm", [6, 7])
                    op("pe", lambda pe: pe.matmul(pp[:, :], lhsT=permb[:, permi, :], rhs=src_b[:], start=True, stop=True),
                       [permb, src_b], [pp])
                    t1, t2 = tmpf(), tmpf()
                    ropeC, ropeS = ropeCs[b], ropeSs[b]
                    op("dve", lambda v: v.tensor_tensor(out=t1[:], in0=pp[:, :], in1=ropeS[:], op=ALU.mult), [pp, ropeS], [t1])
                    op("dve", lambda v: v.tensor_tensor(out=t2[:], in0=src_b[:], in1=ropeC[:], op=ALU.mult), [src_b, ropeC], [t2])
                    op("dve", lambda v: v.tensor_tensor(out=dst_ap, in0=t1[:], in1=t2[:], op=ALU.add), [t1, t2], [ab])

                def qk_post(pq, n, dst_ap, b, gcol):
                    if diff:
                        if not sample:
                            op("act", lambda a: a.copy(out=dst_ap, in_=pq[:, 0:n]), [pq], [ab])
                            return
                        tb = tmpb()
                        op("act", lambda a: a.copy(out=tb[:], in_=pq[:, :]), [pq], [tb])
                        rope_finish(tb, dst_ap, b)
                        return
                    qf, sqb, rs = tmpf(), tmpb(), tmpf()
                    op("act", lambda a: a.copy(out=qf[:, 0:n], in_=pq[:, 0:n]), [pq], [qf])
                    op("act", lambda a: a.activation(out=sqb[:, 0:n], in_=pq[:, 0:n], func=AF.Square), [pq], [sqb])
                    pp = bank("pm", [6, 7])
                    op("pe", lambda pe: pe.matmul(pp[:, 0:n], lhsT=onesb[:], rhs=sqb[:, 0:n], start=True, stop=True),
                       [onesb, sqb], [pp])
                    op("act", lambda a: a.activation(out=rs[:, 0:n], in_=pp[:, 0:n], func=AF.Ln, bias=1e-6, scale=1.0 / 128),
                       [pp], [rs])
                    op("act", lambda a: a.activation(out=rs[:, 0:n], in_=rs[:, 0:n], func=AF.Exp, scale=-0.5), [rs], [rs])
                    if not sample:
                        op("dve", lambda v: v.scalar_tensor_tensor(out=dst_ap, in0=qf[:, 0:n], scalar=hv[:, l, gcol:gcol + 1],
                                                                   in1=rs[:, 0:n], op0=ALU.mult, op1=ALU.mult), [qf, hv, rs], [ab])
                        return
                    tb = tmpb()
                    op("dve", lambda v: v.scalar_tensor_tensor(out=tb[:], in0=qf[:], scalar=hv[:, l, gcol:gcol + 1],
                                                               in1=rs[:], op0=ALU.mult, op1=ALU.mult), [qf, hv, rs], [tb])
                    rope_finish(tb, dst_ap, b)

                for b in range(NB):
                    if sample:
                        ri = 0 if diff else 2
                        if rope_state[b] != ri:
                            rope_state[b] = ri
                            dma("sp", ropeCs[b], ropeCs[b][:], rope_d, rope_d[ri, :, b * 512:(b + 1) * 512])
                            dma("sp", ropeSs[b], ropeSs[b][:], rope_d, rope_d[ri + 1, :, b * 512:(b + 1) * 512])
                    for hh in (range(1 if diff else 2) if do_q else []):
                        pq = bank("pm", [6, 7])
                        mm_group(pq[:, :], [(wv[:, c, hh * 128:(hh + 1) * 128], h[b][:, c, :]) for c in range(8)], [sl, h[b]], pq)
                        qk_post(pq, 512, QT[:, hh, b * 512:(b + 1) * 512], b, 1)
                    if not do_kv:
                        continue
                    if mode == "kv":
                        continue
                    pq = bank("pm", [6, 7])
                    mm_group(pq[:, :], [(wv[:, c, nq:nq + 128], h[b][:, c, :]) for c in range(8)], [sl, h[b]], pq)
                    qk_post(pq, 512, KT[:, b * 512:(b + 1) * 512], b, 2)
                    pv = bank("pm", [6, 7])
                    for t in range(4):
                        mm_group(pv[:, t * 128:(t + 1) * 128],
                                 [(h[b][:, c, t * 128:(t + 1) * 128], wv[:, c, nq + 128:nq + 256]) for c in range(8)], [sl, h[b]], pv)
                    kc0 = b * 4
                    op("act", lambda a, pv=pv, kc0=kc0: a.copy(out=Vt[:, kc0:kc0 + 4, :],
                                                               in_=pv[:, :].rearrange("p (c d) -> p c d", d=128)), [pv], [ab])
                    if not sample and CFG["do_state"]:
                        so_v = sav if diff else sbv
                        so_k = sak if diff else sbk
                        s1 = stg[0]
                        op("act", lambda a, pv=pv: a.copy(out=s1[:], in_=pv[:, :].rearrange("p (c d) -> p c d", d=128)),
                           [pv], [s1])
                        for t4 in (range(4) if CFG["state_mode"] != 1 else []):
                            dma("sp", so_v, so_v.t[t4 // 2, l, (t4 % 2) * 128:(t4 % 2 + 1) * 128, hcol:hcol + 128],
                                s1, s1[:, t4, :])
                        if CFG["state_mode"] == 2:
                            continue
                        pk = bank("pm", [6, 7])
                        for t in range(4):
                            mm_group(pk[:, t * 128:(t + 1) * 128],
                                     [(h[b][:, c, t * 128:(t + 1) * 128], wv[:, c, nq:nq + 128]) for c in range(8)], [sl, h[b]], pk)
                        s2 = stg[1]
                        if diff:
                            op("act", lambda a, pk=pk: a.copy(out=s2[:], in_=pk[:, :].rearrange("p (c d) -> p c d", d=128)),
                               [pk], [s2])
                        else:
                            for t in range(4):
                                junk = tmpf()
                                op("act", lambda a, t=t, pk=pk, junk=junk: a.activation(
                                    out=junk[:, 0:128], in_=pk[:, t * 128:(t + 1) * 128], func=AF.Square), [pk], [junk])
                                op("dve", lambda v, t=t, junk=junk: v.tensor_reduce(out=small[:, t:t + 1], in_=junk[:, 0:128],
                                                                                   axis=AX.X, op=ALU.add), [junk], [small])
                            op("act", lambda a: a.activation(out=small[:, 4:8], in_=small[:, 0:4], func=AF.Sqrt, bias=1e-6,
                                                             scale=1.0 / 128), [small], [small])
                            op("dve", lambda v: v.reciprocal(out=small[:, 4:8], in_=small[:, 4:8]), [small], [small])
                            for t in range(4):
                                op("dve", lambda v, t=t, pk=pk: v.scalar_tensor_tensor(
                                    out=s2[:, t, :], in0=pk[:, t * 128:(t + 1) * 128], scalar=small[:, 4 + t:5 + t],
                                    in1=kng[:, l, :], op0=ALU.mult, op1=ALU.mult), [pk, small, kng], [s2])
                        for t4 in (range(4) if CFG["state_mode"] not in (1, 2) else []):
                            dma("sp", so_k, so_k.t[t4 // 2, l, (t4 % 2) * 128:(t4 % 2 + 1) * 128, hcol:hcol + 128],
                                s2, s2[:, t4, :])

                if mode == "kv":
                    kbase = 4 * (ai % 2)
                    pqs, pvs = [], []
                    for b in range(NB):
                        pq = P[kbase + b]
                        mm_group(pq[:, :], [(wv[:, c, nq:nq + 128], h[b][:, c, :]) for c in range(8)], [sl, h[b]], pq)
                        pqs.append(pq)
                    for b in range(NB):
                        pv = P[kbase + 2 + b]
                        for t in range(4):
                            mm_group(pv[:, t * 128:(t + 1) * 128],
                                     [(h[b][:, c, t * 128:(t + 1) * 128], wv[:, c, nq + 128:nq + 256]) for c in range(8)], [sl, h[b]], pv)
                        pvs.append(pv)
                    for b in range(NB):
                        op("act", lambda a, pv=pvs[b], kc0=b * 4: a.copy(out=Vt[:, kc0:kc0 + 4, :],
                                                                        in_=pv[:, :].rearrange("p (c d) -> p c d", d=128)), [pvs[b]], [ab])
                    for b in range(NB):
                        qk_post(pqs[b], 512, KT[:, b * 512:(b + 1) * 512], b, 2)
                    dma("pool", shk, shk_t[l][u][bass.ds(pv_me, 1), :, :], ab, KT[:, 0:1024])
                    dma("pool", shv, shv_t[l][u][bass.ds(pv_me, 1), :, :], ab, ab.t[:, 6656:6656 + 1024])
                    return None
                scale = 0.125 if diff else 128 ** -0.5
                if diff:
                    subs = [(0, slice(0, 64)), (0, slice(64, 128))]
                else:
                    subs = [(0, slice(0, 128)), (1, slice(0, 128))]

                def attn():
                    for (kcs, qtiles) in (seqs if CFG["do_attn"] else []):
                        for (q0, n) in qtiles:
                            R = []
                            for (hh, rows) in subs:
                                po = P[4]
                                psum_ = P[5]
                                npair = len(kcs) // 2
                                pts = [None] * npair

                                def s_and_exp(i):
                                    j = rot.get("sp", 0)
                                    rot["sp"] = j + 1
                                    j = j % 2
                                    lo, hi = P[2 * j], P[2 * j + 1]

                                    def f(pe):
                                        r_ = None
                                        for t_, pb_ in enumerate((lo, hi)):
                                            kc = kcs[2 * i + t_]
                                            r_ = pe.matmul(pb_[:, 0:n], lhsT=KT[rows, kc * 128:(kc + 1) * 128],
                                                           rhs=QT[rows, hh, q0:q0 + n], start=True, stop=True)
                                        return r_
                                    op("pe", f, [ab], [lo, hi])
                                    ptile = tmpp()
                                    op("act", lambda a: a.activation(
                                        out=ptile[:, :].rearrange("p (k q) -> p k q", k=2)[:, :, 0:n],
                                        in_=PP[j][:, :].rearrange("p (k q) -> p k q", k=2)[:, :, 0:n], func=AF.Exp, scale=scale),
                                       [lo, hi], [ptile])
                                    pts[i] = ptile

                                def pv_sum(i):
                                    ptile = pts[i]

                                    def f(pe):
                                        r_ = None
                                        for t_ in range(2):
                                            kc = kcs[2 * i + t_]
                                            first = (i == 0 and t_ == 0)
                                            last = (i == npair - 1 and t_ == 1)
                                            pe.matmul(po[:, 0:n], lhsT=Vt[:, kc, :], rhs=ptile[:, t_ * 512:t_ * 512 + n],
                                                      start=first, stop=last)
                                            r_ = pe.matmul(psum_[:, 0:n], lhsT=onesb[:], rhs=ptile[:, t_ * 512:t_ * 512 + n],
                                                           start=first, stop=last)
                                        return r_
                                    op("pe", f, [ab, ptile, onesb], [po, psum_])

                                for i in range(npair):
                                    s_and_exp(i)
                                    if i >= 1:
                                        pv_sum(i - 1)
                                    if i == 1 and deferred:
                                        deferred.pop(0)()
                                    if i == 5 and deferred2:
                                        deferred2.pop(0)()
                                pv_sum(npair - 1)
                                if npair < 2 and deferred:
                                    deferred.pop(0)()
                                if npair < 6:
                                    while deferred2:
                                        deferred2.pop(0)()
                                rec, r = tmpf(), tmpr()
                                op("dve", lambda v: v.tensor_scalar(out=r[:, 0:n], in0=po[:, 0:n], scalar1=1.0, scalar2=None,
                                                                    op0=ALU.mult), [po], [r])
                                op("act", lambda a: a.activation(out=rec[:, 0:n], in_=psum_[:, 0:n], func=AF.Ln), [psum_], [rec])
                                op("act", lambda a: a.activation(out=rec[:, 0:n], in_=rec[:, 0:n], func=AF.Exp, scale=-1.0), [rec], [rec])
                                op("dve", lambda v: v.tensor_tensor(out=r[:, 0:n], in0=r[:, 0:n], in1=rec[:, 0:n], op=ALU.mult),
                                   [r, rec], [r])
                                R.append(r)

                            def fin(R=R, q0=q0, n=n):
                                if diff:
                                    o, sqb, rs = tmpf(), tmpb(), tmpf()
                                    op("dve", lambda v: v.scalar_tensor_tensor(out=o[:, 0:n], in0=R[1][:, 0:n], scalar=neglam[:, l:l + 1],
                                                                               in1=R[0][:, 0:n], op0=ALU.mult, op1=ALU.add),
                                       [R[0], R[1], neglam], [o])
                                    op("act", lambda a: a.activation(out=sqb[:, 0:n], in_=o[:, 0:n], func=AF.Square), [o], [sqb])

                                    def fin_b():
                                        pp = P[7]
                                        op("pe", lambda pe: pe.matmul(pp[:, 0:n], lhsT=onesb[:], rhs=sqb[:, 0:n], start=True, stop=True),
                                           [onesb, sqb], [pp])
                                        op("act", lambda a: a.activation(out=rs[:, 0:n], in_=pp[:, 0:n], func=AF.Ln, bias=1e-6,
                                                                         scale=1.0 / 128), [pp], [rs])
                                        op("act", lambda a: a.activation(out=rs[:, 0:n], in_=rs[:, 0:n], func=AF.Exp, scale=-0.5), [rs], [rs])
                                        ot = tmpb()
                                        op("dve", lambda v: v.scalar_tensor_tensor(out=ot[:, 0:n], in0=o[:, 0:n], scalar=gsub[:, l:l + 1],
                                                                                   in1=rs[:, 0:n], op0=ALU.mult, op1=ALU.mult),
                                           [o, gsub, rs], [ot])
                                        dma("sp", ospill, ospill[u * 128:(u + 1) * 128, q0:q0 + n], ot, ot[:, 0:n])
                                    deferred2.append(fin_b)
                                else:
                                    for hh2 in range(2):
                                        ot = tmpb()
                                        op("act", lambda a, hh2=hh2, ot=ot: a.copy(out=ot[:, 0:n], in_=R[hh2][:, 0:n]), [R[hh2]], [ot])
                                        ch = 8 + g * 2 + hh2
                                        dma("sp", ospill, ospill[ch * 128:(ch + 1) * 128, q0:q0 + n], ot, ot[:, 0:n])
                            deferred.append(fin)
                return attn

            def run_units(mode):
                pend = None
                for u in range(CFG["units"]):
                    a_ = unit(u, mode)
                    if pend is not None:
                        pend()
                    pend = a_
                if pend is not None:
                    pend()
                while deferred:
                    deferred.pop(0)()
                while deferred2:
                    deferred2.pop(0)()

            if sample:
                run_units("kv")

                def hs(g_):
                    g_.reg_save(shf_t[l][bass.ds(pv_me, 1), 0:1], rn)
                    for slot_ in range(2):
                        g_.reg_mov(rf, 1)
                        with g_.While(rf):
                            g_.nop(cycle_cnt=4000)
                            g_.reg_load(rf, shf_t[l][slot_:slot_ + 1, 0:1])
                            g_.reg_sub(rf, rf, rn)
                    return g_.nop()
                op("pool", hs, [shk, shv], [shk, shv])
                run_units("q")
            else:
                run_units("all")
            if CFG["stop"] == "attn":
                continue
            for b in range(NB):
                dma("sp", ab, OB, ospill, ospill.t.rearrange("(c p) t -> p c t", p=128)[:, :, b * 512:(b + 1) * 512])
                win = w_in.t[l].rearrange("(c p) f -> p c f", p=128)
                for j in range(8):
                    Q0, Q1, Q2, Q3 = (P[4 * (j % 2) + i_] for i_ in range(4))
                    sl = slot()
                    wv = sl.t[:, :].rearrange("p (c f) -> p c f", c=8)
                    wpiece(sl, wv[:, :, 0:128], w_br_a, w_br_a.t[l].rearrange("(c p) f -> p c f", p=128)[:, :, j * 128:(j + 1) * 128])
                    wpiece(sl, wv[:, :, 128:256], w_br_b, w_br_b.t[l].rearrange("(c p) f -> p c f", p=128)[:, :, j * 128:(j + 1) * 128])
                    wpiece(sl, wv[:, :, 256:384], w_in, win[:, :, OFF_GATE + j * 128:OFF_GATE + (j + 1) * 128])
                    wpiece(sl, wv[:, :, 384:512], w_in, win[:, :, OFF_GATE + D + j * 128:OFF_GATE + D + (j + 1) * 128])
                    mm_group(Q0[:, :], [(wv[:, c, 0:128], OB[:, c, :]) for c in range(8)], [sl, ab], Q0)
                    mm_group(Q1[:, :], [(wv[:, c, 128:256], OB[:, 8 + c, :]) for c in range(8)], [sl, ab], Q1)
                    mm_group(Q2[:, :], [(wv[:, c, 256:384], h[b][:, c, :]) for c in range(8)], [sl, h[b]], Q2)
                    mm_group(Q3[:, :], [(wv[:, c, 384:512], h[b][:, c, :]) for c in range(8)], [sl, h[b]], Q3)
                    sa, sb_ = tmpf(), tmpf()
                    op("act", lambda a, sa=sa: a.activation(out=sa[:], in_=Q2[:, :], func=AF.Sigmoid), [Q2], [sa])
                    op("act", lambda a, sb_=sb_: a.activation(out=sb_[:], in_=Q3[:, :], func=AF.Sigmoid), [Q3], [sb_])
                    op("dve", lambda v, sa=sa: v.tensor_tensor(out=sa[:], in0=Q0[:, :], in1=sa[:], op=ALU.mult), [Q0, sa], [sa])
                    op("dve", lambda v, sb_=sb_: v.tensor_tensor(out=sb_[:], in0=Q1[:, :], in1=sb_[:], op=ALU.mult), [Q1, sb_], [sb_])
                    op("dve", lambda v, j=j, sa=sa, sb_=sb_: v.tensor_tensor(out=cm[:, j, :], in0=sa[:], in1=sb_[:], op=ALU.add),
                       [sa, sb_], [cm])
                for c in range(8):
                    op("act", lambda a, c=c: a.mul(out=x[b][:, c, :], in_=x[b][:, c, :], mul=ALPHA), [x[b]], [x[b]])
                for jh in range(2):
                    sl = slot()
                    wv = sl.t[:, :].rearrange("p (c f) -> p c f", c=8)
                    wpiece(sl, wv, w_out, w_out.t[l].rearrange("(c p) f -> p c f", p=128)[:, :, jh * 512:(jh + 1) * 512])
                    for jj in range(4):
                        j = jh * 4 + jj
                        pmx = bank("proj", [4, 5])
                        mm_group(pmx[:, :], [(wv[:, c, jj * 128:(jj + 1) * 128], cm[:, c, :]) for c in range(8)], [sl, cm], pmx)
                        op("dve", lambda v, j=j, pmx=pmx: v.scalar_tensor_tensor(
                            out=x[b][:, j, :], in0=pmx[:, :], scalar=modv[:, 16 + j:17 + j], in1=x[b][:, j, :],
                            op0=ALU.mult, op1=ALU.add), [pmx, modv, x[b]], [x[b]])
                layer_norm_block(x[b], l, 0)

            if CFG["stop"] == "phaseC":
                continue
            op("dve", lambda v: v.tensor_scalar(out=modv[:, 48:56], in0=modv[:, 32:40], scalar1=1.0, scalar2=None, op0=ALU.add),
               [modv], [modv])
            for b in range(NB):
                for c in range(8):
                    op("act", lambda a, b=b, c=c: a.activation(out=h[b][:, c, :], in_=x[b][:, c, :], func=AF.Identity,
                                                               bias=modv[:, 24 + c:25 + c], scale=modv[:, 48 + c:49 + c]),
                       [x[b], modv], [h[b]])
                for t in range(4):
                    for c in range(8):
                        op("act", lambda a, b=b, c=c, t=t: a.activation(out=h2f[:, c, :], in_=x[b][:, c, t * 128:(t + 1) * 128],
                                                                       func=AF.Identity, bias=modv[:, 24 + c:25 + c],
                                                                       scale=modv[:, 48 + c:49 + c]), [x[b], modv], [h2f])
                    pr = P[7]
                    mm_group(pr[:, 0:NE], [(h2f[:, c, :], wr[:, c, :]) for c in range(8)], [h2f, wr], pr)
                    S_, BZ, TT, E1, WW = rt[:, 0:16], rt[:, 16:32], rt[:, 32:48], rt[:, 48:64], rt[:, 32:48]
                    sm = small
                    op("act", lambda a: a.activation(out=S_, in_=pr[:, 0:NE], func=AF.Sigmoid), [pr], [rt])
                    op("dve", lambda v: v.tensor_tensor(out=BZ, in0=S_, in1=rb[:], op=ALU.add), [rt, rb], [rt])
                    op("dve", lambda v: v.tensor_reduce(out=sm[:, 0:4], in_=BZ.rearrange("p (g e) -> p g e", e=4), axis=AX.X, op=ALU.max),
                       [rt], [sm])
                    for gg in range(4):
                        op("dve", lambda v, gg=gg: v.tensor_scalar(out=TT[:, 4 * gg:4 * gg + 4], in0=BZ[:, 4 * gg:4 * gg + 4],
                                                                   scalar1=sm[:, gg:gg + 1], scalar2=-BIG, op0=ALU.is_equal,
                                                                   op1=ALU.mult), [rt, sm], [rt])
                    op("dve", lambda v: v.tensor_tensor(out=TT, in0=TT, in1=BZ, op=ALU.add), [rt], [rt])
                    op("dve", lambda v: v.tensor_reduce(out=sm[:, 4:8], in_=TT.rearrange("p (g e) -> p g e", e=4), axis=AX.X, op=ALU.max),
                       [rt], [sm])
                    op("dve", lambda v: v.tensor_tensor(out=sm[:, 0:4], in0=sm[:, 0:4], in1=sm[:, 4:8], op=ALU.add), [sm], [sm])
                    op("dve", lambda v: v.tensor_reduce(out=sm[:, 8:9], in_=sm[:, 0:4], axis=AX.X, op=ALU.max), [sm], [sm])
                    op("dve", lambda v: v.tensor_scalar(out=sm[:, 4:8], in0=sm[:, 0:4], scalar1=sm[:, 8:9], scalar2=BIG,
                                                        op0=ALU.is_ge, op1=ALU.mult), [sm], [sm])
                    op("dve", lambda v: v.tensor_scalar(out=sm[:, 4:8], in0=sm[:, 4:8], scalar1=-BIG, scalar2=None,
                                                        op0=ALU.add), [sm], [sm])
                    for gg in range(4):
                        op("dve", lambda v, gg=gg: v.tensor_scalar(out=TT[:, 4 * gg:4 * gg + 4], in0=BZ[:, 4 * gg:4 * gg + 4],
                                                                   scalar1=sm[:, 4 + gg:5 + gg], scalar2=None, op0=ALU.add),
                           [rt, sm], [rt])
                    op("dve", lambda v: v.tensor_reduce(out=sm[:, 9:10], in_=TT, axis=AX.X, op=ALU.max), [rt], [sm])
                    op("dve", lambda v: v.tensor_scalar(out=E1, in0=TT, scalar1=sm[:, 9:10], scalar2=None, op0=ALU.is_equal),
                       [rt, sm], [rt])
                    op("dve", lambda v: v.scalar_tensor_tensor(out=TT, in0=E1, scalar=-BIG, in1=TT, op0=ALU.mult, op1=ALU.add),
                       [rt], [rt])
                    op("dve", lambda v: v.tensor_reduce(out=sm[:, 10:11], in_=TT, axis=AX.X, op=ALU.max), [rt], [sm])
                    op("dve", lambda v: v.scalar_tensor_tensor(out=E1, in0=TT, scalar=sm[:, 10:11], in1=E1, op0=ALU.is_equal,
                                                               op1=ALU.add), [rt, sm], [rt])
                    op("dve", lambda v: v.tensor_tensor(out=WW, in0=S_, in1=E1, op=ALU.mult), [rt], [rt])
                    op("dve", lambda v: v.tensor_reduce(out=sm[:, 11:12], in_=WW, axis=AX.X, op=ALU.add), [rt], [sm])
                    op("dve", lambda v: v.reciprocal(out=sm[:, 11:12], in_=sm[:, 11:12]), [sm], [sm])
                    op("dve", lambda v: v.tensor_scalar(out=WW, in0=WW, scalar1=sm[:, 11:12], scalar2=None, op0=ALU.mult),
                       [rt, sm], [rt])
                    ptr = P[6]
                    op("pe", lambda pe: pe.transpose(ptr[0:NE, 0:128], WW, ident[:]), [rt, ident], [ptr])
                    tcol = b * 512 + t * 128
                    op("act", lambda a, tcol=tcol: a.copy(out=combT[:, tcol:tcol + 128], in_=ptr[0:NE, 0:128]), [ptr], [combT])
                for c in range(8):
                    op("act", lambda a, b=b, c=c: a.mul(out=x[b][:, c, :], in_=x[b][:, c, :], mul=ALPHA), [x[b]], [x[b]])
            for e in range(CFG["experts"]):
                sg_, su_, sd_ = slot(), slot(), slot()
                wg = sg_.t[:, :].rearrange("p (c f) -> p c f", c=8)
                wu = su_.t[:, :].rearrange("p (c f) -> p c f", c=8)
                wd = sd_.t[:, :].rearrange("p (c f) -> p c f", c=4)
                wpiece(sg_, wg, w_gate, w_gate.t[l, e].rearrange("(c p) f -> p c f", p=128))
                wpiece(su_, wu, w_up, w_up.t[l, e].rearrange("(c p) f -> p c f", p=128))
                wpiece(sd_, wd, w_down, w_down.t[l, e].rearrange("(c p) f -> p c f", p=128))
                for b in range(NB):
                    pbc = P[6]
                    op("pe", lambda pe, b=b: pe.matmul(pbc[:, :], lhsT=selb[:, e * 128:(e + 1) * 128],
                                                       rhs=combT[:, b * 512:(b + 1) * 512], start=True, stop=True),
                       [selb, combT], [pbc])
                    bcs = tmpf()
                    op("act", lambda a, bcs=bcs: a.copy(out=bcs[:], in_=pbc[:, :]), [pbc], [bcs])
                    for f in range(4):
                        pg = bank("g", [0, 1])
                        pu = bank("u", [2, 3])
                        mm_group(pg[:, :], [(wg[:, c, f * 128:(f + 1) * 128], h[b][:, c, :]) for c in range(8)], [sg_, h[b]], pg)
                        mm_group(pu[:, :], [(wu[:, c, f * 128:(f + 1) * 128], h[b][:, c, :]) for c in range(8)], [su_, h[b]], pu)
                        sgt = tmpf()
                        op("act", lambda a, pg=pg, sgt=sgt: a.activation(out=sgt[:], in_=pg[:, :], func=AF.Silu), [pg], [sgt])
                        op("dve", lambda v, pu=pu, sgt=sgt: v.tensor_tensor(out=sgt[:], in0=pu[:, :], in1=sgt[:], op=ALU.mult),
                           [pu, sgt], [sgt])
                        op("dve", lambda v, f=f, sgt=sgt, bcs=bcs: v.tensor_tensor(out=cm[:, f, :], in0=sgt[:], in1=bcs[:], op=ALU.mult),
                           [sgt, bcs], [cm])
                    for j in range(8):
                        py = bank("proj", [4, 5])
                        mm_group(py[:, :], [(wd[:, f, j * 128:(j + 1) * 128], cm[:, f, :]) for f in range(4)], [sd_, cm], py)
                        op("dve", lambda v, b=b, j=j, py=py: v.scalar_tensor_tensor(
                            out=x[b][:, j, :], in0=py[:, :], scalar=modv[:, 40 + j:41 + j], in1=x[b][:, j, :],
                            op0=ALU.mult, op1=ALU.add), [py, modv, x[b]], [x[b]])
            for b in range(NB):
                layer_norm_block(x[b], l, 1)

        for b in range(NB):
            dma("sp", yout, yout.t.rearrange("(c p) t -> p c t", p=128)[:, :, b * 512:(b + 1) * 512], x[b], x[b][:])

    for p_ in CFG["passes"]:
        run_pass(p_)
    k.finish([ypT, ysT, sak, sav, sbk, sbv])
    return k


def _rope_tables():
    tok = np.arange(2048)
    row = (tok // 64).astype(np.float64)
    col = (tok % 64).astype(np.float64)
    out = np.zeros((4, 128, 2048), np.float32)
    for p in range(128):
        pp = p % 64
        axis, half, f = pp // 32, (pp % 32) // 16, pp % 16
        inv = 10000.0 ** (-f / 16.0)
        ang = (row if axis == 0 else col) * np.float32(inv).astype(np.float64)
        ang = (np.float32(row if axis == 0 else col) * np.float32(inv)).astype(np.float32)
        out[0, p] = np.cos(ang)
        out[1, p] = np.sin(ang) * (-1.0 if half == 0 else 1.0)
    for p in range(128):
        axis, half, f = p // 64, (p % 64) // 32, p % 32
        inv = np.float32(10000.0) ** np.float32(-f / 32.0)
        ang = (np.float32(row if axis == 0 else col) * np.float32(inv)).astype(np.float32)
        out[2, p] = np.cos(ang)
        out[3, p] = np.sin(ang) * (-1.0 if half == 0 else 1.0)
    perm = np.zeros((128, 2, 128), np.float32)
    for m in range(128):
        pp = m % 64
        half = (pp % 32) // 16
        src = m + 16 if half == 0 else m - 16
        perm[src, 0, m] = 1.0
        half = (m % 64) // 32
        src = m + 32 if half == 0 else m - 32
        perm[src, 1, m] = 1.0
    return out, perm


_CACHE = {}


def kernel(x_prompt, x_sample, cache_a_k, cache_a_v, cache_b_k, cache_b_v, c, c_ctx,
           w_mod, b_mod, w_in, lam_q1, lam_k1, lam_q2, lam_k2, subln_g, qn_g, kn_g,
           w_br_a, w_br_b, w_out, ln1_g, ln1_b, ln2_g, ln2_b, w_router, router_bias,
           w_gate, w_up, w_down):
    f = lambda a: np.ascontiguousarray(np.asarray(a, dtype=np.float32))
    x_prompt, x_sample = f(x_prompt), f(x_sample)
    rope, perm = _rope_tables()
    nonce = np.full((1, 16), (int.from_bytes(os.urandom(3), "little") + 1), np.int32)
    sel = np.zeros((NE, NE * 128), np.float32)
    for e in range(NE):
        sel[e, e * 128:(e + 1) * 128] = 1.0
    shared = {
        "w_mod": f(w_mod), "w_in": f(w_in), "w_br_a": f(w_br_a), "w_br_b": f(w_br_b), "w_out": f(w_out),
        "w_gate": f(w_gate), "w_up": f(w_up), "w_down": f(w_down),
        "bmod": f(np.asarray(b_mod).reshape(L, 48, 128).transpose(2, 0, 1)),
        "lnp": f(np.stack([np.asarray(a).reshape(L, 8, 128) for a in (ln1_g, ln1_b, ln2_g, ln2_b)], 1).transpose(3, 0, 1, 2)),
        "hv": f(np.stack([np.asarray(subln_g), np.asarray(qn_g), np.asarray(kn_g)], -1).transpose(1, 0, 2)),
        "lamv": f(np.broadcast_to(np.stack([
            np.concatenate([np.asarray(lam_q1), np.asarray(lam_q2)], -1).reshape(-1),
            np.concatenate([np.asarray(lam_k1), np.asarray(lam_k2)], -1).reshape(-1)], 0)[None], (128, 2, L * 128))),
        "kng": f(np.broadcast_to(np.asarray(kn_g)[None], (128, L, 128))),
        "wr": f(np.asarray(w_router).reshape(8, 128, NE).transpose(1, 0, 2)),
        "rb": f(np.broadcast_to(np.asarray(router_bias)[None], (128, NE))),
        "perm": perm, "ident": np.eye(128, dtype=np.float32), "sel": sel,
    }
    cak, cav = f(cache_a_k).reshape(4, L, 512, 1024), f(cache_a_v).reshape(4, L, 512, 1024)
    cbk, cbv = f(cache_b_k).reshape(4, L, 512, 512), f(cache_b_v).reshape(4, L, 512, 512)
    c, c_ctx = f(c), f(c_ctx)
    in_maps = []
    NC_ = CFG["cores"]
    for core in range(NC_):
        b = core // 2
        m = dict(shared)
        hp = core % 2
        m["xsT"] = f(x_sample[b, hp * 1024:(hp + 1) * 1024].T)
        m["rope"] = np.ascontiguousarray(rope[:, :, hp * 1024:(hp + 1) * 1024])
        m["nonce"] = nonce
        m["xpT"] = f(x_prompt[2 * core:2 * core + 2].reshape(512, D).T)
        m["cak"], m["cav"], m["cbk"], m["cbv"] = cak[b], cav[b], cbk[b], cbv[b]
        m["cond"] = f(np.stack([c_ctx.reshape(8, 128), c[b].reshape(8, 128)], -1).transpose(1, 0, 2))
        in_maps.append(m)
    if "k" not in _CACHE:
        _CACHE["k"] = build()
    res = list(run_bass_kernel_spmd(_CACHE["k"].nc, in_maps, core_ids=list(range(NC_))).results)
    while len(res) < 8:
        res.append({k_: np.zeros_like(v_) for k_, v_ in res[0].items()})
    y_prompt = np.concatenate([res[i]["ypT"].T.reshape(2, 256, D) for i in range(8)], 0)
    y_sample = np.stack([np.concatenate([res[2 * b]["ysT"].T, res[2 * b + 1]["ysT"].T], 0) for b in range(4)], 0)
    sa_k = np.concatenate([res[i]["sak"] for i in range(8)], 0).reshape(16, L, 256, 8, 128)
    sa_v = np.concatenate([res[i]["sav"] for i in range(8)], 0).reshape(16, L, 256, 8, 128)
    sb_k = np.concatenate([res[i]["sbk"] for i in range(8)], 0).reshape(16, L, 256, 4, 128)
    sb_v = np.concatenate([res[i]["sbv"] for i in range(8)], 0).reshape(16, L, 256, 4, 128)
    return (np.ascontiguousarray(y_prompt, dtype=np.float32), np.ascontiguousarray(y_sample, dtype=np.float32),
            np.ascontiguousarray(sa_k, dtype=np.float32), np.ascontiguousarray(sa_v, dtype=np.float32),
            np.ascontiguousarray(sb_k, dtype=np.float32), np.ascontiguousarray(sb_v, dtype=np.float32))
```

```python
import math
import os
import numpy as np
import concourse.bass as bass
import concourse.mybir as mybir
from concourse.bass_utils import run_bass_kernel_spmd

F32 = mybir.dt.float32
BF16 = mybir.dt.bfloat16
AF = mybir.ActivationFunctionType
ALU = mybir.AluOpType
AX = mybir.AxisListType

D = 1024
L = 4
OFF_KA, OFF_VA, OFF_QB, OFF_KB, OFF_VB, OFF_GATE, D_IN = 1024, 2048, 3072, 4096, 4608, 5120, 7168
ALPHA = (2 * L) ** 0.25
BIG = 1.0e9
NE = 16


def lam_init(l):
    return 0.8 - 0.6 * math.exp(-0.3 * l)


class Buf:
    def __init__(self, t, name):
        self.t = t
        self.name = name
        self.lw = None
        self.rd = {}
        self.dsem = None
        self.dcnt = 0

    def __getitem__(self, idx):
        return self.t[idx]


class Eng:
    def __init__(self, nc, name, e):
        self.e = e
        self.sem = nc.alloc_semaphore("es_" + name)
        self.cnt = 0
        self.seen = {}


class K:
    def __init__(self):
        self.nc = nc = bass.Bass("TRN2", target_bir_lowering=False)
        self.eng = {"pe": Eng(nc, "pe", nc.tensor), "act": Eng(nc, "act", nc.scalar),
                    "dve": Eng(nc, "dve", nc.vector), "pool": Eng(nc, "pool", nc.gpsimd),
                    "sp": Eng(nc, "sp", nc.sync)}

    def sb(self, name, shape, dt):
        return Buf(self.nc.alloc_sbuf_tensor(name, list(shape), dt), name)

    def ps(self, name, shape, dt=F32):
        return Buf(self.nc.alloc_psum_tensor(name, list(shape), dt), name)

    def dram(self, name, shape, dt, kind="Internal", addr_space="Local"):
        return Buf(self.nc.dram_tensor(name, list(shape), dt, kind=kind, addr_space=addr_space).ap(), name)

    def _waits(self, E, reads, writes, skip_self=False):
        need = {}

        def add(ev):
            if ev is None:
                return
            s, v = ev
            if s.num not in need or need[s.num][1] < v:
                need[s.num] = (s, v)

        for b in reads:
            add(b.lw)
        for b in writes:
            add(b.lw)
            for ev in b.rd.values():
                add(ev)
        for num, (s, v) in need.items():
            if skip_self and num == E.sem.num:
                continue
            if E.seen.get(num, 0) >= v:
                continue
            E.e.wait_ge(s, v)
            E.seen[num] = v

    def _record(self, ev, reads, writes):
        for b in writes:
            b.lw = ev
            b.rd = {}
        for b in reads:
            if b in writes:
                continue
            b.rd[ev[0].num] = ev

    def op(self, eng, fn, reads=(), writes=()):
        E = self.eng[eng]
        self._waits(E, reads, writes, skip_self=(eng == "pe"))
        ins = fn(E.e)
        E.cnt += 1
        ins.then_inc(E.sem, 1)
        self._record((E.sem, E.cnt), reads, writes)

    def dma(self, q, dst, dst_ap, src, src_ap, sem_owner=None):
        E = self.eng[q]
        self._waits(E, [src], [dst])
        own = sem_owner if sem_owner is not None else dst
        if own.dsem is None:
            own.dsem = self.nc.alloc_semaphore("ds_" + own.name)
        ins = E.e.dma_start(out=dst_ap, in_=src_ap)
        own.dcnt += 16
        ins.then_inc(own.dsem, 16)
        ev = (own.dsem, own.dcnt)
        self._record(ev, [src], [dst])

    def finish(self, outs):
        E = self.eng["sp"]
        for b in outs:
            if b.lw is not None:
                E.e.wait_ge(b.lw[0], b.lw[1])


CFG = {"state_mode": 0, "do_state": True, "do_attn": True, "cores": 8, "passes": (False, True), "layers": L, "stop": None, "units": 12, "experts": NE}


def build():
    k = K()
    EI = "ExternalInput"
    EO = "ExternalOutput"
    xsT = k.dram("xsT", [D, 1024], F32, EI)
    xpT = k.dram("xpT", [D, 512], F32, EI)
    cak = k.dram("cak", [L, 512, 1024], F32, EI)
    cav = k.dram("cav", [L, 512, 1024], F32, EI)
    cbk = k.dram("cbk", [L, 512, 512], F32, EI)
    cbv = k.dram("cbv", [L, 512, 512], F32, EI)
    cond = k.dram("cond", [128, 8, 2], F32, EI)
    w_mod = k.dram("w_mod", [L, D, 6144], F32, EI)
    w_in = k.dram("w_in", [L, D, D_IN], F32, EI)
    w_br_a = k.dram("w_br_a", [L, D, D], F32, EI)
    w_br_b = k.dram("w_br_b", [L, D, D], F32, EI)
    w_out = k.dram("w_out", [L, D, D], F32, EI)
    w_gate = k.dram("w_gate", [L, NE, D, 512], F32, EI)
    w_up = k.dram("w_up", [L, NE, D, 512], F32, EI)
    w_down = k.dram("w_down", [L, NE, 512, D], F32, EI)
    bmod_d = k.dram("bmod", [128, L, 48], F32, EI)
    lnp_d = k.dram("lnp", [128, L, 4, 8], F32, EI)
    hv_d = k.dram("hv", [128, L, 3], F32, EI)
    lamv_d = k.dram("lamv", [128, 2, L * 128], F32, EI)
    kng_d = k.dram("kng", [128, L, 128], F32, EI)
    wr_d = k.dram("wr", [128, 8, NE], F32, EI)
    rb_d = k.dram("rb", [128, NE], F32, EI)
    rope_d = k.dram("rope", [4, 128, 1024], F32, EI)
    nonce_d = k.dram("nonce", [1, 16], mybir.dt.int32, EI)
    perm_d = k.dram("perm", [128, 2, 128], F32, EI)
    ident_d = k.dram("ident", [128, 128], F32, EI)
    sel_d = k.dram("sel", [NE, NE * 128], F32, EI)
    ypT = k.dram("ypT", [D, 512], F32, EO)
    ysT = k.dram("ysT", [D, 1024], F32, EO)
    sak = k.dram("sak", [2, L, 256, 1024], F32, EO)
    sav = k.dram("sav", [2, L, 256, 1024], F32, EO)
    sbk = k.dram("sbk", [2, L, 256, 512], F32, EO)
    sbv = k.dram("sbv", [2, L, 256, 512], F32, EO)
    ospill = k.dram("ospill", [2 * D, 2048], BF16)
    nc = k.nc
    shk = Buf(None, "shk")
    shv = Buf(None, "shv")
    shk_t = [[nc.dram_tensor(f"shk_{l}_{u}", [2, 128, 1024], BF16, addr_space="Shared").ap() for u in range(12)] for l in range(L)]
    shv_t = [[nc.dram_tensor(f"shv_{l}_{u}", [2, 128, 1024], BF16, addr_space="Shared").ap() for u in range(12)] for l in range(L)]
    shf_t = [nc.dram_tensor(f"shf_{l}", [2, 16], mybir.dt.int32, addr_space="Shared").ap() for l in range(L)]
    pv_me = nc.gpsimd.snap(nc.gpsimd.partition_id() % 2)
    rn = nc.gpsimd.register("rn").__enter__()
    rf = nc.gpsimd.register("rf").__enter__()
    nc.gpsimd.reg_load(rn, nonce_d.t[0:1, 0:1])

    x = [k.sb(f"x{b}", [128, 8, 512], F32) for b in range(2)]
    h = [k.sb(f"h{b}", [128, 8, 512], BF16) for b in range(2)]
    abvs = [Buf(None, "abv0"), Buf(None, "abv1")]
    abs_ = [k.sb(f"ab{i}", [128, 9216], BF16) for i in range(2)]
    ab = abs_[0]
    OB = ab.t[:, 0:8192].rearrange("p (c t) -> p c t", c=16)
    NSLOT = 6
    wsl = [k.sb(f"wsl{i}", [128, 4096], BF16) for i in range(NSLOT)]
    cm = k.sb("cm", [128, 8, 512], BF16)
    NF = 7
    ftmp = [k.sb(f"ft{i}", [128, 512], F32) for i in range(NF)]
    NBT = 6
    btmp = [k.sb(f"bt{i}", [128, 512], BF16) for i in range(NBT)]
    NBP = 3
    bpair = [k.sb(f"bp{i}", [128, 1024], BF16) for i in range(NBP)]
    NRK = 4
    rkeep = [k.sb(f"rk{i}", [128, 512], F32) for i in range(NRK)]
    deferred = []
    deferred2 = []
    ropeCs = [k.sb(f"ropeC{b}", [128, 512], F32) for b in range(2)]
    ropeSs = [k.sb(f"ropeS{b}", [128, 512], F32) for b in range(2)]
    rope_state = [None, None]
    ckf = k.sb("ckf", [128, 4, 128], F32)
    h2f = k.sb("h2f", [128, 8, 128], F32)
    stg = [k.sb(f"stg{i}", [128, 4, 128], F32) for i in range(2)]
    combT = k.sb("combT", [NE, 2048], BF16)
    selb = k.sb("selb", [NE, NE * 128], BF16)
    ident = k.sb("idents", [128, 128], F32)
    onesb = k.sb("onesb", [128, 128], BF16)
    onesf = k.sb("onesf", [128, 128], F32)
    permb = k.sb("permb", [128, 2, 128], BF16)
    bmod = k.sb("bmods", [128, L, 48], F32)
    lnp = k.sb("lnps", [128, L, 4, 8], F32)
    hv = k.sb("hvs", [128, L, 3], F32)
    kng = k.sb("kngs", [128, L, 128], F32)
    wr = k.sb("wrs", [128, 8, NE], F32)
    rb = k.sb("rbs", [128, NE], F32)
    condf = k.sb("condf", [128, 8, 2], F32)
    condb = k.sb("condb", [128, 8, 2], BF16)
    modv = k.sb("modv", [128, 56], F32)
    gsub = k.sb("gsub", [128, L], F32)
    neglam = k.sb("neglam", [128, L], F32)
    rt = k.sb("rt", [128, 64], F32)
    small = k.sb("small", [128, 16], F32)

    PP = [nc.alloc_psum_tensor(f"PP{i}", [128, 1024], F32) for i in range(4)]
    P = [Buf(PP[i // 2][:, (i % 2) * 512:(i % 2 + 1) * 512], f"P{i}") for i in range(8)]
    rot = {}

    def bank(role, ids):
        i = rot.get(role, 0)
        rot[role] = i + 1
        return P[ids[i % len(ids)]]

    def tmpf():
        i = rot.get("tf", 0)
        rot["tf"] = i + 1
        return ftmp[i % NF]

    def tmpb():
        i = rot.get("tb", 0)
        rot["tb"] = i + 1
        return btmp[i % NBT]

    def tmpr():
        i = rot.get("tr", 0)
        rot["tr"] = i + 1
        return rkeep[i % NRK]

    def tmpp():
        i = rot.get("tp", 0)
        rot["tp"] = i + 1
        return bpair[i % NBP]

    def slot():
        i = rot.get("ws", 0)
        rot["ws"] = i + 1
        return wsl[i % NSLOT]

    op = k.op
    dma = k.dma

    dma("sp", ident, ident[:], ident_d, ident_d[:, :])
    dma("sp", bmod, bmod[:], bmod_d, bmod_d[:, :, :])
    dma("sp", lnp, lnp[:], lnp_d, lnp_d[:, :, :, :])
    dma("sp", hv, hv[:], hv_d, hv_d[:, :, :])
    dma("sp", kng, kng[:], kng_d, kng_d[:, :, :])
    dma("sp", wr, wr[:], wr_d, wr_d[:, :, :])
    dma("sp", rb, rb[:], rb_d, rb_d[:, :])
    dma("sp", condf, condf[:], cond, cond[:, :, :])
    dma("pool", permb, permb[:], perm_d, perm_d[:, :, :])
    dma("pool", selb, selb[:], sel_d, sel_d[:, :])
    op("dve", lambda v: v.memset(onesb[:], 1.0), [], [onesb])
    op("dve", lambda v: v.memset(onesf[:], 1.0), [], [onesf])
    op("act", lambda a: a.activation(out=condb[:], in_=condf[:], func=AF.Silu), [condf], [condb])
    la, lb = ftmp[0], ftmp[1]
    dma("sp", la, la[:], lamv_d, lamv_d[:, 0, :])
    dma("sp", lb, lb[:], lamv_d, lamv_d[:, 1, :])
    op("dve", lambda v: v.tensor_tensor(out=la[:], in0=la[:], in1=lb[:], op=ALU.mult), [la, lb], [la])
    op("dve", lambda v: v.tensor_reduce(out=small[:, 0:8], in_=la[:].rearrange("p (g e) -> p g e", e=64),
                                        axis=AX.X, op=ALU.add), [la], [small])
    op("act", lambda a: a.activation(out=small[:, 8:16], in_=small[:, 0:8], func=AF.Exp), [small], [small])
    for l in range(L):
        op("dve", lambda v, l=l: v.scalar_tensor_tensor(out=neglam[:, l:l + 1], in0=small[:, 9 + 2 * l:10 + 2 * l],
                                                        scalar=-lam_init(l), in1=small[:, 8 + 2 * l:9 + 2 * l],
                                                        op0=ALU.add, op1=ALU.subtract), [small], [neglam])
        op("dve", lambda v, l=l: v.tensor_scalar(out=gsub[:, l:l + 1], in0=hv[:, l, 0:1], scalar1=1.0 - lam_init(l),
                                                 scalar2=None, op0=ALU.mult), [hv], [gsub])

    def wpiece(sl, dst_ap, src_buf, src_ap):
        dma("pool", sl, dst_ap, src_buf, src_ap)

    def mm_group(ps_ap, pairs, reads, psb):
        def f(pe):
            n = len(pairs)
            r = None
            for i, (lt, rh) in enumerate(pairs):
                r = pe.matmul(ps_ap, lhsT=lt, rhs=rh, start=(i == 0), stop=(i == n - 1))
            return r
        op("pe", f, reads, [psb])

    def layer_norm_block(xb, l, which):
        pa_ = bank("ln", [6, 7])
        mm_group(pa_[:, :], [(onesf[:], xb[:, c, :]) for c in range(8)], [onesf, xb], pa_)
        pb_ = bank("ln", [6, 7])
        mt, vt, ta, tb_ = tmpf(), tmpf(), tmpf(), tmpf()
        for c in range(8):
            sq = ta if c % 2 == 0 else tb_
            op("act", lambda a, c=c, sq=sq: a.activation(out=sq[:], in_=xb[:, c, :], func=AF.Square), [xb], [sq])
            op("pe", lambda pe, c=c, sq=sq: pe.matmul(pb_[:, :], lhsT=onesf[:], rhs=sq[:], start=(c == 0), stop=(c == 7)),
               [onesf, sq], [pb_])
        op("act", lambda a: a.mul(out=mt[:], in_=pa_[:, :], mul=1.0 / D), [pa_], [mt])
        op("dve", lambda v: v.tensor_tensor(out=vt[:], in0=mt[:], in1=mt[:], op=ALU.mult), [mt], [vt])
        op("dve", lambda v: v.scalar_tensor_tensor(out=vt[:], in0=pb_[:, :], scalar=1.0 / D, in1=vt[:],
                                                   op0=ALU.mult, op1=ALU.subtract), [pb_, vt], [vt])
        op("act", lambda a: a.activation(out=vt[:], in_=vt[:], func=AF.Ln, bias=1e-6, scale=1.0), [vt], [vt])
        op("act", lambda a: a.activation(out=vt[:], in_=vt[:], func=AF.Exp, scale=-0.5), [vt], [vt])
        for c in range(8):
            t = ta if c % 2 == 0 else tb_
            op("dve", lambda v, c=c, t=t: v.tensor_tensor(out=t[:], in0=xb[:, c, :], in1=mt[:], op=ALU.subtract), [xb, mt], [t])
            op("dve", lambda v, t=t: v.tensor_tensor(out=t[:], in0=t[:], in1=vt[:], op=ALU.mult), [t, vt], [t])
            op("act", lambda a, c=c, t=t: a.activation(out=xb[:, c, :], in_=t[:], func=AF.Identity,
                                                       bias=lnp[:, l, 2 * which + 1, c:c + 1],
                                                       scale=lnp[:, l, 2 * which, c:c + 1]), [t, lnp], [xb])

    def modulate(NB, l, which):
        for b in range(NB):
            for c in range(8):
                op("act", lambda a, b=b, c=c: a.activation(out=h[b][:, c, :], in_=x[b][:, c, :], func=AF.Identity,
                                                           bias=modv[:, 24 * which + c:24 * which + c + 1],
                                                           scale=modv[:, 48 + 8 * which + c:48 + 8 * which + c + 1]),
                   [x[b], modv], [h[b]])

    def run_pass(sample):
        T = 1024 if sample else 512
        NB = T // 512
        ccol = 1 if sample else 0
        xin = xsT if sample else xpT
        yout = ysT if sample else ypT
        koff = 512 if sample else 0
        nkc = 20 if sample else 4
        if sample:
            seqs = [(list(range(20)), [(q * 512, 512) for q in range(2)])]
        else:
            seqs = [([0, 1], [(0, 256)]), ([2, 3], [(256, 256)])]
        for b in range(NB):
            dma("sp", x[b], x[b][:], xin, xin.t.rearrange("(c p) t -> p c t", p=128)[:, :, b * 512:(b + 1) * 512])

        for l in range(CFG["layers"]):
            pm = P[6]
            for piece in range(12):
                sl = slot()
                wv = sl.t[:, :].rearrange("p (c f) -> p c f", c=8)
                wpiece(sl, wv, w_mod, w_mod.t[l].rearrange("(c p) f -> p c f", p=128)[:, :, piece * 512:(piece + 1) * 512])
                for jj in range(4):
                    j = piece * 4 + jj
                    mm_group(pm[:, j:j + 1], [(wv[:, c, jj * 128:(jj + 1) * 128], condb[:, c, ccol:ccol + 1]) for c in range(8)],
                             [sl, condb], pm)
            op("dve", lambda v: v.tensor_tensor(out=modv[:, 0:48], in0=pm[:, 0:48], in1=bmod[:, l, :], op=ALU.add),
               [pm, bmod], [modv])
            op("dve", lambda v: v.tensor_scalar(out=modv[:, 48:56], in0=modv[:, 8:16], scalar1=1.0, scalar2=None, op0=ALU.add),
               [modv], [modv])
            if CFG["stop"] == "mod":
                continue
            modulate(NB, l, 0)

            def unit(u, mode):
                ai = rot.get("ab", 0)
                rot["ab"] = ai + 1
                ab = abs_[ai % 2]
                abv = abvs[ai % 2]
                QT = ab.t[:, 0:4096].rearrange("p (h t) -> p h t", h=2)
                KT = ab.t[:, 4096:6656]
                Vt = ab.t[:, 6656:9216].rearrange("p (c d) -> p c d", d=128)
                do_q = mode in ("all", "q")
                do_kv = mode in ("all", "kv")
                diff = u < 8
                g = u - 8
                nq = 128 if diff else 256
                qcol = u * 128 if diff else OFF_QB + g * 256
                kcol = OFF_KA + u * 128 if diff else OFF_KB + g * 128
                vcol = OFF_VA + u * 128 if diff else OFF_VB + g * 128
                sl = slot()
                wv = sl.t[:, :].rearrange("p (c f) -> p c f", c=8)
                win = w_in.t[l].rearrange("(c p) f -> p c f", p=128)
                if do_q:
                    wpiece(sl, wv[:, :, 0:nq], w_in, win[:, :, qcol:qcol + nq])
                if do_kv:
                    wpiece(sl, wv[:, :, nq:nq + 128], w_in, win[:, :, kcol:kcol + 128])
                    wpiece(sl, wv[:, :, nq + 128:nq + 256], w_in, win[:, :, vcol:vcol + 128])
                ckd = cak if diff else cbk
                cvd = cav if diff else cbv
                hcol = (u if diff else g) * 128
                permi = 0 if diff else 1
                if sample and mode == "q":
                    for hf in range(2):
                        dma("sp", ab, KT[:, 512 + hf * 1024:1536 + hf * 1024], shk, shk_t[l][u][hf])
                        dma("sp", ab, ab.t[:, 6656 + 512 + hf * 1024:6656 + 1536 + hf * 1024], shv, shv_t[l][u][hf])
                    dma("sp", ckf, ckf[:], ckd, ckd.t[l].rearrange("(c p) f -> p c f", p=128)[:, :, hcol:hcol + 128])
                    pt_ = bank("pm", [6, 7])
                    for c in range(4):
                        op("pe", lambda pe, c=c: pe.transpose(pt_[:, c * 128:(c + 1) * 128], ckf[:, c, :], ident[:]),
                           [ckf, ident], [pt_])
                    op("act", lambda a: a.copy(out=KT[:, 0:512], in_=pt_[:, :]), [pt_], [ab])
                    dma("pool", ab, Vt[:, 0:4, :], cvd, cvd.t[l].rearrange("(c p) f -> p c f", p=128)[:, :, hcol:hcol + 128],
                        sem_owner=abv)

                def rope_finish(src_b, dst_ap, b):
                    pp = bank("pm", [6, 7])
                    op("pe", lambda pe: pe.matmul(pp[:, :], lhsT=permb[:, permi, :], rhs=src_b[:], start=True, stop=True),
                       [permb, src_b], [pp])
                    t1, t2 = tmpf(), tmpf()
                    ropeC, ropeS = ropeCs[b], ropeSs[b]
                    op("dve", lambda v: v.tensor_tensor(out=t1[:], in0=pp[:, :], in1=ropeS[:], op=ALU.mult), [pp, ropeS], [t1])
                    op("dve", lambda v: v.tensor_tensor(out=t2[:], in0=src_b[:], in1=ropeC[:], op=ALU.mult), [src_b, ropeC], [t2])
                    op("dve", lambda v: v.tensor_tensor(out=dst_ap, in0=t1[:], in1=t2[:], op=ALU.add), [t1, t2], [ab])

                def qk_post(pq, n, dst_ap, b, gcol):
                    if diff:
                        if not sample:
                            op("act", lambda a: a.copy(out=dst_ap, in_=pq[:, 0:n]), [pq], [ab])
                            return
                        tb = tmpb()
                        op("act", lambda a: a.copy(out=tb[:], in_=pq[:, :]), [pq], [tb])
                        rope_finish(tb, dst_ap, b)
                        return
                    qf, sqb, rs = tmpf(), tmpb(), tmpf()
                    op("act", lambda a: a.copy(out=qf[:, 0:n], in_=pq[:, 0:n]), [pq], [qf])
                    op("act", lambda a: a.activation(out=sqb[:, 0:n], in_=pq[:, 0:n], func=AF.Square), [pq], [sqb])
                    pp = bank("pm", [6, 7])
                    op("pe", lambda pe: pe.matmul(pp[:, 0:n], lhsT=onesb[:], rhs=sqb[:, 0:n], start=True, stop=True),
                       [onesb, sqb], [pp])
                    op("act", lambda a: a.activation(out=rs[:, 0:n], in_=pp[:, 0:n], func=AF.Ln, bias=1e-6, scale=1.0 / 128),
                       [pp], [rs])
                    op("act", lambda a: a.activation(out=rs[:, 0:n], in_=rs[:, 0:n], func=AF.Exp, scale=-0.5), [rs], [rs])
                    if not sample:
                        op("dve", lambda v: v.scalar_tensor_tensor(out=dst_ap, in0=qf[:, 0:n], scalar=hv[:, l, gcol:gcol + 1],
                                                                   in1=rs[:, 0:n], op0=ALU.mult, op1=ALU.mult), [qf, hv, rs], [ab])
                        return
                    tb = tmpb()
                    op("dve", lambda v: v.scalar_tensor_tensor(out=tb[:], in0=qf[:], scalar=hv[:, l, gcol:gcol + 1],
                                                               in1=rs[:], op0=ALU.mult, op1=ALU.mult), [qf, hv, rs], [tb])
                    rope_finish(tb, dst_ap, b)

                bi_ = [0]

                def nbank():
                    i_ = bi_[0]
                    bi_[0] += 1
                    return P[i_]
                tiles = []
                vks = []
                for b in range(NB):
                    if sample:
                        ri = 0 if diff else 2
                        if rope_state[b] != ri:
                            rope_state[b] = ri
                            dma("sp", ropeCs[b], ropeCs[b][:], rope_d, rope_d[ri, :, b * 512:(b + 1) * 512])
                            dma("sp", ropeSs[b], ropeSs[b][:], rope_d, rope_d[ri + 1, :, b * 512:(b + 1) * 512])
                    if mode == "kv":
                        continue
                    for hh in (range(1 if diff else 2) if do_q else []):
                        pq = nbank()
                        mm_group(pq[:, :], [(wv[:, c, hh * 128:(hh + 1) * 128], h[b][:, c, :]) for c in range(8)], [sl, h[b]], pq)
                        tiles.append((pq, QT[:, hh, b * 512:(b + 1) * 512], b, 1))
                    if mode != "all":
                        continue
                    pq = nbank()
                    mm_group(pq[:, :], [(wv[:, c, nq:nq + 128], h[b][:, c, :]) for c in range(8)], [sl, h[b]], pq)
                    tiles.append((pq, KT[:, b * 512:(b + 1) * 512], b, 2))
                    pv = nbank()
                    for t in range(4):
                        mm_group(pv[:, t * 128:(t + 1) * 128],
                                 [(h[b][:, c, t * 128:(t + 1) * 128], wv[:, c, nq + 128:nq + 256]) for c in range(8)], [sl, h[b]], pv)
                    pk = None
                    if not sample and CFG["do_state"]:
                        pk = nbank()
                        for t in range(4):
                            mm_group(pk[:, t * 128:(t + 1) * 128],
                                     [(h[b][:, c, t * 128:(t + 1) * 128], wv[:, c, nq:nq + 128]) for c in range(8)], [sl, h[b]], pk)
                    vks.append((b, pv, pk))
                for (b, pv, pk) in vks:
                    kc0 = b * 4
                    op("act", lambda a, pv=pv, kc0=kc0: a.copy(out=Vt[:, kc0:kc0 + 4, :],
                                                               in_=pv[:, :].rearrange("p (c d) -> p c d", d=128)), [pv], [ab])
                    if pk is None:
                        continue
                    so_v = sav if diff else sbv
                    so_k = sak if diff else sbk
                    s1 = stg[0]
                    op("act", lambda a, pv=pv: a.copy(out=s1[:], in_=pv[:, :].rearrange("p (c d) -> p c d", d=128)),
                       [pv], [s1])
                    for t4 in range(4):
                        dma("sp", so_v, so_v.t[t4 // 2, l, (t4 % 2) * 128:(t4 % 2 + 1) * 128, hcol:hcol + 128],
                            s1, s1[:, t4, :])
                    s2 = stg[1]
                    if diff:
                        op("act", lambda a, pk=pk: a.copy(out=s2[:], in_=pk[:, :].rearrange("p (c d) -> p c d", d=128)),
                           [pk], [s2])
                    else:
                        for t in range(4):
                            junk = tmpf()
                            op("act", lambda a, t=t, pk=pk, junk=junk: a.activation(
                                out=junk[:, 0:128], in_=pk[:, t * 128:(t + 1) * 128], func=AF.Square), [pk], [junk])
                            op("dve", lambda v, t=t, junk=junk: v.tensor_reduce(out=small[:, t:t + 1], in_=junk[:, 0:128],
                                                                               axis=AX.X, op=ALU.add), [junk], [small])
                        op("act", lambda a: a.activation(out=small[:, 4:8], in_=small[:, 0:4], func=AF.Ln, bias=1e-6,
                                                         scale=1.0 / 128), [small], [small])
                        op("act", lambda a: a.activation(out=small[:, 4:8], in_=small[:, 4:8], func=AF.Exp, scale=-0.5),
                           [small], [small])
                        for t in range(4):
                            op("dve", lambda v, t=t, pk=pk: v.scalar_tensor_tensor(
                                out=s2[:, t, :], in0=pk[:, t * 128:(t + 1) * 128], scalar=small[:, 4 + t:5 + t],
                                in1=kng[:, l, :], op0=ALU.mult, op1=ALU.mult), [pk, small, kng], [s2])
                    for t4 in range(4):
                        dma("sp", so_k, so_k.t[t4 // 2, l, (t4 % 2) * 128:(t4 % 2 + 1) * 128, hcol:hcol + 128],
                            s2, s2[:, t4, :])
                for (pq, dst_, b, gcol_) in tiles:
                    qk_post(pq, 512, dst_, b, gcol_)

                if mode == "kv":
                    kbase = 4 * (ai % 2)
                    pqs, pvs = [], []
                    for b in range(NB):
                        pq = P[kbase + b]
                        mm_group(pq[:, :], [(wv[:, c, nq:nq + 128], h[b][:, c, :]) for c in range(8)], [sl, h[b]], pq)
                        pqs.append(pq)
                    for b in range(NB):
                        pv = P[kbase + 2 + b]
                        for t in range(4):
                            mm_group(pv[:, t * 128:(t + 1) * 128],
                                     [(h[b][:, c, t * 128:(t + 1) * 128], wv[:, c, nq + 128:nq + 256]) for c in range(8)], [sl, h[b]], pv)
                        pvs.append(pv)
                    for b in range(NB):
                        op("act", lambda a, pv=pvs[b], kc0=b * 4: a.copy(out=Vt[:, kc0:kc0 + 4, :],
                                                                        in_=pv[:, :].rearrange("p (c d) -> p c d", d=128)), [pvs[b]], [ab])
                    for b in range(NB):
                        qk_post(pqs[b], 512, KT[:, b * 512:(b + 1) * 512], b, 2)
                    dma("pool", shk, shk_t[l][u][bass.ds(pv_me, 1), :, :], ab, KT[:, 0:1024])
                    dma("pool", shv, shv_t[l][u][bass.ds(pv_me, 1), :, :], ab, ab.t[:, 6656:6656 + 1024])
                    return None
                scale = 0.125 if diff else 128 ** -0.5
                if diff:
                    subs = [(0, slice(0, 64)), (0, slice(64, 128))]
                else:
                    subs = [(0, slice(0, 128)), (1, slice(0, 128))]

                def attn():
                    for (kcs, qtiles) in (seqs if CFG["do_attn"] else []):
                        for (q0, n) in qtiles:
                            R = []
                            for (hh, rows) in subs:
                                po = P[4]
                                psum_ = P[5]
                                npair = len(kcs) // 2
                                pts = [None] * npair

                                def s_and_exp(i):
                                    j = rot.get("sp", 0)
                                    rot["sp"] = j + 1
                                    j = j % 2
                                    lo, hi = P[2 * j], P[2 * j + 1]

                                    def f(pe):
                                        r_ = None
                                        for t_, pb_ in enumerate((lo, hi)):
                                            kc = kcs[2 * i + t_]
                                            r_ = pe.matmul(pb_[:, 0:n], lhsT=KT[rows, kc * 128:(kc + 1) * 128],
                                                           rhs=QT[rows, hh, q0:q0 + n], start=True, stop=True)
                                        return r_
                                    op("pe", f, [ab], [lo, hi])
                                    ptile = tmpp()
                                    op("act", lambda a: a.activation(
                                        out=ptile[:, :].rearrange("p (k q) -> p k q", k=2)[:, :, 0:n],
                                        in_=PP[j][:, :].rearrange("p (k q) -> p k q", k=2)[:, :, 0:n], func=AF.Exp, scale=scale),
                                       [lo, hi], [ptile])
                                    pts[i] = ptile

                                def pv_sum(i):
                                    ptile = pts[i]

                                    def f(pe):
                                        r_ = None
                                        for t_ in range(2):
                                            kc = kcs[2 * i + t_]
                                            first = (i == 0 and t_ == 0)
                                            last = (i == npair - 1 and t_ == 1)
                                            pe.matmul(po[:, 0:n], lhsT=Vt[:, kc, :], rhs=ptile[:, t_ * 512:t_ * 512 + n],
                                                      start=first, stop=last)
                                            r_ = pe.matmul(psum_[:, 0:n], lhsT=onesb[:], rhs=ptile[:, t_ * 512:t_ * 512 + n],
                                                           start=first, stop=last)
                                        return r_
                                    op("pe", f, [ab, ptile, onesb], [po, psum_])

                                for i in range(npair):
                                    s_and_exp(i)
                                    if i >= 1:
                                        pv_sum(i - 1)
                                    if i == 1 and deferred:
                                        deferred.pop(0)()
                                    if i == 5 and deferred2:
                                        deferred2.pop(0)()
                                pv_sum(npair - 1)
                                if npair < 2 and deferred:
                                    deferred.pop(0)()
                                if npair < 6:
                                    while deferred2:
                                        deferred2.pop(0)()
                                rec, r = tmpf(), tmpr()
                                op("dve", lambda v: v.tensor_scalar(out=r[:, 0:n], in0=po[:, 0:n], scalar1=1.0, scalar2=None,
                                                                    op0=ALU.mult), [po], [r])
                                op("act", lambda a: a.activation(out=rec[:, 0:n], in_=psum_[:, 0:n], func=AF.Ln), [psum_], [rec])
                                op("act", lambda a: a.activation(out=rec[:, 0:n], in_=rec[:, 0:n], func=AF.Exp, scale=-1.0), [rec], [rec])
                                op("dve", lambda v: v.tensor_tensor(out=r[:, 0:n], in0=r[:, 0:n], in1=rec[:, 0:n], op=ALU.mult),
                                   [r, rec], [r])
                                R.append(r)

                            def fin(R=R, q0=q0, n=n):
                                if diff:
                                    o, sqb, rs = tmpf(), tmpb(), tmpf()
                                    op("dve", lambda v: v.scalar_tensor_tensor(out=o[:, 0:n], in0=R[1][:, 0:n], scalar=neglam[:, l:l + 1],
                                                                               in1=R[0][:, 0:n], op0=ALU.mult, op1=ALU.add),
                                       [R[0], R[1], neglam], [o])
                                    op("act", lambda a: a.activation(out=sqb[:, 0:n], in_=o[:, 0:n], func=AF.Square), [o], [sqb])

                                    def fin_b():
                                        pp = P[7]
                                        op("pe", lambda pe: pe.matmul(pp[:, 0:n], lhsT=onesb[:], rhs=sqb[:, 0:n], start=True, stop=True),
                                           [onesb, sqb], [pp])
                                        op("act", lambda a: a.activation(out=rs[:, 0:n], in_=pp[:, 0:n], func=AF.Ln, bias=1e-6,
                                                                         scale=1.0 / 128), [pp], [rs])
                                        op("act", lambda a: a.activation(out=rs[:, 0:n], in_=rs[:, 0:n], func=AF.Exp, scale=-0.5), [rs], [rs])
                                        ot = tmpb()
                                        op("dve", lambda v: v.scalar_tensor_tensor(out=ot[:, 0:n], in0=o[:, 0:n], scalar=gsub[:, l:l + 1],
                                                                                   in1=rs[:, 0:n], op0=ALU.mult, op1=ALU.mult),
                                           [o, gsub, rs], [ot])
                                        dma("sp", ospill, ospill[u * 128:(u + 1) * 128, q0:q0 + n], ot, ot[:, 0:n])
                                    deferred2.append(fin_b)
                                else:
                                    for hh2 in range(2):
                                        ot = tmpb()
                                        op("act", lambda a, hh2=hh2, ot=ot: a.copy(out=ot[:, 0:n], in_=R[hh2][:, 0:n]), [R[hh2]], [ot])
                                        ch = 8 + g * 2 + hh2
                                        dma("sp", ospill, ospill[ch * 128:(ch + 1) * 128, q0:q0 + n], ot, ot[:, 0:n])
                            deferred.append(fin)
                return attn

            def run_units(mode):
                pend = None
                for u in range(CFG["units"]):
                    a_ = unit(u, mode)
                    if pend is not None:
                        pend()
                    pend = a_
                if pend is not None:
                    pend()
                while deferred:
                    deferred.pop(0)()
                while deferred2:
                    deferred2.pop(0)()

            if sample:
                run_units("kv")

                def hs(g_):
                    g_.reg_save(shf_t[l][bass.ds(pv_me, 1), 0:1], rn)
                    for slot_ in range(2):
                        g_.reg_mov(rf, 1)
                        with g_.While(rf):
                            g_.nop(cycle_cnt=4000)
                            g_.reg_load(rf, shf_t[l][slot_:slot_ + 1, 0:1])
                            g_.reg_sub(rf, rf, rn)
                    return g_.nop()
                op("pool", hs, [shk, shv], [shk, shv])
                run_units("q")
            else:
                run_units("all")
            if CFG["stop"] == "attn":
                continue
            for b in range(NB):
                dma("sp", ab, OB, ospill, ospill.t.rearrange("(c p) t -> p c t", p=128)[:, :, b * 512:(b + 1) * 512])
                win = w_in.t[l].rearrange("(c p) f -> p c f", p=128)
                for j in range(8):
                    Q0, Q1, Q2, Q3 = (P[4 * (j % 2) + i_] for i_ in range(4))
                    sl = slot()
                    wv = sl.t[:, :].rearrange("p (c f) -> p c f", c=8)
                    wpiece(sl, wv[:, :, 0:128], w_br_a, w_br_a.t[l].rearrange("(c p) f -> p c f", p=128)[:, :, j * 128:(j + 1) * 128])
                    wpiece(sl, wv[:, :, 128:256], w_br_b, w_br_b.t[l].rearrange("(c p) f -> p c f", p=128)[:, :, j * 128:(j + 1) * 128])
                    wpiece(sl, wv[:, :, 256:384], w_in, win[:, :, OFF_GATE + j * 128:OFF_GATE + (j + 1) * 128])
                    wpiece(sl, wv[:, :, 384:512], w_in, win[:, :, OFF_GATE + D + j * 128:OFF_GATE + D + (j + 1) * 128])
                    mm_group(Q0[:, :], [(wv[:, c, 0:128], OB[:, c, :]) for c in range(8)], [sl, ab], Q0)
                    mm_group(Q1[:, :], [(wv[:, c, 128:256], OB[:, 8 + c, :]) for c in range(8)], [sl, ab], Q1)
                    mm_group(Q2[:, :], [(wv[:, c, 256:384], h[b][:, c, :]) for c in range(8)], [sl, h[b]], Q2)
                    mm_group(Q3[:, :], [(wv[:, c, 384:512], h[b][:, c, :]) for c in range(8)], [sl, h[b]], Q3)
                    sa, sb_ = tmpf(), tmpf()
                    op("act", lambda a, sa=sa: a.activation(out=sa[:], in_=Q2[:, :], func=AF.Sigmoid), [Q2], [sa])
                    op("act", lambda a, sb_=sb_: a.activation(out=sb_[:], in_=Q3[:, :], func=AF.Sigmoid), [Q3], [sb_])
                    op("dve", lambda v, sa=sa: v.tensor_tensor(out=sa[:], in0=Q0[:, :], in1=sa[:], op=ALU.mult), [Q0, sa], [sa])
                    op("dve", lambda v, sb_=sb_: v.tensor_tensor(out=sb_[:], in0=Q1[:, :], in1=sb_[:], op=ALU.mult), [Q1, sb_], [sb_])
                    op("dve", lambda v, j=j, sa=sa, sb_=sb_: v.tensor_tensor(out=cm[:, j, :], in0=sa[:], in1=sb_[:], op=ALU.add),
                       [sa, sb_], [cm])
                for c in range(8):
                    op("act", lambda a, c=c: a.mul(out=x[b][:, c, :], in_=x[b][:, c, :], mul=ALPHA), [x[b]], [x[b]])
                for jh in range(2):
                    sl = slot()
                    wv = sl.t[:, :].rearrange("p (c f) -> p c f", c=8)
                    wpiece(sl, wv, w_out, w_out.t[l].rearrange("(c p) f -> p c f", p=128)[:, :, jh * 512:(jh + 1) * 512])
                    for jj in range(4):
                        j = jh * 4 + jj
                        pmx = bank("proj", [4, 5])
                        mm_group(pmx[:, :], [(wv[:, c, jj * 128:(jj + 1) * 128], cm[:, c, :]) for c in range(8)], [sl, cm], pmx)
                        op("dve", lambda v, j=j, pmx=pmx: v.scalar_tensor_tensor(
                            out=x[b][:, j, :], in0=pmx[:, :], scalar=modv[:, 16 + j:17 + j], in1=x[b][:, j, :],
                            op0=ALU.mult, op1=ALU.add), [pmx, modv, x[b]], [x[b]])
                layer_norm_block(x[b], l, 0)

            if CFG["stop"] == "phaseC":
                continue
            op("dve", lambda v: v.tensor_scalar(out=modv[:, 48:56], in0=modv[:, 32:40], scalar1=1.0, scalar2=None, op0=ALU.add),
               [modv], [modv])
            for b in range(NB):
                for c in range(8):
                    op("act", lambda a, b=b, c=c: a.activation(out=h[b][:, c, :], in_=x[b][:, c, :], func=AF.Identity,
                                                               bias=modv[:, 24 + c:25 + c], scale=modv[:, 48 + c:49 + c]),
                       [x[b], modv], [h[b]])
                for t in range(4):
                    for c in range(8):
                        op("act", lambda a, b=b, c=c, t=t: a.activation(out=h2f[:, c, :], in_=x[b][:, c, t * 128:(t + 1) * 128],
                                                                       func=AF.Identity, bias=modv[:, 24 + c:25 + c],
                                                                       scale=modv[:, 48 + c:49 + c]), [x[b], modv], [h2f])
                    pr = P[7]
                    mm_group(pr[:, 0:NE], [(h2f[:, c, :], wr[:, c, :]) for c in range(8)], [h2f, wr], pr)
                    S_, BZ, TT, E1, WW = rt[:, 0:16], rt[:, 16:32], rt[:, 32:48], rt[:, 48:64], rt[:, 32:48]
                    sm = small
                    op("act", lambda a: a.activation(out=S_, in_=pr[:, 0:NE], func=AF.Sigmoid), [pr], [rt])
                    op("dve", lambda v: v.tensor_tensor(out=BZ, in0=S_, in1=rb[:], op=ALU.add), [rt, rb], [rt])
                    op("dve", lambda v: v.tensor_reduce(out=sm[:, 0:4], in_=BZ.rearrange("p (g e) -> p g e", e=4), axis=AX.X, op=ALU.max),
                       [rt], [sm])
                    for gg in range(4):
                        op("dve", lambda v, gg=gg: v.tensor_scalar(out=TT[:, 4 * gg:4 * gg + 4], in0=BZ[:, 4 * gg:4 * gg + 4],
                                                                   scalar1=sm[:, gg:gg + 1], scalar2=-BIG, op0=ALU.is_equal,
                                                                   op1=ALU.mult), [rt, sm], [rt])
                    op("dve", lambda v: v.tensor_tensor(out=TT, in0=TT, in1=BZ, op=ALU.add), [rt], [rt])
                    op("dve", lambda v: v.tensor_reduce(out=sm[:, 4:8], in_=TT.rearrange("p (g e) -> p g e", e=4), axis=AX.X, op=ALU.max),
                       [rt], [sm])
                    op("dve", lambda v: v.tensor_tensor(out=sm[:, 0:4], in0=sm[:, 0:4], in1=sm[:, 4:8], op=ALU.add), [sm], [sm])
                    op("dve", lambda v: v.tensor_reduce(out=sm[:, 8:9], in_=sm[:, 0:4], axis=AX.X, op=ALU.max), [sm], [sm])
                    op("dve", lambda v: v.tensor_scalar(out=sm[:, 4:8], in0=sm[:, 0:4], scalar1=sm[:, 8:9], scalar2=BIG,
                                                        op0=ALU.is_ge, op1=ALU.mult), [sm], [sm])
                    op("dve", lambda v: v.tensor_scalar(out=sm[:, 4:8], in0=sm[:, 4:8], scalar1=-BIG, scalar2=None,
                                                        op0=ALU.add), [sm], [sm])
                    for gg in range(4):
                        op("dve", lambda v, gg=gg: v.tensor_scalar(out=TT[:, 4 * gg:4 * gg + 4], in0=BZ[:, 4 * gg:4 * gg + 4],
                                                                   scalar1=sm[:, 4 + gg:5 + gg], scalar2=None, op0=ALU.add),
                           [rt, sm], [rt])
                    op("dve", lambda v: v.tensor_reduce(out=sm[:, 9:10], in_=TT, axis=AX.X, op=ALU.max), [rt], [sm])
                    op("dve", lambda v: v.tensor_scalar(out=E1, in0=TT, scalar1=sm[:, 9:10], scalar2=None, op0=ALU.is_equal),
                       [rt, sm], [rt])
                    op("dve", lambda v: v.scalar_tensor_tensor(out=TT, in0=E1, scalar=-BIG, in1=TT, op0=ALU.mult, op1=ALU.add),
                       [rt], [rt])
                    op("dve", lambda v: v.tensor_reduce(out=sm[:, 10:11], in_=TT, axis=AX.X, op=ALU.max), [rt], [sm])
                    op("dve", lambda v: v.scalar_tensor_tensor(out=E1, in0=TT, scalar=sm[:, 10:11], in1=E1, op0=ALU.is_equal,
                                                               op1=ALU.add), [rt, sm], [rt])
                    op("dve", lambda v: v.tensor_tensor(out=WW, in0=S_, in1=E1, op=ALU.mult), [rt], [rt])
                    op("dve", lambda v: v.tensor_reduce(out=sm[:, 11:12], in_=WW, axis=AX.X, op=ALU.add), [rt], [sm])
                    op("dve", lambda v: v.reciprocal(out=sm[:, 11:12], in_=sm[:, 11:12]), [sm], [sm])
                    op("dve", lambda v: v.tensor_scalar(out=WW, in0=WW, scalar1=sm[:, 11:12], scalar2=None, op0=ALU.mult),
                       [rt, sm], [rt])
                    ptr = P[6]
                    op("pe", lambda pe: pe.transpose(ptr[0:NE, 0:128], WW, ident[:]), [rt, ident], [ptr])
                    tcol = b * 512 + t * 128
                    op("act", lambda a, tcol=tcol: a.copy(out=combT[:, tcol:tcol + 128], in_=ptr[0:NE, 0:128]), [ptr], [combT])
                for c in range(8):
                    op("act", lambda a, b=b, c=c: a.mul(out=x[b][:, c, :], in_=x[b][:, c, :], mul=ALPHA), [x[b]], [x[b]])
            for e in range(CFG["experts"]):
                sg_, su_, sd_ = slot(), slot(), slot()
                wg = sg_.t[:, :].rearrange("p (c f) -> p c f", c=8)
                wu = su_.t[:, :].rearrange("p (c f) -> p c f", c=8)
                wd = sd_.t[:, :].rearrange("p (c f) -> p c f", c=4)
                wpiece(sg_, wg, w_gate, w_gate.t[l, e].rearrange("(c p) f -> p c f", p=128))
                wpiece(su_, wu, w_up, w_up.t[l, e].rearrange("(c p) f -> p c f", p=128))
                wpiece(sd_, wd, w_down, w_down.t[l, e].rearrange("(c p) f -> p c f", p=128))
                for b in range(NB):
                    pbc = P[6]
                    op("pe", lambda pe, b=b: pe.matmul(pbc[:, :], lhsT=selb[:, e * 128:(e + 1) * 128],
                                                       rhs=combT[:, b * 512:(b + 1) * 512], start=True, stop=True),
                       [selb, combT], [pbc])
                    bcs = tmpf()
                    op("act", lambda a, bcs=bcs: a.copy(out=bcs[:], in_=pbc[:, :]), [pbc], [bcs])
                    for f in range(4):
                        pg = bank("g", [0, 1])
                        pu = bank("u", [2, 3])
                        mm_group(pg[:, :], [(wg[:, c, f * 128:(f + 1) * 128], h[b][:, c, :]) for c in range(8)], [sg_, h[b]], pg)
                        mm_group(pu[:, :], [(wu[:, c, f * 128:(f + 1) * 128], h[b][:, c, :]) for c in range(8)], [su_, h[b]], pu)
                        sgt = tmpf()
                        op("act", lambda a, pg=pg, sgt=sgt: a.activation(out=sgt[:], in_=pg[:, :], func=AF.Silu), [pg], [sgt])
                        op("dve", lambda v, pu=pu, sgt=sgt: v.tensor_tensor(out=sgt[:], in0=pu[:, :], in1=sgt[:], op=ALU.mult),
                           [pu, sgt], [sgt])
                        op("dve", lambda v, f=f, sgt=sgt, bcs=bcs: v.tensor_tensor(out=cm[:, f, :], in0=sgt[:], in1=bcs[:], op=ALU.mult),
                           [sgt, bcs], [cm])
                    for j in range(8):
                        py = bank("proj", [4, 5])
                        mm_group(py[:, :], [(wd[:, f, j * 128:(j + 1) * 128], cm[:, f, :]) for f in range(4)], [sd_, cm], py)
                        op("dve", lambda v, b=b, j=j, py=py: v.scalar_tensor_tensor(
                            out=x[b][:, j, :], in0=py[:, :], scalar=modv[:, 40 + j:41 + j], in1=x[b][:, j, :],
                            op0=ALU.mult, op1=ALU.add), [py, modv, x[b]], [x[b]])
            for b in range(NB):
                layer_norm_block(x[b], l, 1)

        for b in range(NB):
            dma("sp", yout, yout.t.rearrange("(c p) t -> p c t", p=128)[:, :, b * 512:(b + 1) * 512], x[b], x[b][:])

    for p_ in CFG["passes"]:
        run_pass(p_)
    k.finish([ypT, ysT, sak, sav, sbk, sbv])
    return k


def _rope_tables():
    tok = np.arange(2048)
    row = (tok // 64).astype(np.float64)
    col = (tok % 64).astype(np.float64)
    out = np.zeros((4, 128, 2048), np.float32)
    for p in range(128):
        pp = p % 64
        axis, half, f = pp // 32, (pp % 32) // 16, pp % 16
        inv = 10000.0 ** (-f / 16.0)
        ang = (row if axis == 0 else col) * np.float32(inv).astype(np.float64)
        ang = (np.float32(row if axis == 0 else col) * np.float32(inv)).astype(np.float32)
        out[0, p] = np.cos(ang)
        out[1, p] = np.sin(ang) * (-1.0 if half == 0 else 1.0)
    for p in range(128):
        axis, half, f = p // 64, (p % 64) // 32, p % 32
        inv = np.float32(10000.0) ** np.float32(-f / 32.0)
        ang = (np.float32(row if axis == 0 else col) * np.float32(inv)).astype(np.float32)
        out[2, p] = np.cos(ang)
        out[3, p] = np.sin(ang) * (-1.0 if half == 0 else 1.0)
    perm = np.zeros((128, 2, 128), np.float32)
    for m in range(128):
        pp = m % 64
        half = (pp % 32) // 16
        src = m + 16 if half == 0 else m - 16
        perm[src, 0, m] = 1.0
        half = (m % 64) // 32
        src = m + 32 if half == 0 else m - 32
        perm[src, 1, m] = 1.0
    return out, perm


_CACHE = {}


def kernel(x_prompt, x_sample, cache_a_k, cache_a_v, cache_b_k, cache_b_v, c, c_ctx,
           w_mod, b_mod, w_in, lam_q1, lam_k1, lam_q2, lam_k2, subln_g, qn_g, kn_g,
           w_br_a, w_br_b, w_out, ln1_g, ln1_b, ln2_g, ln2_b, w_router, router_bias,
           w_gate, w_up, w_down):
    f = lambda a: np.ascontiguousarray(np.asarray(a, dtype=np.float32))
    x_prompt, x_sample = f(x_prompt), f(x_sample)
    rope, perm = _rope_tables()
    nonce = np.full((1, 16), (int.from_bytes(os.urandom(3), "little") + 1), np.int32)
    sel = np.zeros((NE, NE * 128), np.float32)
    for e in range(NE):
        sel[e, e * 128:(e + 1) * 128] = 1.0
    shared = {
        "w_mod": f(w_mod), "w_in": f(w_in), "w_br_a": f(w_br_a), "w_br_b": f(w_br_b), "w_out": f(w_out),
        "w_gate": f(w_gate), "w_up": f(w_up), "w_down": f(w_down),
        "bmod": f(np.asarray(b_mod).reshape(L, 48, 128).transpose(2, 0, 1)),
        "lnp": f(np.stack([np.asarray(a).reshape(L, 8, 128) for a in (ln1_g, ln1_b, ln2_g, ln2_b)], 1).transpose(3, 0, 1, 2)),
        "hv": f(np.stack([np.asarray(subln_g), np.asarray(qn_g), np.asarray(kn_g)], -1).transpose(1, 0, 2)),
        "lamv": f(np.broadcast_to(np.stack([
            np.concatenate([np.asarray(lam_q1), np.asarray(lam_q2)], -1).reshape(-1),
            np.concatenate([np.asarray(lam_k1), np.asarray(lam_k2)], -1).reshape(-1)], 0)[None], (128, 2, L * 128))),
        "kng": f(np.broadcast_to(np.asarray(kn_g)[None], (128, L, 128))),
        "wr": f(np.asarray(w_router).reshape(8, 128, NE).transpose(1, 0, 2)),
        "rb": f(np.broadcast_to(np.asarray(router_bias)[None], (128, NE))),
        "perm": perm, "ident": np.eye(128, dtype=np.float32), "sel": sel,
    }
    cak, cav = f(cache_a_k).reshape(4, L, 512, 1024), f(cache_a_v).reshape(4, L, 512, 1024)
    cbk, cbv = f(cache_b_k).reshape(4, L, 512, 512), f(cache_b_v).reshape(4, L, 512, 512)
    c, c_ctx = f(c), f(c_ctx)
    in_maps = []
    NC_ = CFG["cores"]
    for core in range(NC_):
        b = core // 2
        m = dict(shared)
        hp = core % 2
        m["xsT"] = f(x_sample[b, hp * 1024:(hp + 1) * 1024].T)
        m["rope"] = np.ascontiguousarray(rope[:, :, hp * 1024:(hp + 1) * 1024])
        m["nonce"] = nonce
        m["xpT"] = f(x_prompt[2 * core:2 * core + 2].reshape(512, D).T)
        m["cak"], m["cav"], m["cbk"], m["cbv"] = cak[b], cav[b], cbk[b], cbv[b]
        m["cond"] = f(np.stack([c_ctx.reshape(8, 128), c[b].reshape(8, 128)], -1).transpose(1, 0, 2))
        in_maps.append(m)
    if "k" not in _CACHE:
        _CACHE["k"] = build()
    res = list(run_bass_kernel_spmd(_CACHE["k"].nc, in_maps, core_ids=list(range(NC_))).results)
    while len(res) < 8:
        res.append({k_: np.zeros_like(v_) for k_, v_ in res[0].items()})
    y_prompt = np.concatenate([res[i]["ypT"].T.reshape(2, 256, D) for i in range(8)], 0)
    y_sample = np.stack([np.concatenate([res[2 * b]["ysT"].T, res[2 * b + 1]["ysT"].T], 0) for b in range(4)], 0)
    sa_k = np.concatenate([res[i]["sak"] for i in range(8)], 0).reshape(16, L, 256, 8, 128)
    sa_v = np.concatenate([res[i]["sav"] for i in range(8)], 0).reshape(16, L, 256, 8, 128)
    sb_k = np.concatenate([res[i]["sbk"] for i in range(8)], 0).reshape(16, L, 256, 4, 128)
    sb_v = np.concatenate([res[i]["sbv"] for i in range(8)], 0).reshape(16, L, 256, 4, 128)
    return (np.ascontiguousarray(y_prompt, dtype=np.float32), np.ascontiguousarray(y_sample, dtype=np.float32),
            np.ascontiguousarray(sa_k, dtype=np.float32), np.ascontiguousarray(sa_v, dtype=np.float32),
            np.ascontiguousarray(sb_k, dtype=np.float32), np.ascontiguousarray(sb_v, dtype=np.float32))
```
